# Optimizing a Trainium2 kernel written in Bass

```python
import math
import jax, jax.numpy as jnp
from jax import lax
import numpy as np

D_MODEL = 1024
BATCH = 32
SEQ = 256
DEPTH = 4
DEC_BATCH = 4
DEC_SEQ = 1024
PAST_LEN = 256

GRID_W = 64
N_MIXERS = 3
N_SSM_LAYERS = (DEPTH + 2) // 3
N_GMLP_LAYERS = (DEPTH + 1) // 3
N_CONV_LAYERS = DEPTH // 3

SSM_WIDTH = D_MODEL
SSM_GROUP = 16
SSM_GROUPS = SSM_WIDTH // SSM_GROUP
SSM_STATE = 64
N_DIR = 2
DT_MIN = 1e-3
DT_MAX = 1e-1

GMLP_WIDTH = 2 * D_MODEL
GMLP_CHUNK = 128
GMLP_GROUP_DIM = 128
GMLP_GROUPS = GMLP_WIDTH // GMLP_GROUP_DIM

CONV_WIDTH = 3
FFN_HIDDEN = 4 * D_MODEL
N_MOD = 6
EPS = 1e-6

kernel_name = "hybrid_s5_gmlp_conv_diffusion_step"


def rms_norm(x, g):
    xf = x.astype(jnp.float32)
    y = xf * lax.rsqrt(jnp.mean(xf * xf, axis=-1, keepdims=True) + EPS)
    return (y * g.astype(jnp.float32)).astype(x.dtype)


def layer_norm_plain(x):
    xf = x.astype(jnp.float32)
    mu = jnp.mean(xf, axis=-1, keepdims=True)
    xc = xf - mu
    return (xc * lax.rsqrt(jnp.mean(xc * xc, axis=-1, keepdims=True) + EPS)).astype(x.dtype)


def adaln(cond, w, b):
    m = jax.nn.silu(cond) @ w + b
    return jnp.split(m[:, None, :], N_MOD, axis=-1)


def modulate(h, shift, scale):
    return h * (1.0 + scale) + shift


def grid_pos_embed(n_tokens):
    rows = n_tokens // GRID_W
    t = jnp.arange(rows * GRID_W)
    r = (t // GRID_W).astype(jnp.float32)
    col = (t % GRID_W).astype(jnp.float32)
    quarter = D_MODEL // 4
    freq = 1.0 / (10000.0 ** (jnp.arange(quarter, dtype=jnp.float32) / quarter))
    ar = r[:, None] * freq
    ac = col[:, None] * freq
    return jnp.concatenate([jnp.sin(ar), jnp.cos(ar), jnp.sin(ac), jnp.cos(ac)], axis=-1)


def cmul(ar, ai, br, bi):
    return ar * br - ai * bi, ar * bi + ai * br


def _scan_combine(e1, e2):
    a1r, a1i, b1r, b1i = e1
    a2r, a2i, b2r, b2i = e2
    ar, ai = cmul(a2r, a2i, a1r, a1i)
    br, bi = cmul(a2r, a2i, b1r, b1i)
    return ar, ai, br + b2r, bi + b2i


def zoh(lam_re, lam_im, log_dt, b_re, b_im):
    dt = jnp.exp(log_dt.astype(jnp.float32))[..., None]
    lr = lam_re.astype(jnp.float32)
    li = lam_im.astype(jnp.float32)
    mag = jnp.exp(lr * dt)
    ab_re = mag * jnp.cos(li * dt)
    ab_im = mag * jnp.sin(li * dt)
    den = lr * lr + li * li
    nr, ni = cmul(ab_re - 1.0, ab_im, lr, -li)
    f_re, f_im = nr / den, ni / den
    bb_re, bb_im = cmul(f_re[..., None], f_im[..., None], b_re.astype(jnp.float32), b_im.astype(jnp.float32))
    return ab_re, ab_im, bb_re, bb_im


def ssm_mixer(h, w_in, lam_re, lam_im, log_dt, b_re, b_im, c_re, c_im, d_skip, w_out, h0_re, h0_im):
    bsz, length, _ = h.shape
    u = (h @ w_in).astype(jnp.float32)
    ug = u.reshape(bsz, length, SSM_GROUPS, SSM_GROUP)
    ab_re, ab_im, bb_re, bb_im = zoh(lam_re, lam_im, log_dt, b_re, b_im)
    bu_re = jnp.einsum('blgp,kgnp->kblgn', ug, bb_re)
    bu_im = jnp.einsum('blgp,kgnp->kblgn', ug, bb_im)
    y = d_skip.astype(jnp.float32) * u
    states = []
    for k, rev in ((0, False), (1, True)):
        a_re = jnp.broadcast_to(ab_re[k][None, None], bu_re[k].shape)
        a_im = jnp.broadcast_to(ab_im[k][None, None], bu_im[k].shape)
        acr, aci, sr, si = lax.associative_scan(_scan_combine, (a_re, a_im, bu_re[k], bu_im[k]), reverse=rev, axis=1)
        if h0_re is not None:
            pr, pim = cmul(acr, aci, h0_re[:, k][:, None].astype(jnp.float32), h0_im[:, k][:, None].astype(jnp.float32))
            sr, si = sr + pr, si + pim
        yk = (jnp.einsum('blgn,gpn->blgp', sr, c_re[k].astype(jnp.float32))
              - jnp.einsum('blgn,gpn->blgp', si, c_im[k].astype(jnp.float32)))
        y = y + yk.reshape(bsz, length, SSM_WIDTH)
        states.append((sr, si))
    z = jax.nn.gelu(y.astype(h.dtype))
    a, g = jnp.split(z @ w_out, 2, axis=-1)
    return a * jax.nn.sigmoid(g), states[0], states[1]


def gmlp_mixer(h, w_in, w_s, b_s, w_out):
    bsz, length, _ = h.shape
    z = jax.nn.gelu(h @ w_in)
    u, v = jnp.split(z, 2, axis=-1)
    v = layer_norm_plain(v)
    vc = v.reshape(bsz, length // GMLP_CHUNK, GMLP_CHUNK, GMLP_GROUPS, GMLP_GROUP_DIM)
    s = jnp.einsum('gpq,bcqgd->bcpgd', w_s, vc) + b_s.T[None, None, :, :, None]
    return (u * s.reshape(bsz, length, GMLP_WIDTH)) @ w_out


def conv_mixer(h, w_in, conv_w, w_out):
    gb, gc, xh = jnp.split(h @ w_in, 3, axis=-1)
    y = lax.conv_general_dilated(gc * xh, conv_w[:, None, :], window_strides=(1,),
                                 padding=((CONV_WIDTH // 2, CONV_WIDTH // 2),),
                                 dimension_numbers=('NWC', 'WIO', 'NWC'),
                                 feature_group_count=D_MODEL)
    return (gb * y) @ w_out


def sqrelu_ffn(h, w1, w2):
    return jnp.square(jax.nn.relu(h @ w1)) @ w2


def setup_inputs(seed: int = 0) -> dict:
    key = jax.random.key(seed)
    ks = jax.random.split(key, 32)
    f32 = jnp.float32

    def nrm(k, shape, scale):
        return jax.random.normal(k, shape, f32) * scale

    n_idx = jnp.arange(SSM_STATE, dtype=f32)
    ssm_shape = (N_SSM_LAYERS, N_DIR, SSM_GROUPS, SSM_STATE)
    return {
        "x_prompt": nrm(ks[0], (BATCH, SEQ, D_MODEL), 1.0),
        "x_sample": nrm(ks[1], (DEC_BATCH, DEC_SEQ, D_MODEL), 1.0),
        "state_ssm_re": nrm(ks[2], (DEC_BATCH, N_SSM_LAYERS, N_DIR, SSM_GROUPS, SSM_STATE), 0.05),
        "state_ssm_im": nrm(ks[3], (DEC_BATCH, N_SSM_LAYERS, N_DIR, SSM_GROUPS, SSM_STATE), 0.05),
        "c": nrm(ks[4], (DEC_BATCH, D_MODEL), 1.0),
        "c_ctx": nrm(ks[5], (D_MODEL,), 1.0),
        "w_mod": nrm(ks[6], (DEPTH, D_MODEL, N_MOD * D_MODEL), 0.5 * D_MODEL ** -0.5),
        "b_mod": nrm(ks[7], (DEPTH, N_MOD * D_MODEL), 0.01),
        "g_mix": 1.0 + nrm(ks[8], (DEPTH, D_MODEL), 0.02),
        "g_ffn": 1.0 + nrm(ks[9], (DEPTH, D_MODEL), 0.02),
        "ffn_w1": nrm(ks[10], (DEPTH, D_MODEL, FFN_HIDDEN), D_MODEL ** -0.5),
        "ffn_w2": nrm(ks[11], (DEPTH, FFN_HIDDEN, D_MODEL), FFN_HIDDEN ** -0.5),
        "ssm_w_in": nrm(ks[12], (N_SSM_LAYERS, D_MODEL, SSM_WIDTH), D_MODEL ** -0.5),
        "ssm_lam_re": -0.5 + nrm(ks[13], ssm_shape, 0.01),
        "ssm_lam_im": math.pi * n_idx + nrm(ks[14], ssm_shape, 0.01),
        "ssm_log_dt": jax.random.uniform(ks[15], (N_SSM_LAYERS, N_DIR, SSM_GROUPS), f32,
                                         minval=math.log(DT_MIN), maxval=math.log(DT_MAX)),
        "ssm_b_re": nrm(ks[16], ssm_shape + (SSM_GROUP,), (2.0 * SSM_GROUP) ** -0.5),
        "ssm_b_im": nrm(ks[17], ssm_shape + (SSM_GROUP,), (2.0 * SSM_GROUP) ** -0.5),
        "ssm_c_re": nrm(ks[18], (N_SSM_LAYERS, N_DIR, SSM_GROUPS, SSM_GROUP, SSM_STATE), (2.0 * SSM_STATE) ** -0.5),
        "ssm_c_im": nrm(ks[19], (N_SSM_LAYERS, N_DIR, SSM_GROUPS, SSM_GROUP, SSM_STATE), (2.0 * SSM_STATE) ** -0.5),
        "ssm_d": nrm(ks[20], (N_SSM_LAYERS, SSM_WIDTH), 1.0),
        "ssm_w_out": nrm(ks[21], (N_SSM_LAYERS, SSM_WIDTH, 2 * D_MODEL), SSM_WIDTH ** -0.5),
        "gmlp_w_in": nrm(ks[22], (N_GMLP_LAYERS, D_MODEL, 2 * GMLP_WIDTH), D_MODEL ** -0.5),
        "gmlp_w_s": nrm(ks[23], (N_GMLP_LAYERS, GMLP_GROUPS, GMLP_CHUNK, GMLP_CHUNK), GMLP_CHUNK ** -0.5),
        "gmlp_b_s": 1.0 + nrm(ks[24], (N_GMLP_LAYERS, GMLP_GROUPS, GMLP_CHUNK), 0.02),
        "gmlp_w_out": nrm(ks[25], (N_GMLP_LAYERS, GMLP_WIDTH, D_MODEL), GMLP_WIDTH ** -0.5),
        "conv_w_in": nrm(ks[26], (N_CONV_LAYERS, D_MODEL, 3 * D_MODEL), D_MODEL ** -0.5),
        "conv_w": nrm(ks[27], (N_CONV_LAYERS, CONV_WIDTH, D_MODEL), CONV_WIDTH ** -0.5),
        "conv_w_out": nrm(ks[28], (N_CONV_LAYERS, D_MODEL, D_MODEL), D_MODEL ** -0.5),
        "g_final": 1.0 + nrm(ks[29], (D_MODEL,), 0.02),
    }


def reference(x_prompt, x_sample, state_ssm_re, state_ssm_im, c, c_ctx, w_mod, b_mod, g_mix, g_ffn,
              ffn_w1, ffn_w2, ssm_w_in, ssm_lam_re, ssm_lam_im, ssm_log_dt, ssm_b_re, ssm_b_im,
              ssm_c_re, ssm_c_im, ssm_d, ssm_w_out, gmlp_w_in, gmlp_w_s, gmlp_b_s, gmlp_w_out,
              conv_w_in, conv_w, conv_w_out, g_final):
    xp = x_prompt
    n_lat = x_sample.shape[1]
    xs = x_sample + grid_pos_embed(n_lat).astype(x_sample.dtype)[None]
    cond_p = c_ctx[None, :]
    new_re, new_im = [], []
    for i in range(DEPTH):
        kind, j = i % N_MIXERS, i // N_MIXERS
        mp = adaln(cond_p, w_mod[i], b_mod[i])
        ms = adaln(c, w_mod[i], b_mod[i])
        hp = modulate(rms_norm(xp, g_mix[i]), mp[0], mp[1])
        hs = modulate(rms_norm(xs, g_mix[i]), ms[0], ms[1])
        if kind == 0:
            sp = (ssm_w_in[j], ssm_lam_re[j], ssm_lam_im[j], ssm_log_dt[j], ssm_b_re[j], ssm_b_im[j],
                  ssm_c_re[j], ssm_c_im[j], ssm_d[j], ssm_w_out[j])
            op, (fr, fi), (br, bi) = ssm_mixer(hp, *sp, None, None)
            os_, _, _ = ssm_mixer(hs, *sp, state_ssm_re[:, j], state_ssm_im[:, j])
            new_re.append(jnp.stack([fr[:, -1], br[:, 0]], axis=1))
            new_im.append(jnp.stack([fi[:, -1], bi[:, 0]], axis=1))
        elif kind == 1:
            op = gmlp_mixer(hp, gmlp_w_in[j], gmlp_w_s[j], gmlp_b_s[j], gmlp_w_out[j])
            os_ = gmlp_mixer(hs, gmlp_w_in[j], gmlp_w_s[j], gmlp_b_s[j], gmlp_w_out[j])
        else:
            op = conv_mixer(hp, conv_w_in[j], conv_w[j], conv_w_out[j])
            os_ = conv_mixer(hs, conv_w_in[j], conv_w[j], conv_w_out[j])
        xp = xp + mp[2] * op.astype(xp.dtype)
        xs = xs + ms[2] * os_.astype(xs.dtype)
        xp = xp + mp[5] * sqrelu_ffn(modulate(rms_norm(xp, g_ffn[i]), mp[3], mp[4]), ffn_w1[i], ffn_w2[i])
        xs = xs + ms[5] * sqrelu_ffn(modulate(rms_norm(xs, g_ffn[i]), ms[3], ms[4]), ffn_w1[i], ffn_w2[i])
    y_prompt = rms_norm(xp, g_final)
    y_sample = rms_norm(xs, g_final)
    return (y_prompt, y_sample, jnp.stack(new_re, axis=1), jnp.stack(new_im, axis=1))
```

```python
import contextlib
import math
import numpy as np
import concourse.bass as bass
import concourse.mybir as mybir
from concourse.bass_utils import run_bass_kernel_spmd

F32 = mybir.dt.float32
BF16 = mybir.dt.bfloat16
I32 = mybir.dt.int32
F32R = mybir.dt.float32r
AF = mybir.ActivationFunctionType
ALU = mybir.AluOpType

D = 1024
NSEG = 6
SEGL = 256
NTOK = NSEG * SEGL
NTB = NTOK // 512
TCH = 8
NCH = SEGL // TCH
NCOL = NSEG * NCH
NSLOT = NCH + 2
NG = 4
QLEN = 200
EPS = 1e-6
PREFETCH = True


class Res:
    def __init__(self, name):
        self.name = name
        self.whole_w = None
        self.whole_r = []
        self.sub = {}

    def __getitem__(self, key):
        return (self, key)


def _norm(r):
    if isinstance(r, Res):
        return (r, None)
    return r


class Op:
    __slots__ = ("eng", "fn", "deps", "dma", "signal", "idx", "ticket", "sem")

    def __init__(self, eng, fn, dma):
        self.eng = eng
        self.fn = fn
        self.deps = set()
        self.dma = dma
        self.signal = False
        self.ticket = None
        self.sem = None


class Sched:
    ENGS = ("pe", "act", "dve", "pool", "sp")

    def __init__(self, nc, ndma_sems=8):
        self.nc = nc
        self.ops = []
        self.ndma = ndma_sems
        self.last = {}
        self.pending_dma = []
        self.marks = []

    def mark(self, name):
        self.marks.append((name, sum(1 for o in self.ops if o.eng == "pe" and not o.dma)))

    def _collect(self, op, r, is_write):
        res, key = _norm(r)
        deps = op.deps
        if res.whole_w is not None:
            deps.add(res.whole_w)
        if is_write:
            deps.update(res.whole_r)
        if key is None:
            for (w, rd) in res.sub.values():
                if w is not None:
                    deps.add(w)
                if is_write:
                    deps.update(rd)
        else:
            st = res.sub.get(key)
            if st is not None:
                if st[0] is not None:
                    deps.add(st[0])
                if is_write:
                    deps.update(st[1])

    def _commit(self, op, r, is_write):
        res, key = _norm(r)
        if key is None:
            if is_write:
                res.whole_w = op.idx
                res.whole_r = []
                res.sub = {}
            else:
                res.whole_r.append(op.idx)
        else:
            st = res.sub.get(key)
            if st is None:
                st = [None, []]
                res.sub[key] = st
            if is_write:
                st[0] = op.idx
                st[1] = []
            else:
                st[1].append(op.idx)

    def op(self, eng, fn, reads=(), writes=(), dma=False):
        o = Op(eng, fn, dma)
        o.idx = len(self.ops)
        for r in reads:
            self._collect(o, r, False)
        for r in writes:
            self._collect(o, r, True)
        for r in reads:
            self._commit(o, r, False)
        for r in writes:
            self._commit(o, r, True)
        o.deps.discard(o.idx)
        self.ops.append(o)
        if dma:
            self.pending_dma.append(o.idx)
        else:
            self.last[eng] = o.idx
        return o

    def barrier(self):
        deps = set(self.last.values()) | set(self.pending_dma)
        self.pending_dma = []
        for e in self.ENGS:
            o = Op(e, (lambda eng: eng.nop()), False)
            o.idx = len(self.ops)
            o.deps = set(deps)
            self.ops.append(o)

    def emit(self):
        nc = self.nc
        ops = self.ops

        def needs(o, dop):
            return dop.dma or o.dma or dop.eng != o.eng or o.eng != "pe"

        for o in ops:
            for d in o.deps:
                if needs(o, ops[d]):
                    ops[d].signal = True
        for o in ops:
            if o.dma:
                o.signal = True
        with contextlib.ExitStack() as es:
            esem = {e: es.enter_context(nc.semaphore("s_" + e)) for e in self.ENGS}
            dsem = {}
            for e in ("sp", "act"):
                dsem[e] = [es.enter_context(nc.semaphore("d_%s_%d" % (e, i))) for i in range(self.ndma if e == "sp" else 2)]
            ecount = {e: 0 for e in self.ENGS}
            dcount = {e: [0] * len(dsem[e]) for e in dsem}
            drr = {e: 0 for e in dsem}
            for o in ops:
                if o.dma:
                    k = drr[o.eng] % len(dsem[o.eng])
                    drr[o.eng] += 1
                    dcount[o.eng][k] += 1
                    o.sem = dsem[o.eng][k]
                    o.ticket = 16 * dcount[o.eng][k]
                elif o.signal:
                    ecount[o.eng] += 1
                    o.sem = esem[o.eng]
                    o.ticket = ecount[o.eng]
            per_eng = {e: [o for o in ops if o.eng == e] for e in self.ENGS}
            block = es.enter_context(nc.Block())

            def make(e):
                def body(eng):
                    waited = {}
                    for o in per_eng[e]:
                        waits = {}
                        for d in o.deps:
                            dop = ops[d]
                            if not needs(o, dop):
                                continue
                            key = id(dop.sem)
                            if waits.get(key, (None, 0))[1] < dop.ticket:
                                waits[key] = (dop.sem, dop.ticket)
                        if o.dma and o.ticket > 16:
                            key = id(o.sem)
                            if waits.get(key, (None, 0))[1] < o.ticket - 16:
                                waits[key] = (o.sem, o.ticket - 16)
                        for key, (sem, val) in waits.items():
                            if waited.get(key, 0) >= val:
                                continue
                            eng.wait_ge(sem, val)
                            waited[key] = val
                        ins = o.fn(eng)
                        if o.signal:
                            ins.then_inc(o.sem, 16 if o.dma else 1)
                    if e == "sp":
                        for q in dsem:
                            for k in range(len(dsem[q])):
                                if dcount[q][k] > 0:
                                    eng.wait_ge(dsem[q][k], 16 * dcount[q][k])

                return body

            block.tensor(make("pe"))
            block.scalar(make("act"))
            block.vector(make("dve"))
            block.gpsimd(make("pool"))
            block.sync(make("sp"))
        return nc


class T:
    def __init__(self, t, name):
        self.t = t
        self.r = Res(name)

    def __getitem__(self, k):
        return self.t[k]


def build_nc(layers=(0, 1, 2, 3), do_final=True):
    nc = bass.Bass("TRN2", target_bir_lowering=False)
    S = Sched(nc)

    def din(name, shape):
        return nc.dram_tensor(name, list(shape), F32, kind="ExternalInput").ap()

    def dout(name, shape):
        return nc.dram_tensor(name, list(shape), F32, kind="ExternalOutput").ap()

    xT_d = din("xT", [D, NTOK])
    condT_d = din("condT", [D, NSEG])
    flag_d = din("flag", [1, 1])
    link_d = din("link", [1, 2 * NSEG])
    h0_d = din("h0", [2, 2, 128, 64])
    w_mod_d = din("w_mod", [4, D, 6 * D])
    b_modT_d = din("b_modT", [4, 128, 48])
    g_mixT_d = din("g_mixT", [4, 128, 8])
    g_ffnT_d = din("g_ffnT", [4, 128, 8])
    g_finT_d = din("g_finT", [128, 8])
    ffn_w1_d = din("ffn_w1", [4, D, 4 * D])
    ffn_w2_d = din("ffn_w2", [4, 4 * D, D])
    ssm_w_in_d = din("ssm_w_in", [2, D, D])
    ssm_w_out_d = din("ssm_w_out", [2, D, 2 * D])
    ssm_small_d = din("ssm_small", [2, 3, 128, 64])
    ssm_bc_d = din("ssm_bc", [2, 4, 128, 64, 16])
    ssm_dpp_d = din("ssm_dpp", [2, 128, 64])
    gm_w_in_d = din("gm_w_in", [D, 4 * D])
    gm_wsT_d = din("gm_wsT", [128, 16, 128])
    gm_bs_d = din("gm_bs", [1, 16 * 128])
    gm_w_out_d = din("gm_w_out", [2 * D, D])
    cv_w_in_d = din("cv_w_in", [D, 3 * D])
    cv_wT_d = din("cv_wT", [128, 3, 8])
    cv_w_out_d = din("cv_w_out", [D, D])
    consts_d = din("consts", [5, 128, 128])
    band_d = din("band", [128, 8 * 240])
    yT_d = dout("yT", [D, NTOK])
    st_d = dout("st", [2, 2, 128, 64 * NSEG])

    es = contextlib.ExitStack()
    base = (int(nc.sbuf_base) + 63) // 64 * 64
    top = 229344
    arena = es.enter_context(nc.sbuf_tensor("arena", [128, (top - base) // 4 - 64], F32))
    cur = [base]

    def alloc(name, shape, dt, off=None):
        nbytes = int(np.prod(shape[1:])) * (4 if dt in (F32, I32, F32R) else 2)
        nbytes = (nbytes + 31) // 32 * 32
        if off is None:
            off = cur[0]
            cur[0] += nbytes
        assert off + nbytes <= top - 256, (name, off, nbytes)
        t = nc.alloc_sbuf_tensor_at(name, list(shape), dt, offset=off)
        return T(t, name), off + nbytes

    def palloc(name, shape, dt):
        return alloc(name, shape, dt)[0]

    XT = palloc("XT", [128, 8, NTOK], F32)
    HT = palloc("HT", [128, 8, NTOK], BF16)
    MODSL = [palloc("MODSa", [128, 48, NSEG], F32), palloc("MODSb", [128, 48, NSEG], F32)]
    MS = [MODSL[0]]
    A1 = palloc("A1", [128, 8, NSEG], F32)
    A2 = palloc("A2", [128, 8, NSEG], F32)
    BMOD = palloc("BMOD", [128, 4, 48], F32)
    GMIX = palloc("GMIX", [128, 4, 8], F32)
    GFFN = palloc("GFFN", [128, 4, 8], F32)
    GFIN = palloc("GFIN", [128, 8], F32)
    CONDS = palloc("CONDS", [128, 8, NSEG], F32)
    CONDSB = palloc("CONDSB", [128, 8, NSEG], BF16)
    IDENT = palloc("IDENT", [128, 128], F32)
    MASKL = palloc("MASKL", [128, 128], F32)
    MASKU = palloc("MASKU", [128, 128], F32)
    SGF = palloc("SGF", [128, 128], F32)
    ONES = palloc("ONES", [128, 128], BF16)
    BAND = palloc("BAND", [128, 8, 240], BF16)
    FLAG = palloc("FLAG", [128, 1], F32)
    LINK = palloc("LINK", [128, 2 * NSEG], F32)
    EPSB = palloc("EPSB", [128, 1], F32)
    CVW = palloc("CVW", [128, 3, 8], F32)
    NL = palloc("NL", [128, NSEG], F32)
    NCVW = palloc("NCVW", [128, 3, 8], F32)
    CB = palloc("CB", [128, 8], F32)
    ST1 = palloc("ST1", [128, 32], F32)
    ARENA0 = cur[0]

    off = ARENA0
    WB, _ = alloc("WB", [128, 4, 4096], BF16, off)
    STG, off = alloc("STG", [128, 3, 2048], F32, off + 24576)
    BIG_OFF = off
    BIG, off = alloc("BIG", [128, 16, NTOK], BF16, off)
    RSTD_OFF = off
    RSTD, off = alloc("RSTD", [128, NTOK], F32, off)
    XN, off = alloc("XN", [128, NTOK], F32, off)
    TMP_OFF = off
    TMP, off = alloc("TMP", [128, 3, 512], F32, off)
    GEN_END = off
    WBr = [Res("WB%d" % i) for i in range(4)]
    STGr = [Res("STG%d" % i) for i in range(6)]
    stg_slots = [0, 1, 2]
    STG2 = nc.alloc_sbuf_tensor_at("STG2", [128, 3, 2048], F32, offset=BIG_OFF + 8 * NTOK * 2)

    def stg_ap(s, n):
        return STG[:, s, 0:n] if s < 3 else STG2[:, s - 3, 0:n]

    def next_stg():
        s = stg_slots[stg_rr[0] % len(stg_slots)]
        stg_rr[0] += 1
        return s
    TMPr = [Res("TMP%d" % i) for i in range(3)]

    PS = [T(es.enter_context(nc.psum_tensor("ps%d" % i, [128, 512], F32)), "ps%d" % i) for i in range(8)]
    bank_rr = [0]

    def next_bank():
        b = PS[bank_rr[0] % 8]
        bank_rr[0] += 1
        return b

    dmaq_rr = [0]

    def dma(out_ap, in_ap, reads=(), writes=(), q=None):
        if q is None:
            q = "sp"
        S.op(q, lambda e: e.dma_start(out=out_ap, in_=in_ap), reads=reads, writes=writes, dma=True)

    dma(IDENT[:], consts_d[0], writes=[IDENT.r])
    dma(MASKL[:], consts_d[1], writes=[MASKL.r])
    dma(MASKU[:], consts_d[2], writes=[MASKU.r])
    dma(SGF[:], consts_d[3], writes=[SGF.r])
    dma(STG[:, 2, 0:1920], band_d, writes=[STGr[2]])
    S.op("dve", lambda e: e.tensor_copy(out=BAND[:].rearrange("p a b -> p (a b)"), in_=STG[:, 2, 0:1920]), reads=[STGr[2]], writes=[BAND.r])
    S.op("dve", lambda e: e.memset(ONES[:], 1.0 / D), writes=[ONES.r])
    S.op("dve", lambda e: e.memset(EPSB[:], EPS), writes=[EPSB.r])
    dma(FLAG[:], flag_d.partition_broadcast(128), writes=[FLAG.r])
    dma(LINK[:], link_d.partition_broadcast(128), writes=[LINK.r])
    dma(BMOD[:], b_modT_d.rearrange("i p o -> p i o"), writes=[BMOD.r])
    dma(GMIX[:], g_mixT_d.rearrange("i p o -> p i o"), writes=[GMIX.r])
    dma(GFFN[:], g_ffnT_d.rearrange("i p o -> p i o"), writes=[GFFN.r])
    dma(GFIN[:], g_finT_d, writes=[GFIN.r])
    dma(CVW[:], cv_wT_d, writes=[CVW.r])
    dma(CONDS[:], condT_d.rearrange("(kt p) s -> p kt s", p=128), writes=[CONDS.r])
    xT_v = xT_d.rearrange("(kt p) n -> p kt n", p=128)
    for kt in range(8):
        dma(XT[:, kt, :], xT_v[:, kt, :], writes=[XT.r[kt]])
    S.op("act", lambda e: e.activation(out=CONDSB[:], in_=CONDS[:], func=AF.Silu), reads=[CONDS.r], writes=[CONDSB.r])

    rcI = nc.alloc_sbuf_tensor_at("rcI", [128, 2, 1024], I32, offset=BIG_OFF + 20480)
    rcF = nc.alloc_sbuf_tensor_at("rcF", [128, 2, 1024], F32, offset=BIG_OFF + 20480 + 8192)
    angF = nc.alloc_sbuf_tensor_at("angF", [128, 1024], F32, offset=BIG_OFF + 20480 + 16384)
    angI = nc.alloc_sbuf_tensor_at("angI", [128, 1024], I32, offset=BIG_OFF + 20480 + 20480)
    angK = nc.alloc_sbuf_tensor_at("angK", [128, 1024], F32, offset=BIG_OFF + 20480 + 24576)
    PEr = Res("pe_scratch")
    S.op("pool", lambda e: e.iota(rcI[:, 0, :], pattern=[[1, 16], [0, 64]], base=0, channel_multiplier=0), writes=[PEr])
    S.op("pool", lambda e: e.iota(rcI[:, 1, :], pattern=[[0, 16], [1, 64]], base=0, channel_multiplier=0), writes=[PEr])
    S.op("dve", lambda e: e.tensor_copy(out=rcF[:], in_=rcI[:]), reads=[PEr], writes=[PEr])
    TWO_PI = 2.0 * math.pi
    for tile in range(8):
        which = tile // 4
        is_cos = (tile // 2) % 2
        fcol = 1 + (tile % 2)
        shift = 0.75 if is_cos else 0.5
        S.op("dve", lambda e, which=which, fcol=fcol, shift=shift: e.tensor_scalar(
            out=angF[:], in0=rcF[:, which, :], scalar1=SGF[:, fcol:fcol + 1], scalar2=shift, op0=ALU.mult, op1=ALU.add),
            reads=[PEr, SGF.r], writes=[PEr])
        S.op("dve", lambda e: e.tensor_copy(out=angI[:], in_=angF[:]), reads=[PEr], writes=[PEr])
        S.op("dve", lambda e: e.tensor_copy(out=angK[:], in_=angI[:]), reads=[PEr], writes=[PEr])
        S.op("dve", lambda e: e.tensor_tensor(out=angF[:], in0=angF[:], in1=angK[:], op=ALU.subtract), reads=[PEr], writes=[PEr])
        S.op("dve", lambda e: e.tensor_scalar(out=angK[:], in0=angF[:], scalar1=0.0, scalar2=None, op0=ALU.is_lt), reads=[PEr], writes=[PEr])
        S.op("dve", lambda e: e.tensor_tensor(out=angF[:], in0=angF[:], in1=angK[:], op=ALU.add), reads=[PEr], writes=[PEr])
        S.op("dve", lambda e: e.tensor_scalar(out=angF[:], in0=angF[:], scalar1=TWO_PI, scalar2=-math.pi, op0=ALU.mult, op1=ALU.add), reads=[PEr], writes=[PEr])
        S.op("dve", lambda e: e.tensor_scalar(out=angF[:], in0=angF[:], scalar1=-math.pi, scalar2=math.pi, op0=ALU.max, op1=ALU.min), reads=[PEr], writes=[PEr])
        S.op("act", lambda e: e.activation(out=angK[:], in_=angF[:], func=AF.Sin), reads=[PEr], writes=[PEr])
        S.op("dve", lambda e, tile=tile: e.scalar_tensor_tensor(
            out=XT[:, tile, 0:1024], in0=angK[:], scalar=FLAG[:, 0:1], in1=XT[:, tile, 0:1024], op0=ALU.mult, op1=ALU.add),
            reads=[PEr, FLAG.r, XT.r[tile]], writes=[XT.r[tile]])

    stg_rr = [0]
    wb_rr = [0]
    cast_rr = [0]

    def load_chunk(src_aps, slot=None):
        if slot is None:
            slot = wb_rr[0] % 2
            wb_rr[0] += 1
        for i, sap in enumerate(src_aps):
            a, b = sap.shape[1], sap.shape[2]
            n = a * b
            assert n <= 2048
            s = next_stg()
            dma(STG[:, s, 0:n].rearrange("p (a b) -> p a b", a=a), sap, writes=[STGr[s]])
            ce = "act" if (cast_rr[0] % 6) != 5 else "pool"
            cast_rr[0] += 1
            if ce == "pool":
                S.op("pool", lambda e, s=s, slot=slot, i=i, n=n: e.tensor_copy(out=WB[:, slot, i * 2048:i * 2048 + n], in_=STG[:, s, 0:n]),
                     reads=[STGr[s]], writes=[WBr[slot]])
            else:
                S.op("act", lambda e, s=s, slot=slot, i=i, n=n: e.activation(out=WB[:, slot, i * 2048:i * 2048 + n], in_=STG[:, s, 0:n], func=AF.Copy),
                     reads=[STGr[s]], writes=[WBr[slot]])
        return slot

    def wview(w2d, kt0, nkt, c0, ncols):
        return w2d.rearrange("(kt p) n -> p kt n", p=128)[:, kt0:kt0 + nkt, c0:c0 + ncols]

    def proj_chunk(slot, nkt, n_ot, src, src_reads, consumer, kt_layout_cols):
        wv = WB[:, slot, 0:nkt * kt_layout_cols].rearrange("p (k c) -> p k c", k=nkt)
        for ot in range(n_ot):
            banks = [next_bank() for _ in range(NTB)]
            for kt in range(nkt):
                for tb in range(NTB):
                    S.op("pe", lambda e, ot=ot, kt=kt, tb=tb, bk=banks[tb]: e.matmul(
                        bk[:], lhsT=wv[:, kt, ot * 128:(ot + 1) * 128], rhs=src(kt, tb), start=(kt == 0), stop=(kt == nkt - 1)),
                        reads=[WBr[slot]] + list(src_reads(kt)), writes=[banks[tb].r])
            for tb in range(NTB):
                consumer(ot, tb, banks[tb])

    loaded = {}

    def next_wb():
        for _ in range(4):
            slot = wb_rr[0] % 3
            wb_rr[0] += 1
            if slot not in loaded.values():
                return slot
        raise AssertionError("no free WB slot")

    def load_chunk(key, w2d, K, cols):
        if key in loaded:
            return
        nkt = K // 128
        ncol = 128 * len(cols)
        assert nkt * ncol <= 4096
        per_kt = ncol
        kpp = 1
        for k in range(1, nkt + 1):
            if nkt % k == 0 and k * per_kt <= 2048:
                kpp = k
        slot = next_wb()
        npieces = nkt // kpp
        runs = []
        for j, c0 in enumerate(cols):
            if runs and runs[-1][1] + runs[-1][2] == c0:
                runs[-1][2] += 128
            else:
                runs.append([j, c0, 128])
        for p in range(npieces):
            s = next_stg()
            n = kpp * per_kt
            sap = stg_ap(s, n)
            stv = sap.rearrange("p (k c) -> p k c", k=kpp)
            for (j, c0, wdt) in runs:
                dma(stv[:, :, j * 128:j * 128 + wdt], wview(w2d, p * kpp, kpp, c0, wdt), writes=[STGr[s]])
            ce = "act" if (cast_rr[0] % 6) != 5 else "pool"
            cast_rr[0] += 1
            dst = WB[:, slot, p * n:(p + 1) * n]
            if ce == "pool":
                S.op("pool", lambda e, sap=sap, dst=dst: e.tensor_copy(out=dst, in_=sap), reads=[STGr[s]], writes=[WBr[slot]])
            else:
                S.op("act", lambda e, sap=sap, dst=dst: e.activation(out=dst, in_=sap, func=AF.Copy), reads=[STGr[s]], writes=[WBr[slot]])
        loaded[key] = slot

    def proj(w2d, K, cols_list, src, src_reads, consumer, name=None, hint=None, after=None):
        nkt = K // 128
        if name is None:
            name = ("anon", len(S.ops))
        for ci, cols in enumerate(cols_list):
            load_chunk((name, ci), w2d, K, cols)
            if PREFETCH:
                if ci + 1 < len(cols_list):
                    load_chunk((name, ci + 1), w2d, K, cols_list[ci + 1])
                elif hint is not None:
                    hint()
            slot = loaded.pop((name, ci))
            proj_chunk(slot, nkt, len(cols), src, src_reads, lambda ot, tb, bk, ci=ci: consumer(ci, ot, tb, bk), 128 * len(cols))
            if after is not None:
                after(ci)

    def adaln_load(i, cb):
        load_chunk(("ada", i, cb), w_mod_d[i], D, [cb * 512 + o * 128 for o in range(4)])

    XNr2 = [Res("XNa0"), Res("XNa1")]

    def adaln_mm(i, cb, Mdst):
        key = ("ada", i, cb)
        load_chunk(key, w_mod_d[i], D, [cb * 512 + o * 128 for o in range(4)])
        slot = loaded.pop(key)
        wv = WB[:, slot, 0:4096].rearrange("p (k c) -> p k c", k=8)
        bk = next_bank()
        for kt in range(8):
            S.op("pe", lambda e, kt=kt: e.matmul(bk[0:NSEG, :], lhsT=CONDSB[:, kt, :], rhs=wv[:, kt, :], start=(kt == 0), stop=(kt == 7)),
                 reads=[WBr[slot], CONDSB.r], writes=[bk.r])
        x0 = (cb % 2) * 512
        S.op("act", lambda e: e.activation(out=XN[0:NSEG, x0:x0 + 512], in_=bk[0:NSEG, :], func=AF.Copy), reads=[bk.r], writes=[XN.r, XNr2[cb % 2]])

    def adaln_tr(i, cb, Mdst):
        x0 = (cb % 2) * 512
        bkT = next_bank()
        for c in range(4):
            S.op("pe", lambda e, c=c: e.transpose(bkT[:, c * 8:c * 8 + NSEG], XN[0:NSEG, x0 + c * 128:x0 + (c + 1) * 128], IDENT[0:NSEG, 0:NSEG]),
                 reads=[XNr2[cb % 2], IDENT.r], writes=[bkT.r])
        bb = BMOD[:, i, 4 * cb:4 * cb + 4].unsqueeze(2).to_broadcast([128, 4, NSEG])
        S.op("dve", lambda e: e.tensor_tensor(out=Mdst[:, 4 * cb:4 * cb + 4, :], in0=bkT[:, 0:32].rearrange("p (c s) -> p c s", c=4)[:, :, 0:NSEG], in1=bb, op=ALU.add),
             reads=[bkT.r, BMOD.r], writes=[Mdst.r[cb]])

    def adaln_block(i, cb, Mdst):
        adaln_mm(i, cb, Mdst)
        adaln_tr(i, cb, Mdst)

    def merge_cols(cols):
        return cols

    def norm_mod(Atile, Bsel):
        for dt in range(8):
            S.op("act", lambda e, dt=dt: e.activation(out=HT[:, dt, :], in_=XT[:, dt, :], func=AF.Square), reads=[XT.r[dt]], writes=[HT.r[dt]])
        for tb in range(NTB):
            bk = next_bank()
            for dt in range(8):
                S.op("pe", lambda e, dt=dt, tb=tb, bk=bk: e.matmul(bk[:], lhsT=ONES[:], rhs=HT[:, dt, tb * 512:(tb + 1) * 512], start=(dt == 0), stop=(dt == 7)),
                     reads=[ONES.r, HT.r[dt]], writes=[bk.r])
            S.op("act", lambda e, tb=tb, bk=bk: e.activation(out=RSTD[:, tb * 512:(tb + 1) * 512], in_=bk[:], func=AF.Sqrt, bias=EPSB[:, 0:1], scale=1.0),
                 reads=[bk.r, EPSB.r], writes=[RSTD.r[tb]])
            S.op("dve", lambda e, tb=tb: e.reciprocal(out=RSTD[:, tb * 512:(tb + 1) * 512], in_=RSTD[:, tb * 512:(tb + 1) * 512]),
                 reads=[RSTD.r[tb]], writes=[RSTD.r[tb]])
        TMPf = TMP[:].rearrange("p a b -> p (a b)")
        for dt in range(8):
            if dt % 2 == 0:
                xb, xr = XN[:], [XN.r]
            else:
                xb, xr = TMPf, list(TMPr)
            S.op("dve", lambda e, dt=dt, xb=xb: e.tensor_tensor(out=xb, in0=XT[:, dt, :], in1=RSTD[:], op=ALU.mult),
                 reads=[XT.r[dt], RSTD.r], writes=xr)
            for sg in range(NSEG):
                if Bsel is not None:
                    bap = Bsel(dt)[:, sg:sg + 1]
                    S.op("act", lambda e, dt=dt, sg=sg, bap=bap, xb=xb: e.activation(out=HT[:, dt, sg * SEGL:(sg + 1) * SEGL], in_=xb[:, sg * SEGL:(sg + 1) * SEGL],
                                                                      func=AF.Identity, scale=Atile[:, dt, sg:sg + 1], bias=bap),
                         reads=xr + [Atile.r, MS[0].r], writes=[HT.r[dt]])

    def x_update(gate_ot0):
        def cons(dt, tb, bk):
            Mt = MS[0]
            for h in range(2):
                sg = tb * 2 + h
                S.op("dve", lambda e, dt=dt, sg=sg, h=h, bk=bk, Mt=Mt: e.scalar_tensor_tensor(
                    out=XT[:, dt, sg * SEGL:(sg + 1) * SEGL], in0=bk[:, h * SEGL:(h + 1) * SEGL], scalar=Mt[:, gate_ot0 + dt, sg:sg + 1],
                    in1=XT[:, dt, sg * SEGL:(sg + 1) * SEGL], op0=ALU.mult, op1=ALU.add),
                    reads=[bk.r, Mt.r, XT.r[dt]], writes=[XT.r[dt]])
        return cons

    def ht_src(kt, tb):
        return HT[:, kt, tb * 512:(tb + 1) * 512]

    def ht_reads(kt):
        return [HT.r[kt]]

    BIGr = BIG.r

    def big_src(kt, tb):
        return BIG[:, kt, tb * 512:(tb + 1) * 512]

    def big_reads(kt):
        return [BIGr[kt]]

    S.op("dve", lambda e: e.tensor_scalar(out=NL[:], in0=LINK[:, 0:NSEG], scalar1=-1.0, scalar2=1.0, op0=ALU.mult, op1=ALU.add), reads=[LINK.r], writes=[NL.r])
    S.op("dve", lambda e: e.tensor_scalar(out=NCVW[:], in0=CVW[:], scalar1=-1.0, scalar2=None, op0=ALU.mult), reads=[CVW.r], writes=[NCVW.r])

    def conv_layer(i):
        def cons(ci, ot, tb, bk):
            dt = ci
            sl = slice(tb * 512, (tb + 1) * 512)
            if ot == 0:
                S.op("act", lambda e: e.activation(out=BIG[:, dt, sl], in_=bk[:], func=AF.Copy), reads=[bk.r], writes=[BIGr[dt]])
            elif ot == 1:
                S.op("act", lambda e: e.activation(out=TMP[:, tb, :], in_=bk[:], func=AF.Copy), reads=[bk.r], writes=[TMPr[tb]])
            else:
                S.op("dve", lambda e: e.tensor_tensor(out=BIG[:, 8 + dt, sl], in0=bk[:], in1=TMP[:, tb, :], op=ALU.mult),
                     reads=[bk.r, TMPr[tb]], writes=[BIGr[8 + dt]])
        def conv_dt(dt):
            V = BIG[:, 8 + dt, :]
            rd = [BIGr[8 + dt], CVW.r, NCVW.r, NL.r]
            S.op("dve", lambda e, V=V, dt=dt: e.tensor_scalar(out=XN[:], in0=V, scalar1=CVW[:, 1, dt:dt + 1], scalar2=None, op0=ALU.mult), reads=rd, writes=[XN.r])
            S.op("dve", lambda e, V=V, dt=dt: e.scalar_tensor_tensor(out=XN[:, 1:NTOK], in0=V[:, 0:NTOK - 1], scalar=CVW[:, 0, dt:dt + 1], in1=XN[:, 1:NTOK], op0=ALU.mult, op1=ALU.add),
                 reads=rd + [XN.r], writes=[XN.r])
            S.op("dve", lambda e, V=V, dt=dt: e.scalar_tensor_tensor(out=XN[:, 0:NTOK - 1], in0=V[:, 1:NTOK], scalar=CVW[:, 2, dt:dt + 1], in1=XN[:, 0:NTOK - 1], op0=ALU.mult, op1=ALU.add),
                 reads=rd + [XN.r], writes=[XN.r])
            S.op("dve", lambda e, V=V: e.tensor_tensor(out=CB[:, 0:5], in0=V[:, SEGL - 1:NTOK - 1:SEGL], in1=NL[:, 1:NSEG], op=ALU.mult), reads=rd, writes=[CB.r])
            S.op("dve", lambda e, dt=dt: e.scalar_tensor_tensor(out=XN[:, SEGL:NTOK:SEGL], in0=CB[:, 0:5], scalar=NCVW[:, 0, dt:dt + 1], in1=XN[:, SEGL:NTOK:SEGL], op0=ALU.mult, op1=ALU.add),
                 reads=rd + [CB.r, XN.r], writes=[XN.r])
            S.op("dve", lambda e, V=V: e.tensor_tensor(out=CB[:, 0:5], in0=V[:, SEGL:NTOK:SEGL], in1=NL[:, 1:NSEG], op=ALU.mult), reads=rd + [CB.r], writes=[CB.r])
            S.op("dve", lambda e, dt=dt: e.scalar_tensor_tensor(out=XN[:, SEGL - 1:NTOK - 1:SEGL], in0=CB[:, 0:5], scalar=NCVW[:, 2, dt:dt + 1], in1=XN[:, SEGL - 1:NTOK - 1:SEGL], op0=ALU.mult, op1=ALU.add),
                 reads=rd + [CB.r, XN.r], writes=[XN.r])
            S.op("dve", lambda e, V=V, dt=dt: e.tensor_tensor(out=V, in0=BIG[:, dt, :], in1=XN[:], op=ALU.mult), reads=[BIGr[dt], XN.r], writes=[BIGr[8 + dt]])

        proj(cv_w_in_d, D, [[dt * 128, D + dt * 128, 2 * D + dt * 128] for dt in range(8)], ht_src, ht_reads, cons, after=conv_dt, name=("mix", i))
        upd = x_update(16)
        proj(cv_w_out_d, D, [[c * 512 + o * 128 for o in range(4)] for c in range(2)],
             lambda kt, tb: BIG[:, 8 + kt, tb * 512:(tb + 1) * 512], lambda kt: [BIGr[8 + kt]],
             lambda ci, ot, tb, bk: upd(ci * 4 + ot, tb, bk), hint=lambda: load_chunk((("f1", i, 0), 0), ffn_w1_d[i], D, [o * 128 for o in range(4)]))

    def gmlp_layer(i):
        def consu(ci, ot, tb, bk):
            S.op("act", lambda e: e.activation(out=BIG[:, ci * 4 + ot, tb * 512:(tb + 1) * 512], in_=bk[:], func=AF.Gelu_apprx_tanh),
                 reads=[bk.r], writes=[BIGr[ci * 4 + ot]])
        proj(gm_w_in_d, D, [[c * 512 + o * 128 for o in range(4)] for c in range(4)], ht_src, ht_reads, consu, name=("mix", i))
        VTs = [nc.alloc_sbuf_tensor_at("VT%d" % k, [128, 2048], BF16, offset=RSTD_OFF + 4096 * k) for k in range(2)]
        VNs = [nc.alloc_sbuf_tensor_at("VN0", [128, 2048], BF16, offset=RSTD_OFF + 8192),
               nc.alloc_sbuf_tensor_at("VN1", [128, 2048], BF16, offset=GEN_END)]
        VTrs, VNrs = [Res("VT0"), Res("VT1")], [Res("VN0"), Res("VN1")]
        WST = nc.alloc_sbuf_tensor_at("WST", [128, 16, 128], BF16, offset=TMP_OFF + 2048)
        WSTr = Res("WST")
        S.barrier()
        stg_slots[:] = [1, 2]
        for sl4 in range(4):
            for pc in range(2):
                s = next_stg()
                stv = STG[:, s, :].rearrange("p (k c) -> p k c", k=4)
                dma(stv, wview(gm_w_in_d, pc * 4, 4, 2 * D + sl4 * 512, 512), writes=[STGr[s]])
                S.op("act", lambda e, s=s, sl4=sl4, pc=pc: e.activation(out=WB[:, sl4, pc * 2048:(pc + 1) * 2048], in_=STG[:, s, :], func=AF.Copy), reads=[STGr[s]], writes=[WBr[sl4]])
        dma(STG[:, 1, :].rearrange("p (g q) -> p g q", g=16), gm_wsT_d, writes=[STGr[1]])
        S.op("pool", lambda e: e.tensor_copy(out=WST[:].rearrange("p g q -> p (g q)"), in_=STG[:, 1, :]), reads=[STGr[1]], writes=[WSTr])
        dma(STG[:, 2, :], gm_bs_d.partition_broadcast(128), writes=[STGr[2]])
        VB = {}

        def vmm_pe(tt):
            tsl = slice(tt * 128, (tt + 1) * 128)
            bks = []
            for cb in range(4):
                bk = next_bank()
                bks.append(bk)
                for kt in range(8):
                    S.op("pe", lambda e, kt=kt, cb=cb, bk=bk, tsl=tsl: e.matmul(bk[:], lhsT=HT[:, kt, tsl], rhs=WB[:, cb, kt * 512:(kt + 1) * 512], start=(kt == 0), stop=(kt == 7)),
                         reads=[HT.r[kt], WBr[cb]], writes=[bk.r])
            VB[tt] = bks

        def vmm_evac(tt):
            VT, VTr = VTs[tt % 2], VTrs[tt % 2]
            c0 = 16 * (tt % 2)
            for cb, bk in enumerate(VB.pop(tt)):
                S.op("act", lambda e, cb=cb, bk=bk: e.activation(out=VT[:, cb * 512:(cb + 1) * 512], in_=bk[:], func=AF.Gelu_apprx_tanh, accum_out=ST1[:, c0 + cb:c0 + cb + 1]),
                     reads=[bk.r], writes=[VTr, ST1.r[tt % 2]])

        def chain(tt):
            VT, VN, VTr, VNr = VTs[tt % 2], VNs[tt % 2], VTrs[tt % 2], VNrs[tt % 2]
            c0 = 16 * (tt % 2)
            SR = ST1.r[tt % 2]
            cs = lambda a_, b_: ST1[:, c0 + a_:c0 + b_]
            S.op("act", lambda e: e.activation(out=VN[:], in_=VT[:], func=AF.Square, accum_out=cs(4, 5)), reads=[VTr], writes=[VNr, SR])
            S.op("dve", lambda e: e.tensor_tensor(out=cs(5, 7), in0=cs(0, 2), in1=cs(2, 4), op=ALU.add), reads=[SR], writes=[SR])
            S.op("dve", lambda e: e.tensor_tensor(out=cs(7, 8), in0=cs(5, 6), in1=cs(6, 7), op=ALU.add), reads=[SR], writes=[SR])
            S.op("dve", lambda e: e.tensor_scalar(out=cs(8, 9), in0=cs(7, 8), scalar1=1.0 / 2048, scalar2=None, op0=ALU.mult), reads=[SR], writes=[SR])
            S.op("dve", lambda e: e.tensor_tensor(out=cs(9, 10), in0=cs(8, 9), in1=cs(8, 9), op=ALU.mult), reads=[SR], writes=[SR])
            S.op("dve", lambda e: e.scalar_tensor_tensor(out=cs(10, 11), in0=cs(4, 5), scalar=1.0 / 2048, in1=cs(9, 10), op0=ALU.mult, op1=ALU.subtract), reads=[SR], writes=[SR])
            S.op("act", lambda e: e.activation(out=cs(11, 12), in_=cs(10, 11), func=AF.Sqrt, bias=EPSB[:, 0:1], scale=1.0), reads=[SR, EPSB.r], writes=[SR])
            S.op("dve", lambda e: e.reciprocal(out=cs(12, 13), in_=cs(11, 12)), reads=[SR], writes=[SR])
            S.op("dve", lambda e: e.tensor_scalar(out=VN[:], in0=VT[:], scalar1=cs(8, 9), scalar2=cs(12, 13), op0=ALU.subtract, op1=ALU.mult),
                 reads=[VTr, SR], writes=[VNr])

        def smm(tt):
            tsl = slice(tt * 128, (tt + 1) * 128)
            VN, VNr = VNs[tt % 2], VNrs[tt % 2]
            for b4 in range(4):
                bk = next_bank()
                for g4 in range(4):
                    g = b4 * 4 + g4
                    S.op("pe", lambda e, g=g, g4=g4, bk=bk: e.matmul(bk[:, g4 * 128:(g4 + 1) * 128], lhsT=VN[:, g * 128:(g + 1) * 128], rhs=WST[:, g, :], start=True, stop=True),
                         reads=[VNr, WSTr], writes=[bk.r])
                S.op("dve", lambda e, b4=b4, bk=bk: e.tensor_tensor(out=TMP[:, 0, :], in0=bk[:], in1=STG[:, 2, b4 * 512:(b4 + 1) * 512], op=ALU.add),
                     reads=[bk.r, STGr[2]], writes=[TMPr[0]])
                S.op("dve", lambda e, b4=b4: e.tensor_tensor(out=BIG[:, b4 * 4:(b4 + 1) * 4, tsl], in0=TMP[:, 0, :].rearrange("p (g q) -> p g q", g=4),
                                                      in1=BIG[:, b4 * 4:(b4 + 1) * 4, tsl], op=ALU.mult),
                     reads=[TMPr[0]] + [BIGr[b4 * 4 + k] for k in range(4)], writes=[BIGr[b4 * 4 + k] for k in range(4)])
        vmm_pe(0)
        vmm_evac(0)
        for tt in range(12):
            if tt + 1 < 12:
                vmm_pe(tt + 1)
            chain(tt)
            if tt + 1 < 12:
                vmm_evac(tt + 1)
            smm(tt)
        S.barrier()
        stg_slots[:] = [0, 1, 2]
        upd = x_update(16)
        proj(gm_w_out_d, 2 * D, [[c * 256, c * 256 + 128] for c in range(4)], big_src, big_reads,
             lambda ci, ot, tb, bk: upd(ci * 2 + ot, tb, bk), hint=lambda: load_chunk((("f1", i, 0), 0), ffn_w1_d[i], D, [o * 128 for o in range(4)]))

    def ssm_layer(i, j):
        def consu(ci, ot, tb, bk):
            S.op("act", lambda e: e.activation(out=BIG[:, ci * 4 + ot, tb * 512:(tb + 1) * 512], in_=bk[:], func=AF.Copy),
                 reads=[bk.r], writes=[BIGr[ci * 4 + ot]])
        proj(ssm_w_in_d[j], D, [[c * 512 + o * 128 for o in range(4)] for c in range(2)], ht_src, ht_reads, consu, name=("mix", i))
        oa = [ARENA0]
        ob = [BIG_OFF + 8 * NTOK * 2]

        def sa(name, shape, dt, reg=oa):
            nb = int(np.prod(shape[1:])) * (4 if dt in (F32, I32) else 2)
            nb = (nb + 31) // 32 * 32
            t = nc.alloc_sbuf_tensor_at("%s_%d" % (name, i), list(shape), dt, offset=reg[0])
            reg[0] += nb
            return t

        N4 = NG * NSEG * NSLOT
        Xre, Xim, Gre, Gim, T1, T2 = [sa(n, [128, NG, NSEG, NSLOT], F32) for n in ("Xre", "Xim", "Gre", "Gim", "T1", "T2")]
        PTAB = sa("PTAB", [128, NG, 2, QLEN], F32)
        QTAB = sa("QTAB", [128, NG, 2, QLEN], F32)
        Pre, Pim, Qre, Qim = PTAB[:, :, 0, :], PTAB[:, :, 1, :], QTAB[:, :, 0, :], QTAB[:, :, 1, :]
        COEF = sa("COEF", [128, NG, NSEG, NSLOT], F32)
        Bsre, Bsim = [sa(n, [128, NG, 128], F32) for n in ("Bsre", "Bsim")]
        BCT = sa("BCT", [128, 4, NG, 16], F32)
        bfn = ("Bbre", "Bbim", "Csre", "Csni", "Cfre", "Cfni", "Cbre", "Cbni", "W1fr", "W1fi", "W1br", "W1bi", "W2")
        Bbre, Bbim, Csre, Csni, Cfre, Cfni, Cbre, Cbni, W1fr, W1fi, W1br, W1bi, W2 = [sa(n, [128, NG, 128], BF16, ob) for n in bfn]
        Ub = [sa("U%d" % k, [128, NG, NSEG, NSLOT], BF16) for k in range(2)]
        Hre = [sa("Hre%d" % k, [128, NG, NSEG, NSLOT], BF16, ob) for k in range(2)]
        Him = [sa("Him%d" % k, [128, NG, NSEG, NSLOT], BF16, ob) for k in range(2)]
        Ysb = sa("Ysb", [128, 8, NCOL], BF16)
        assert oa[0] <= BIG_OFF, (oa[0], BIG_OFF)
        pwBr, pwBi, pwCr, pwCi, wpr = [sa(n, [128, 64, 8], F32, ob) for n in ("pwBr", "pwBi", "pwCr", "pwCi", "wpr")]
        WIP = sa("WIP", [128, 64, 8, 2], F32, ob)
        nwpi, wpi = WIP[:, :, :, 0], WIP[:, :, :, 1]
        g64 = {}
        for n in ("LR", "LI", "DTt", "ANG", "LRDT", "Cc", "Sn", "EP", "EN", "SSg", "nur", "nui", "mur", "mui", "abr", "abi", "fr", "fi",
                  "ta", "tb", "tc", "mu8r", "mu8i", "k1r", "k1i", "k3r", "k3i", "rho8", "spr", "spi", "stPr", "stPi", "stQr", "stQi", "h0r", "h0i", "dpp"):
            g64[n] = sa(n, [128, 64], F32, ob)
        angI = sa("angI", [128, 64], I32, ob)
        Er, Ei = [sa(n, [128, 64, NSEG], F32, ob) for n in ("Er", "Ei")]
        STOr, STOi = [sa(n, [128, 64, NSEG], F32, ob) for n in ("STOr", "STOi")]
        NSG = sa("NSG", [128, 1], F32, ob)
        RG = Res("ssmgen%d" % i)
        RW = Res("ssmw%d" % i)
        RU = [Res("ssmU%d_%d" % (i, k)) for k in range(2)]
        RH = [Res("ssmH%d_%d" % (i, k)) for k in range(2)]
        RY = Res("ssmY%d" % i)
        G = g64

        def dv(fn, reads=(), writes=()):
            S.op("dve", fn, reads=[RG] + list(reads), writes=[RG] + list(writes))

        def tt(out, a, b, op, **kw):
            dv(lambda e: e.tensor_tensor(out=out, in0=a, in1=b, op=op), **kw)

        def cmul(o_re, o_im, a_re, a_im, b_re, b_im, t1, t2, **kw):
            tt(t1, a_re, b_re, ALU.mult, **kw)
            tt(t2, a_im, b_im, ALU.mult, **kw)
            dv(lambda e: e.tensor_tensor(out=t2, in0=t1, in1=t2, op=ALU.subtract), **kw)
            tt(t1, a_re, b_im, ALU.mult, **kw)
            dv(lambda e: e.tensor_tensor(out=o_im, in0=a_im, in1=b_re, op=ALU.mult), **kw)
            dv(lambda e: e.tensor_tensor(out=o_im, in0=o_im, in1=t1, op=ALU.add), **kw)
            dv(lambda e: e.tensor_copy(out=o_re, in_=t2), **kw)

        sm = ssm_small_d[j]
        nd = [HT.r, XN.r, RSTD.r] + list(TMPr)
        dma(G["LR"][:], sm[0], reads=nd, writes=[RG]); dma(G["LI"][:], sm[1], writes=[RG]); dma(G["DTt"][:], sm[2], writes=[RG])
        dma(G["h0r"][:], h0_d[j, 0], writes=[RG]); dma(G["h0i"][:], h0_d[j, 1], writes=[RG]); dma(G["dpp"][:], ssm_dpp_d[j], writes=[RG])
        dv(lambda e: e.tensor_scalar(out=NSG[:], in0=SGF[:, 0:1], scalar1=-1.0, scalar2=None, op0=ALU.mult), reads=[SGF.r])
        S.op("act", lambda e: e.activation(out=G["DTt"][:], in_=G["DTt"][:], func=AF.Exp), reads=[RG], writes=[RG])
        tt(G["ANG"][:], G["LI"][:], G["DTt"][:], ALU.mult)
        tt(G["LRDT"][:], G["LR"][:], G["DTt"][:], ALU.mult)
        TWO_PI_ = 2.0 * math.pi

        def sinlike(out, shift):
            dv(lambda e: e.tensor_scalar(out=G["ta"][:], in0=G["ANG"][:], scalar1=1.0 / TWO_PI_, scalar2=shift, op0=ALU.mult, op1=ALU.add))
            dv(lambda e: e.tensor_copy(out=angI[:], in_=G["ta"][:]))
            dv(lambda e: e.tensor_copy(out=G["tb"][:], in_=angI[:]))
            tt(G["ta"][:], G["ta"][:], G["tb"][:], ALU.subtract)
            dv(lambda e: e.tensor_scalar(out=G["tb"][:], in0=G["ta"][:], scalar1=0.0, scalar2=None, op0=ALU.is_lt))
            tt(G["ta"][:], G["ta"][:], G["tb"][:], ALU.add)
            dv(lambda e: e.tensor_scalar(out=G["ta"][:], in0=G["ta"][:], scalar1=TWO_PI_, scalar2=-math.pi, op0=ALU.mult, op1=ALU.add))
            dv(lambda e: e.tensor_scalar(out=G["ta"][:], in0=G["ta"][:], scalar1=-math.pi, scalar2=math.pi, op0=ALU.max, op1=ALU.min))
            S.op("act", lambda e: e.activation(out=out, in_=G["ta"][:], func=AF.Sin), reads=[RG], writes=[RG])
        sinlike(G["Sn"][:], 0.5)
        sinlike(G["Cc"][:], 0.75)
        S.op("act", lambda e: e.activation(out=G["EP"][:], in_=G["LRDT"][:], func=AF.Exp, scale=SGF[:, 0:1]), reads=[RG, SGF.r], writes=[RG])
        S.op("act", lambda e: e.activation(out=G["EN"][:], in_=G["LRDT"][:], func=AF.Exp, scale=NSG[:, 0:1]), reads=[RG], writes=[RG])
        S.op("act", lambda e: e.activation(out=G["ta"][:], in_=G["LRDT"][:], func=AF.Exp), reads=[RG], writes=[RG])
        S.op("act", lambda e: e.activation(out=G["rho8"][:], in_=G["LRDT"][:], func=AF.Exp, scale=8.0), reads=[RG], writes=[RG])
        tt(G["abr"][:], G["ta"][:], G["Cc"][:], ALU.mult)
        tt(G["abi"][:], G["ta"][:], G["Sn"][:], ALU.mult)
        dv(lambda e: e.tensor_scalar(out=G["SSg"][:], in0=G["Sn"][:], scalar1=SGF[:, 0:1], scalar2=None, op0=ALU.mult), reads=[SGF.r])
        tt(G["nur"][:], G["EP"][:], G["Cc"][:], ALU.mult)
        tt(G["nui"][:], G["EP"][:], G["SSg"][:], ALU.mult)
        tt(G["mur"][:], G["EN"][:], G["Cc"][:], ALU.mult)
        tt(G["mui"][:], G["EN"][:], G["SSg"][:], ALU.mult)
        dv(lambda e: e.tensor_scalar(out=G["mui"][:], in0=G["mui"][:], scalar1=-1.0, scalar2=None, op0=ALU.mult))
        dv(lambda e: e.tensor_scalar(out=G["ta"][:], in0=G["abr"][:], scalar1=-1.0, scalar2=None, op0=ALU.add))
        tt(G["tb"][:], G["ta"][:], G["LR"][:], ALU.mult)
        tt(G["tc"][:], G["abi"][:], G["LI"][:], ALU.mult)
        tt(G["fr"][:], G["tb"][:], G["tc"][:], ALU.add)
        tt(G["tb"][:], G["abi"][:], G["LR"][:], ALU.mult)
        tt(G["tc"][:], G["ta"][:], G["LI"][:], ALU.mult)
        tt(G["fi"][:], G["tb"][:], G["tc"][:], ALU.subtract)
        tt(G["tb"][:], G["LR"][:], G["LR"][:], ALU.mult)
        tt(G["tc"][:], G["LI"][:], G["LI"][:], ALU.mult)
        tt(G["tb"][:], G["tb"][:], G["tc"][:], ALU.add)
        dv(lambda e: e.reciprocal(out=G["tb"][:], in_=G["tb"][:]))
        tt(G["fr"][:], G["fr"][:], G["tb"][:], ALU.mult)
        tt(G["fi"][:], G["fi"][:], G["tb"][:], ALU.mult)
        dv(lambda e: e.memset(pwCr[:, :, 0:1], 1.0)); dv(lambda e: e.memset(pwCi[:, :, 0:1], 0.0))
        dv(lambda e: e.memset(pwBr[:, :, 0:1], 1.0)); dv(lambda e: e.memset(pwBi[:, :, 0:1], 0.0))
        for t in range(1, 8):
            cmul(pwCr[:, :, t], pwCi[:, :, t], pwCr[:, :, t - 1], pwCi[:, :, t - 1], G["nur"][:], G["nui"][:], G["tb"][:], G["tc"][:])
            cmul(pwBr[:, :, t], pwBi[:, :, t], pwBr[:, :, t - 1], pwBi[:, :, t - 1], G["mur"][:], G["mui"][:], G["tb"][:], G["tc"][:])
        cmul(G["mu8r"][:], G["mu8i"][:], pwBr[:, :, 7], pwBi[:, :, 7], G["mur"][:], G["mui"][:], G["tb"][:], G["tc"][:])
        dv(lambda e: e.memset(G["k1r"][:], 1.0)); dv(lambda e: e.memset(G["k1i"][:], 0.0))
        dv(lambda e: e.tensor_copy(out=G["k1r"][0:64, :], in_=pwCr[0:64, :, 7])); dv(lambda e: e.tensor_copy(out=G["k1i"][0:64, :], in_=pwCi[0:64, :, 7]))
        dv(lambda e: e.tensor_copy(out=G["k3r"][0:64, :], in_=G["nur"][0:64, :])); dv(lambda e: e.tensor_copy(out=G["k3i"][0:64, :], in_=G["nui"][0:64, :]))
        dv(lambda e: e.tensor_copy(out=G["k3r"][64:128, :], in_=G["mu8r"][64:128, :])); dv(lambda e: e.tensor_copy(out=G["k3i"][64:128, :], in_=G["mu8i"][64:128, :]))
        for hh in range(2):
            t1v = Er[:].rearrange("p g s -> p (g s)")[:, 0:256].rearrange("p (g t) -> p g t", g=64)
            t2v = Ei[:].rearrange("p g s -> p (g s)")[:, 0:256].rearrange("p (g t) -> p g t", g=64)
            frb = G["fr"][:].unsqueeze(2).to_broadcast([128, 64, 4]); fib = G["fi"][:].unsqueeze(2).to_broadcast([128, 64, 4])
            sl_ = slice(4 * hh, 4 * hh + 4)
            cmul(pwBr[:, :, sl_], pwBi[:, :, sl_], pwBr[:, :, sl_], pwBi[:, :, sl_], frb, fib, t1v, t2v)
        dv(lambda e: e.tensor_copy(out=wpr[:, :, 0], in_=G["Cc"][:])); dv(lambda e: e.tensor_copy(out=wpi[:, :, 0], in_=G["Sn"][:]))
        for _ in range(3):
            cmul(wpr[:, :, 0], wpi[:, :, 0], wpr[:, :, 0], wpi[:, :, 0], wpr[:, :, 0], wpi[:, :, 0], G["tb"][:], G["tc"][:])
        dv(lambda e: e.tensor_scalar(out=wpi[:, :, 0], in0=wpi[:, :, 0], scalar1=NSG[:, 0:1], scalar2=None, op0=ALU.mult))
        for k in range(1, 8):
            cmul(wpr[:, :, k], wpi[:, :, k], wpr[:, :, k - 1], wpi[:, :, k - 1], wpr[:, :, k - 1], wpi[:, :, k - 1], G["tb"][:], G["tc"][:])
        dv(lambda e: e.tensor_scalar(out=nwpi, in0=wpi, scalar1=-1.0, scalar2=None, op0=ALU.mult))
        dv(lambda e: e.tensor_copy(out=G["spr"][0:64, :], in_=wpr[0:64, :, 0])); dv(lambda e: e.tensor_copy(out=G["spi"][0:64, :], in_=nwpi[0:64, :, 0]))
        dv(lambda e: e.tensor_copy(out=G["spr"][64:128, :], in_=wpr[64:128, :, 7])); dv(lambda e: e.tensor_copy(out=G["spi"][64:128, :], in_=nwpi[64:128, :, 7]))
        cmul(G["stPr"][:], G["stPi"][:], G["k1r"][:], G["k1i"][:], G["spr"][:], G["spi"][:], G["tb"][:], G["tc"][:])
        dv(lambda e: e.tensor_scalar(out=G["ta"][:], in0=G["spi"][:], scalar1=-1.0, scalar2=None, op0=ALU.mult))
        cmul(G["stQr"][:], G["stQi"][:], G["k3r"][:], G["k3i"][:], G["spr"][:], G["ta"][:], G["tb"][:], G["tc"][:])
        dv(lambda e: e.tensor_copy(out=Er[0:64, :, 0], in_=wpr[0:64, :, 5])); dv(lambda e: e.tensor_copy(out=Ei[0:64, :, 0], in_=nwpi[0:64, :, 5]))
        dv(lambda e: e.tensor_copy(out=Er[64:128, :, 0], in_=wpr[64:128, :, 7])); dv(lambda e: e.tensor_copy(out=Ei[64:128, :, 0], in_=wpi[64:128, :, 7]))
        for sg_ in range(1, NSEG):
            cmul(Er[:, :, sg_], Ei[:, :, sg_], Er[:, :, sg_ - 1], Ei[:, :, sg_ - 1], wpr[:, :, 5], nwpi[:, :, 5], G["tb"][:], G["tc"][:])
        S.barrier()

        flat = lambda t_: t_[:].rearrange("p a b c -> p (a b c)")
        PT1 = sa("PT1", [128, NG, 2, 64], F32)
        PT2 = sa("PT2", [128, NG, 2, 64], F32, ob)
        assert oa[0] <= BIG_OFF, (oa[0], BIG_OFF)
        assert ob[0] <= top - 256, (ob[0], top)
        R_T1, R_T2, R_X, R_Gs, R_P, R_Q, R_CO = [Res("ssm_%s_%d" % (n, i)) for n in ("T1", "T2", "X", "Gs", "P", "Q", "CO")]
        R_Bs, R_Cs, R_Bb, R_W1, R_W2, R_STO, R_BCT, R_PT = [Res("ssm_%s_%d" % (n, i)) for n in ("Bs", "Cs", "Bb", "W1", "W2", "STO", "BCT", "PT")]

        for tl in (Cfre, Cfni, Cbre, Cbni):
            S.op("pool", lambda e, tl=tl: e.memset(tl[:], 0.0), writes=[R_Cs])
        for tl in (W1fr, W1fi, W1br, W1bi):
            S.op("pool", lambda e, tl=tl: e.memset(tl[:], 0.0), writes=[R_W1])

        def Dv(fn, r=(), w=()):
            S.op("dve", fn, reads=list(r), writes=list(w))

        def Pl(fn, r=(), w=()):
            S.op("pool", fn, reads=list(r), writes=list(w))

        def TT(eng, out, a_, b_, op, r=(), w=()):
            S.op(eng, lambda e: e.tensor_tensor(out=out, in0=a_, in1=b_, op=op), reads=list(r), writes=list(w))

        def gen_table(blk, which, part="all"):
            g0 = blk * NG
            gs = slice(g0, g0 + NG)
            (TB, s_r, s_i, Rt) = ((PTAB, G["stPr"], G["stPi"], R_P), (QTAB, G["stQr"], G["stQi"], R_Q))[which]
            if part in ("all", "lo"):
                Pl(lambda e: e.tensor_copy(out=TB[:, :, 0, 0], in_=s_r[:, gs]), r=[RG], w=[Rt])
                Pl(lambda e: e.tensor_copy(out=TB[:, :, 1, 0], in_=s_i[:, gs]), r=[RG], w=[Rt])
            L = 1
            k = 0
            while L < QLEN:
                ntot = min(L, QLEN - L)
                on_dve = (L >= 64) and part != "all"
                do = (part == "all") or (part == "hi" and on_dve) or (part == "lo" and not on_dve)
                o0 = 0
                while do and o0 < ntot:
                    n_ = min(64, ntot - o0)
                    wr_b = wpr[:, gs, k:k + 1].unsqueeze(3).to_broadcast([128, NG, 2, n_])
                    wsel = WIP[:, gs, k, :] if which == 0 else WIP[:, gs, k, ::-1]
                    wi_b = wsel.unsqueeze(3).to_broadcast([128, NG, 2, n_])
                    a_ = TB[:, :, :, o0:o0 + n_]
                    a_sw = TB[:, :, ::-1, o0:o0 + n_]
                    o_ = TB[:, :, :, L + o0:L + o0 + n_]
                    if on_dve:
                        x1 = flat(T1)[:, 0:NG * 2 * n_].rearrange("p (g c m) -> p g c m", g=NG, c=2)
                        x2 = flat(T2)[:, 0:NG * 2 * n_].rearrange("p (g c m) -> p g c m", g=NG, c=2)
                        TT("dve", x1, a_, wr_b, ALU.mult, r=[Rt, RG], w=[R_T1])
                        TT("dve", x2, a_sw, wi_b, ALU.mult, r=[Rt, RG], w=[R_T2])
                        TT("dve", o_, x1, x2, ALU.add, r=[R_T1, R_T2], w=[Rt])
                    else:
                        x1, x2 = PT1[:, :, :, 0:n_], PT2[:, :, :, 0:n_]
                        TT("pool", x1, a_, wr_b, ALU.mult, r=[Rt, RG], w=[R_PT["1"]])
                        TT("pool", x2, a_sw, wi_b, ALU.mult, r=[Rt, RG], w=[R_PT["2"]])
                        TT("pool", o_, x1, x2, ALU.add, r=[R_PT], w=[Rt])
                    o0 += n_
                L += ntot
                k += 1

        def gen_coef(blk):
            g0 = blk * NG
            gs = slice(g0, g0 + NG)
            rb = G["rho8"][:, gs].unsqueeze(2).unsqueeze(3).to_broadcast([128, NG, NSEG, NSLOT])

            def Ac(out, in_, **kw):
                S.op("act", lambda e: e.activation(out=out, in_=in_, func=AF.Copy, **kw), reads=[RG, LINK.r], writes=[R_CO])
            Ac(COEF[:], rb)
            Ac(COEF[0:64, :, :, 0:1], COEF[0:64, :, :, 2:3], scale=0.0, bias=1.0)
            Ac(COEF[64:128, :, :, NSLOT - 1:NSLOT], COEF[64:128, :, :, 2:3], scale=0.0, bias=1.0)
            Ac(COEF[0:64, :, 0:1, 0:1], COEF[0:64, :, 0:1, 2:3], scale=0.0)
            Ac(COEF[64:128, :, NSEG - 1:NSEG, NSLOT - 1:NSLOT], COEF[64:128, :, NSEG - 1:NSEG, 2:3], scale=0.0)
            lf = LINK[0:64, 0:NSEG].unsqueeze(1).unsqueeze(3).to_broadcast([64, NG, NSEG, 1])
            lb = LINK[64:128, NSEG:2 * NSEG].unsqueeze(1).unsqueeze(3).to_broadcast([64, NG, NSEG, 1])
            Ac(COEF[0:64, :, :, 1:2], lf)
            Ac(COEF[64:128, :, :, NSLOT - 2:NSLOT - 1], lb)

        def gen_Bs(blk):
            g0 = blk * NG
            gs = slice(g0, g0 + NG)
            dma(BCT[:], ssm_bc_d[j].rearrange("k p g q -> p k g q")[:, :, gs, :], writes=[R_BCT])

            def outer(o_re, o_im, pr, pi, vr, vi, neg_im, Rout):
                prb = pr[:, gs, :].unsqueeze(3).to_broadcast([128, NG, 8, 16]); pib = pi[:, gs, :].unsqueeze(3).to_broadcast([128, NG, 8, 16])
                vrb = vr.unsqueeze(2).to_broadcast([128, NG, 8, 16]); vib = vi.unsqueeze(2).to_broadcast([128, NG, 8, 16])
                t1_ = flat(T1)[:, 0:512].rearrange("p (g t q) -> p g t q", g=NG, t=8)
                t2_ = flat(T2)[:, 0:512].rearrange("p (g t q) -> p g t q", g=NG, t=8)
                ore = o_re[:].rearrange("p g (t q) -> p g t q", t=8); oim = o_im[:].rearrange("p g (t q) -> p g t q", t=8)
                rin = [RG, R_BCT]
                TT("dve", t1_, prb, vrb, ALU.mult, r=rin, w=[R_T1]); TT("dve", t2_, pib, vib, ALU.mult, r=rin, w=[R_T2])
                TT("dve", ore, t1_, t2_, ALU.subtract, r=[R_T1, R_T2], w=[Rout])
                TT("dve", t1_, prb, vib, ALU.mult, r=rin, w=[R_T1]); TT("dve", t2_, pib, vrb, ALU.mult, r=rin, w=[R_T2])
                if neg_im:
                    Dv(lambda e: e.scalar_tensor_tensor(out=oim, in0=t1_, scalar=-1.0, in1=t2_, op0=ALU.mult, op1=ALU.subtract), r=[R_T1, R_T2], w=[Rout])
                else:
                    TT("dve", oim, t1_, t2_, ALU.add, r=[R_T1, R_T2], w=[Rout])
            outer(Bsre, Bsim, pwBr, pwBi, BCT[:, 0], BCT[:, 1], False, R_Bs)
            for (src, df, db) in ((Bsre, W1fr, W1br), (Bsim, W1fi, W1bi)):
                bk = next_bank()
                for g in range(NG):
                    S.op("pe", lambda e, src=src, g=g, bk=bk: e.transpose(bk[:, g * 128:(g + 1) * 128], src[:, g, :], IDENT[:]), reads=[R_Bs, IDENT.r], writes=[bk.r])
                bv = bk[:].rearrange("p (g m) -> p g m", g=NG)
                S.op("act", lambda e, bv=bv, df=df: e.activation(out=df[:, :, 0:64], in_=bv[:, :, 0:64], func=AF.Copy), reads=[bk.r], writes=[R_W1])
                S.op("act", lambda e, bv=bv, db=db: e.activation(out=db[:, :, 64:128], in_=bv[:, :, 64:128], func=AF.Copy), reads=[bk.r], writes=[R_W1])

        def gen_Cs(blk):
            g0 = blk * NG
            gs = slice(g0, g0 + NG)

            def outer(o_re, o_im, pr, pi, vr, vi, neg_im, Rout):
                prb = pr[:, gs, :].unsqueeze(3).to_broadcast([128, NG, 8, 16]); pib = pi[:, gs, :].unsqueeze(3).to_broadcast([128, NG, 8, 16])
                vrb = vr.unsqueeze(2).to_broadcast([128, NG, 8, 16]); vib = vi.unsqueeze(2).to_broadcast([128, NG, 8, 16])
                t1_ = flat(T1)[:, 0:512].rearrange("p (g t q) -> p g t q", g=NG, t=8)
                t2_ = flat(T2)[:, 0:512].rearrange("p (g t q) -> p g t q", g=NG, t=8)
                ore = o_re[:].rearrange("p g (t q) -> p g t q", t=8); oim = o_im[:].rearrange("p g (t q) -> p g t q", t=8)
                rin = [RG, R_BCT]
                TT("dve", t1_, prb, vrb, ALU.mult, r=rin, w=[R_T1]); TT("dve", t2_, pib, vib, ALU.mult, r=rin, w=[R_T2])
                TT("dve", ore, t1_, t2_, ALU.subtract, r=[R_T1, R_T2], w=[Rout])
                TT("dve", t1_, prb, vib, ALU.mult, r=rin, w=[R_T1]); TT("dve", t2_, pib, vrb, ALU.mult, r=rin, w=[R_T2])
                if neg_im:
                    Dv(lambda e: e.scalar_tensor_tensor(out=oim, in0=t1_, scalar=-1.0, in1=t2_, op0=ALU.mult, op1=ALU.subtract), r=[R_T1, R_T2], w=[Rout])
                else:
                    TT("dve", oim, t1_, t2_, ALU.add, r=[R_T1, R_T2], w=[Rout])
            outer(Csre, Csni, pwCr, pwCi, BCT[:, 2], BCT[:, 3], True, R_Cs)
            S.op("act", lambda e: e.activation(out=Bbre[:], in_=Bsre[:], func=AF.Copy), reads=[R_Bs], writes=[R_Bb])
            S.op("act", lambda e: e.activation(out=Bbim[:], in_=Bsim[:], func=AF.Copy), reads=[R_Bs], writes=[R_Bb])
            for (dst, src, lo, hi) in ((Cfre, Csre, 0, 64), (Cfni, Csni, 0, 64), (Cbre, Csre, 64, 128), (Cbni, Csni, 64, 128)):
                S.op("act", lambda e, dst=dst, src=src, lo=lo, hi=hi: e.activation(out=dst[lo:hi], in_=src[lo:hi], func=AF.Copy), reads=[R_Cs], writes=[R_Cs["m"]])
            bkf, bkb = next_bank(), next_bank()
            for g in range(NG):
                for (bk_, cr, cn) in ((bkf, Cfre, Cfni), (bkb, Cbre, Cbni)):
                    S.op("pe", lambda e, g=g, bk_=bk_, cr=cr: e.matmul(bk_[:, g * 128:(g + 1) * 128], lhsT=Bbre[:, g, :], rhs=cr[:, g, :], start=True, stop=False), reads=[R_Bb, R_Cs], writes=[bk_.r])
                    S.op("pe", lambda e, g=g, bk_=bk_, cn=cn: e.matmul(bk_[:, g * 128:(g + 1) * 128], lhsT=Bbim[:, g, :], rhs=cn[:, g, :], start=False, stop=True), reads=[R_Bb, R_Cs], writes=[bk_.r])
            mLb = MASKL[:].unsqueeze(1).to_broadcast([128, NG, 128]); mUb = MASKU[:].unsqueeze(1).to_broadcast([128, NG, 128])
            t1w = flat(T1)[:, 0:512].rearrange("p (g m) -> p g m", g=NG); t2w = flat(T2)[:, 0:512].rearrange("p (g m) -> p g m", g=NG)
            TT("dve", t1w, bkf[:].rearrange("p (g m) -> p g m", g=NG), mLb, ALU.mult, r=[bkf.r, MASKL.r], w=[R_T1])
            TT("dve", t2w, bkb[:].rearrange("p (g m) -> p g m", g=NG), mUb, ALU.mult, r=[bkb.r, MASKU.r], w=[R_T2])
            TT("dve", t1w, t1w, t2w, ALU.add, r=[R_T1, R_T2], w=[R_T1])
            for g in range(NG):
                Dv(lambda e, g=g: e.scalar_tensor_tensor(out=W2[:, g, :], in0=IDENT[:], scalar=G["dpp"][:, g0 + g:g0 + g + 1], in1=t1w[:, g, :], op0=ALU.mult, op1=ALU.add),
                   r=[IDENT.r, RG, R_T1], w=[R_W2[g]])

        SBK = {}
        ssm_rr = [0]

        def next_bank():
            b_ = PS[4 + ssm_rr[0] % 4]
            ssm_rr[0] += 1
            return b_

        def usel(blk):
            g0 = blk * NG
            ct = blk // 2
            par = blk % 2
            gs = slice(g0, g0 + NG)
            U = Ub[par]
            Hr, Hi = Hre[par], Him[par]
            ubanks = [next_bank(), next_bank()]
            uv = BIG[:, ct, :].rearrange("p (c t) -> p c t", t=TCH)
            for g in range(NG):
                gl = (g0 + g) % 8
                bk = ubanks[g // 2]
                for t_ in range(8):
                    S.op("pe", lambda e, g=g, gl=gl, t_=t_, bk=bk: e.matmul(bk[:, (g % 2) * NCOL:(g % 2 + 1) * NCOL], lhsT=BAND[:, gl, 112 - 16 * t_:240 - 16 * t_],
                                                                            rhs=uv[:, :, t_], start=(t_ == 0), stop=(t_ == 7)),
                         reads=[BAND.r, BIGr[ct]], writes=[bk.r])
            for h in range(2):
                S.op("act", lambda e, h=h: e.activation(out=U[:, 2 * h:2 * h + 2, :, 0:NCH], in_=ubanks[h][:, 0:2 * NCOL].rearrange("p (g s k) -> p g s k", g=2, s=NSEG), func=AF.Copy),
                     reads=[ubanks[h].r], writes=[RU[par]])

        def sprime(blk):
            g0 = blk * NG
            ct = blk // 2
            par = blk % 2
            gs = slice(g0, g0 + NG)
            U = Ub[par]
            Hr, Hi = Hre[par], Him[par]
            sb_re = [PS[0], PS[1]]
            sb_im = [PS[2], PS[3]]
            for g in range(NG):
                for (bks, wf, wb_) in ((sb_re, W1fr, W1br), (sb_im, W1fi, W1bi)):
                    bk = bks[g // 2]
                    ov = bk[:, (g % 2) * 204:(g % 2 + 1) * 204].rearrange("p (s k) -> p s k", s=NSEG)
                    uin = U[:, g, :, 0:NCH]
                    for s_ in range(NSEG):
                        S.op("pe", lambda e, g=g, ov=ov, uin=uin, wf=wf, s_=s_: e.matmul(ov[:, s_, 2:34], lhsT=wf[:, g, :], rhs=uin[:, s_, :], start=True, stop=False, skip_group_check=True), reads=[R_W1, RU[par]], writes=[bk.r])
                        S.op("pe", lambda e, g=g, ov=ov, uin=uin, wb_=wb_, s_=s_: e.matmul(ov[:, s_, 0:32], lhsT=wb_[:, g, :], rhs=uin[:, s_, :], start=False, stop=True, skip_group_check=True), reads=[R_W1, RU[par]], writes=[bk.r])
            SBK[blk] = (sb_re, sb_im)

        def core(blk):
            g0 = blk * NG
            ct = blk // 2
            par = blk % 2
            gs = slice(g0, g0 + NG)
            U = Ub[par]
            Hr, Hi = Hre[par], Him[par]
            sb_re, sb_im = SBK.pop(blk)
            for h in range(2):
                sre = sb_re[h][:, 0:408].rearrange("p (g s k) -> p g s k", g=2, s=NSEG)
                sim_ = sb_im[h][:, 0:408].rearrange("p (g s k) -> p g s k", g=2, s=NSEG)

                def win(Tt):
                    base_ap = Tt[:, 2 * h:2 * h + 2, 0:NSLOT]
                    return bass.AP(tensor=base_ap.tensor, offset=base_ap.offset, ap=[list(base_ap.ap[0]), list(base_ap.ap[1]), [NCH, NSEG], [1, NSLOT]])
                pr_, pi_ = win(Pre), win(Pim)
                xr, xi = Xre[:, 2 * h:2 * h + 2], Xim[:, 2 * h:2 * h + 2]
                a1, a2 = T1[:, 2 * h:2 * h + 2], T2[:, 2 * h:2 * h + 2]
                rdb = [sb_re[h].r, sb_im[h].r, R_P]
                TT("dve", a1, sre, pr_, ALU.mult, r=rdb, w=[R_T1[h]]); TT("dve", a2, sim_, pi_, ALU.mult, r=rdb, w=[R_T2[h]])
                TT("dve", xr, a1, a2, ALU.subtract, r=[R_T1[h], R_T2[h]], w=[R_X["r%d" % h]])
                TT("dve", a1, sim_, pr_, ALU.mult, r=rdb, w=[R_T1[h]]); TT("dve", a2, sre, pi_, ALU.mult, r=rdb, w=[R_T2[h]])
                TT("dve", xi, a1, a2, ALU.add, r=[R_T1[h], R_T2[h]], w=[R_X["i%d" % h]])
            if blk + 1 < 16:
                gen_table(blk + 1, 0)
            Dv(lambda e: e.tensor_copy(out=Xre[0:64, :, 0, 1], in_=G["h0r"][0:64, gs]), r=[RG, R_X], w=[R_X])
            Dv(lambda e: e.tensor_copy(out=Xim[0:64, :, 0, 1], in_=G["h0i"][0:64, gs]), r=[RG], w=[R_X["hi0"]])
            Dv(lambda e: e.tensor_copy(out=Xre[64:128, :, 3, 32], in_=G["h0r"][64:128, gs]), r=[RG], w=[R_X["hr1"]])
            Dv(lambda e: e.tensor_copy(out=Xim[64:128, :, 3, 32], in_=G["h0i"][64:128, gs]), r=[RG], w=[R_X["hi1"]])
            for (Xs, Gs, nm) in ((Xre, Gre, "r"), (Xim, Gim, "i")):
                xf, gf, cf = flat(Xs), flat(Gs), flat(COEF)
                Dv(lambda e, xf=xf, gf=gf, cf=cf: e.tensor_tensor_scan(out=gf[0:64, :], data0=cf[0:64, :], data1=xf[0:64, :], initial=0.0, op0=ALU.mult, op1=ALU.add),
                   r=[R_X, R_CO], w=[R_Gs[nm + "f"]])
                Dv(lambda e, xf=xf, gf=gf, cf=cf: e.tensor_tensor_scan(out=gf[64:128, ::-1], data0=cf[64:128, ::-1], data1=xf[64:128, ::-1], initial=0.0, op0=ALU.mult, op1=ALU.add),
                   r=[R_X, R_CO], w=[R_Gs[nm + "b"]])
            if blk + 1 < 16:
                gen_coef(blk + 1)

            def win4(Tt):
                base_ap = Tt[:, :, 0:NSLOT]
                return bass.AP(tensor=base_ap.tensor, offset=base_ap.offset, ap=[list(base_ap.ap[0]), list(base_ap.ap[1]), [NCH, NSEG], [1, NSLOT]])
            qr_, qi_ = win4(Qre), win4(Qim)
            TT("dve", T1[:], Gre[:], qr_, ALU.mult, r=[R_Gs, R_Q], w=[R_T1]); TT("dve", T2[:], Gim[:], qi_, ALU.mult, r=[R_Gs, R_Q], w=[R_T2])
            TT("dve", Hr[:], T1[:], T2[:], ALU.subtract, r=[R_T1, R_T2], w=[RH[par]["r"]])
            TT("dve", T1[:], Gim[:], qr_, ALU.mult, r=[R_Gs, R_Q], w=[R_T1]); TT("dve", T2[:], Gre[:], qi_, ALU.mult, r=[R_Gs, R_Q], w=[R_T2])
            TT("dve", Hi[:], T1[:], T2[:], ALU.add, r=[R_T1, R_T2], w=[RH[par]["i"]])
            if blk + 1 < 16:
                gen_table(blk + 1, 1)
            for (lo, hi, sl_) in ((0, 64, NSLOT - 1), (64, 128, 0)):
                S.op("act", lambda e, lo=lo, hi=hi, sl_=sl_: e.activation(out=STOr[lo:hi, gs, :], in_=Gre[lo:hi, :, :, sl_], func=AF.Copy),
                     reads=[R_Gs], writes=[R_STO[(blk, lo, 0)]])
                S.op("act", lambda e, lo=lo, hi=hi, sl_=sl_: e.activation(out=STOi[lo:hi, gs, :], in_=Gim[lo:hi, :, :, sl_], func=AF.Copy),
                     reads=[R_Gs], writes=[R_STO[(blk, lo, 1)]])

        def ymm(blk):
            g0 = blk * NG
            ct = blk // 2
            par = blk % 2
            gs = slice(g0, g0 + NG)
            U = Ub[par]
            Hr, Hi = Hre[par], Him[par]
            ybanks = [next_bank(), next_bank()]
            for g in range(NG):
                bk = ybanks[g // 2]
                ov = bk[:, (g % 2) * 204:(g % 2 + 1) * 204].rearrange("p (s k) -> p s k", s=NSEG)[:, :, 0:NCH]
                for s_ in range(NSEG):
                    S.op("pe", lambda e, g=g, ov=ov, s_=s_: e.matmul(ov[:, s_, :], lhsT=W2[:, g, :], rhs=U[:, g, s_, 0:NCH], start=True, stop=False), reads=[R_W2, RU[par]], writes=[bk.r])
                    S.op("pe", lambda e, g=g, ov=ov, s_=s_: e.matmul(ov[:, s_, :], lhsT=Csre[:, g, :], rhs=Hr[:, g, s_, 1:33], start=False, stop=False), reads=[R_Cs, RH[par]], writes=[bk.r])
                    S.op("pe", lambda e, g=g, ov=ov, s_=s_: e.matmul(ov[:, s_, :], lhsT=Csni[:, g, :], rhs=Hi[:, g, s_, 1:33], start=False, stop=True), reads=[R_Cs, RH[par]], writes=[bk.r])
            for h in range(2):
                S.op("act", lambda e, h=h: e.activation(out=Ysb[:, par * NG + 2 * h:par * NG + 2 * h + 2, :].rearrange("p g (s k) -> p g s k", s=NSEG), in_=ybanks[h][:, 0:408].rearrange("p (g s k) -> p g s k", g=2, s=NSEG)[:, :, :, 0:NCH], func=AF.Copy),
                     reads=[ybanks[h].r], writes=[RY[(par, h)]])

        def selback(blk):
            g0 = blk * NG
            ct = blk // 2
            par = blk % 2
            gs = slice(g0, g0 + NG)
            U = Ub[par]
            Hr, Hi = Hre[par], Him[par]
            if par == 1:
                zv = HT[:, ct, :].rearrange("p (c t) -> p c t", t=TCH)
                for tp in range(4):
                    bk = next_bank()
                    for t2_ in range(2):
                        t_ = tp * 2 + t2_
                        for gl in range(8):
                            S.op("pe", lambda e, t_=t_, t2_=t2_, gl=gl, bk=bk: e.matmul(bk[:, t2_ * NCOL:(t2_ + 1) * NCOL], lhsT=BAND[:, t_, 112 - 16 * gl:240 - 16 * gl],
                                                                                        rhs=Ysb[:, gl, :], start=(gl == 0), stop=(gl == 7)),
                                 reads=[BAND.r, RY], writes=[bk.r])
                        S.op("act", lambda e, t_=t_, t2_=t2_, bk=bk: e.activation(out=zv[:, :, t_], in_=bk[:, t2_ * NCOL:(t2_ + 1) * NCOL], func=AF.Gelu_apprx_tanh),
                             reads=[bk.r], writes=[HT.r[ct]])
        S.mark("L%d ssmcore" % i)
        gen_table(0, 0)
        gen_coef(0)
        gen_table(0, 1)
        usel(0)
        gen_Bs(0)
        sprime(0)
        gen_Cs(0)
        for blk_ in range(16):
            core(blk_)
            nxt = blk_ + 1 < 16
            if nxt:
                usel(blk_ + 1)
            ymm(blk_)
            if nxt:
                gen_Bs(blk_ + 1)
                sprime(blk_ + 1)
            selback(blk_)
            if nxt:
                gen_Cs(blk_ + 1)
        sv = lambda tl: flat(tl)[:, 0:64 * NSEG].rearrange("p (g s) -> p g s", g=64)
        x1_, x2_ = sv(T1), sv(T2)
        y1_ = flat(Xre)[:, 0:64 * NSEG].rearrange("p (g s) -> p g s", g=64)
        TT("dve", x1_, STOr[:], Er[:], ALU.mult, r=[R_STO, RG], w=[R_T1]); TT("dve", x2_, STOi[:], Ei[:], ALU.mult, r=[R_STO, RG], w=[R_T2])
        TT("dve", y1_, x1_, x2_, ALU.subtract, r=[R_T1, R_T2], w=[R_X])
        TT("dve", x1_, STOr[:], Ei[:], ALU.mult, r=[R_STO, RG], w=[R_T1]); TT("dve", x2_, STOi[:], Er[:], ALU.mult, r=[R_STO, RG], w=[R_T2])
        TT("dve", STOi[:], x1_, x2_, ALU.add, r=[R_T1, R_T2], w=[R_STO])
        Dv(lambda e: e.tensor_copy(out=STOr[:], in_=y1_), r=[R_X], w=[R_STO])
        dma(st_d[j, 0], STOr[:].rearrange("p g s -> p (g s)"), reads=[R_STO])
        dma(st_d[j, 1], STOi[:].rearrange("p g s -> p (g s)"), reads=[R_STO])
        S.barrier()
        S.mark("L%d ssmout" % i)
        def conso(ci, ot, tb, bk):
            dt = ci * 2 + ot // 2
            if ot % 2 == 0:
                pend[tb] = bk
            else:
                bka = pend.pop(tb)
                Mt = MS[0]
                S.op("act", lambda e: e.activation(out=TMP[:, tb, :], in_=bk[:], func=AF.Sigmoid), reads=[bk.r], writes=[TMPr[tb]])
                S.op("dve", lambda e: e.tensor_tensor(out=TMP[:, tb, :], in0=bka[:], in1=TMP[:, tb, :], op=ALU.mult), reads=[bka.r, TMPr[tb]], writes=[TMPr[tb]])
                for h in range(2):
                    sg_ = tb * 2 + h
                    S.op("dve", lambda e, h=h, sg_=sg_: e.scalar_tensor_tensor(
                        out=XT[:, dt, sg_ * SEGL:(sg_ + 1) * SEGL], in0=TMP[:, tb, h * SEGL:(h + 1) * SEGL], scalar=Mt[:, 16 + dt, sg_:sg_ + 1],
                        in1=XT[:, dt, sg_ * SEGL:(sg_ + 1) * SEGL], op0=ALU.mult, op1=ALU.add),
                        reads=[TMPr[tb], Mt.r, XT.r[dt]], writes=[XT.r[dt]])
        pend = {}
        proj(ssm_w_out_d[j], D, [[c * 256, D + c * 256, c * 256 + 128, D + c * 256 + 128] for c in range(4)], ht_src, ht_reads, conso, hint=lambda: load_chunk((("f1", i, 0), 0), ffn_w1_d[i], D, [o * 128 for o in range(4)]))

    for li, i in enumerate(layers):
        kind, j = i % 3, i // 3
        Mc = MODSL[li % 2]
        MS[0] = Mc
        if li == 0:
            adaln_load(i, 0)
            for cb in range(12):
                if cb + 1 < 12:
                    adaln_load(i, cb + 1)
                adaln_block(i, cb, Mc)
        for (At, Gt, o0) in ((A1, GMIX, 8), (A2, GFFN, 32)):
            for dt in range(8):
                S.op("dve", lambda e, At=At, Gt=Gt, o0=o0, dt=dt, i=i, Mc=Mc: e.tensor_scalar(
                    out=At[:, dt, :], in0=Mc[:, o0 + dt, :], scalar1=1.0, scalar2=Gt[:, i, dt:dt + 1], op0=ALU.add, op1=ALU.mult),
                    reads=[Mc.r, Gt.r], writes=[At.r])
        S.mark("L%d norm1" % i)
        norm_mod(A1, lambda dt, Mc=Mc: Mc[:, 0 + dt, :])
        S.mark("L%d mixer" % i)
        if kind == 2:
            conv_layer(i)
        elif kind == 1:
            gmlp_layer(i)
        else:
            ssm_layer(i, j)
        S.barrier()
        S.mark("L%d norm2" % i)
        norm_mod(A2, lambda dt, Mc=Mc: Mc[:, 24 + dt, :])
        S.mark("L%d ffn" % i)
        upd2 = x_update(40)
        mix_hint = None
        if li + 1 < len(layers):
            ni = layers[li + 1]
            nk, nj = ni % 3, ni // 3
            if nk == 2:
                mix_hint = lambda ni=ni: load_chunk((("mix", ni), 0), cv_w_in_d, D, [0, D, 2 * D])
            elif nk == 1:
                mix_hint = lambda ni=ni: load_chunk((("mix", ni), 0), gm_w_in_d, D, [o * 128 for o in range(4)])
            else:
                mix_hint = lambda ni=ni, nj=nj: load_chunk((("mix", ni), 0), ssm_w_in_d[nj], D, [o * 128 for o in range(4)])
        stg_slots[:] = [0, 1, 2, 3, 4, 5]
        ada_q = [(layers[li + 1], cb, MODSL[(li + 1) % 2]) for cb in range(12)] if li + 1 < len(layers) else []
        ada_ld = []
        ada_tr = []

        def ada_step():
            if ada_tr:
                adaln_tr(*ada_tr.pop(0))
            if ada_ld:
                blk_ = ada_ld.pop(0)
                adaln_mm(*blk_)
                ada_tr.append(blk_)
            if ada_q:
                nx = ada_q.pop(0)
                adaln_load(nx[0], nx[1])
                ada_ld.append(nx)
        for fc in range(8):
            hb = fc % 2

            def cons1(ci, ot, tb, bk, hb=hb):
                tslot = (ot * NTB + tb) % 3
                S.op("act", lambda e: e.activation(out=TMP[:, tslot, :], in_=bk[:], func=AF.Relu), reads=[bk.r], writes=[TMPr[tslot]])
                S.op("act", lambda e: e.activation(out=BIG[:, hb * 4 + ot, tb * 512:(tb + 1) * 512], in_=TMP[:, tslot, :], func=AF.Square),
                     reads=[TMPr[tslot]], writes=[BIGr[hb * 4 + ot]])
            w1cols = lambda f: [f * 512 + o * 128 for o in range(4)]
            w2cols = [o * 128 for o in range(8)]
            w2v = lambda f: ffn_w2_d[i][f * 512:(f + 1) * 512, :]
            proj(ffn_w1_d[i], D, [w1cols(fc)], ht_src, ht_reads, cons1, name=("f1", i, fc),
                 hint=lambda fc=fc: load_chunk((("f2", i, fc), 0), w2v(fc), 512, w2cols))
            ada_step()
            proj(w2v(fc), 512, [w2cols],
                 lambda kt, tb, hb=hb: BIG[:, hb * 4 + kt, tb * 512:(tb + 1) * 512], lambda kt, hb=hb: [BIGr[hb * 4 + kt]],
                 lambda ci, ot, tb, bk: upd2(ot, tb, bk), name=("f2", i, fc),
                 hint=(lambda fc=fc: load_chunk((("f1", i, fc + 1), 0), ffn_w1_d[i], D, w1cols(fc + 1))) if fc < 7 else mix_hint)
            ada_step()
        while ada_ld or ada_q or ada_tr:
            ada_step()
        stg_slots[:] = [0, 1, 2]
        S.barrier()

    S.mark("final")
    if do_final:
        for dt in range(8):
            S.op("act", lambda e, dt=dt: e.activation(out=HT[:, dt, :], in_=XT[:, dt, :], func=AF.Square), reads=[XT.r[dt]], writes=[HT.r[dt]])
        for tb in range(NTB):
            bk = next_bank()
            for dt in range(8):
                S.op("pe", lambda e, dt=dt, tb=tb, bk=bk: e.matmul(bk[:], lhsT=ONES[:], rhs=HT[:, dt, tb * 512:(tb + 1) * 512], start=(dt == 0), stop=(dt == 7)),
                     reads=[ONES.r, HT.r[dt]], writes=[bk.r])
            S.op("act", lambda e, tb=tb, bk=bk: e.activation(out=RSTD[:, tb * 512:(tb + 1) * 512], in_=bk[:], func=AF.Sqrt, bias=EPSB[:, 0:1], scale=1.0),
                 reads=[bk.r, EPSB.r], writes=[RSTD.r[tb]])
            S.op("dve", lambda e, tb=tb: e.reciprocal(out=RSTD[:, tb * 512:(tb + 1) * 512], in_=RSTD[:, tb * 512:(tb + 1) * 512]),
                 reads=[RSTD.r[tb]], writes=[RSTD.r[tb]])
        yT_v = yT_d.rearrange("(kt p) n -> p kt n", p=128)
        for dt in range(8):
            S.op("dve", lambda e, dt=dt: e.scalar_tensor_tensor(out=XT[:, dt, :], in0=XT[:, dt, :], scalar=GFIN[:, dt:dt + 1], in1=RSTD[:], op0=ALU.mult, op1=ALU.mult),
                 reads=[XT.r[dt], RSTD.r, GFIN.r], writes=[XT.r[dt]])
            dma(yT_v[:, dt, :], XT[:, dt, :], reads=[XT.r[dt]])
    S.emit()
    es.close()
    nc._marks = S.marks
    return nc


def _consts():
    c = np.zeros((5, 128, 128), np.float32)
    c[0] = np.eye(128, dtype=np.float32)
    k = np.arange(128)
    tk = k // 16
    c[1] = (tk[None, :] >= tk[:, None]).astype(np.float32)
    c[2] = (tk[:, None] >= tk[None, :]).astype(np.float32)
    c[3, :64, 0] = 1.0
    c[3, 64:, 0] = -1.0
    freq = 1.0 / (10000.0 ** (np.arange(256, dtype=np.float64) / 256.0))
    c[3, :, 1] = (freq[:128] / (2 * np.pi)).astype(np.float32)
    c[3, :, 2] = (freq[128:] / (2 * np.pi)).astype(np.float32)
    band = np.zeros((128, 8, 240), np.float32)
    for a in range(8):
        for kk in range(16 * a, 16 * a + 16):
            band[kk, a, kk - 16 * a + 112] = 1.0
    return c, band.reshape(128, 8 * 240)


def _core_segments(c):
    if c < 4:
        return [("s", c, s) for s in range(4)] + [("p", 2 * c), ("p", 2 * c + 1)]
    return [("p", 8 + 6 * (c - 4) + s) for s in range(6)]


def make_in_maps(inp):
    f = lambda a: np.ascontiguousarray(np.asarray(a, dtype=np.float32))
    consts, band = _consts()
    shared = {
        "w_mod": f(inp["w_mod"]),
        "b_modT": f(np.asarray(inp["b_mod"]).reshape(4, 48, 128).transpose(0, 2, 1)),
        "g_mixT": f(np.asarray(inp["g_mix"]).reshape(4, 8, 128).transpose(0, 2, 1)),
        "g_ffnT": f(np.asarray(inp["g_ffn"]).reshape(4, 8, 128).transpose(0, 2, 1)),
        "g_finT": f(np.asarray(inp["g_final"]).reshape(8, 128).T),
        "ffn_w1": f(inp["ffn_w1"]), "ffn_w2": f(inp["ffn_w2"]),
        "ssm_w_in": f(inp["ssm_w_in"]), "ssm_w_out": f(inp["ssm_w_out"]),
        "gm_w_in": f(np.asarray(inp["gmlp_w_in"])[0]),
        "gm_wsT": f(np.asarray(inp["gmlp_w_s"])[0].transpose(2, 0, 1)),
        "gm_bs": f(np.asarray(inp["gmlp_b_s"])[0].reshape(1, 2048)),
        "gm_w_out": f(np.asarray(inp["gmlp_w_out"])[0]),
        "cv_w_in": f(np.asarray(inp["conv_w_in"])[0]),
        "cv_wT": f(np.asarray(inp["conv_w"])[0].reshape(3, 8, 128).transpose(2, 0, 1)),
        "cv_w_out": f(np.asarray(inp["conv_w_out"])[0]),
        "consts": consts, "band": band,
    }
    lr = np.asarray(inp["ssm_lam_re"]); li = np.asarray(inp["ssm_lam_im"]); ld = np.asarray(inp["ssm_log_dt"])
    small = np.zeros((2, 3, 128, 64), np.float32)
    bc = np.zeros((2, 4, 128, 64, 16), np.float32)
    dpp = np.zeros((2, 128, 64), np.float32)
    for j in range(2):
        small[j, 0] = lr[j].transpose(0, 2, 1).reshape(128, 64)
        small[j, 1] = li[j].transpose(0, 2, 1).reshape(128, 64)
        small[j, 2] = np.broadcast_to(ld[j][:, None, :], (2, 64, 64)).reshape(128, 64)
        bc[j, 0] = np.asarray(inp["ssm_b_re"])[j].transpose(0, 2, 1, 3).reshape(128, 64, 16)
        bc[j, 1] = np.asarray(inp["ssm_b_im"])[j].transpose(0, 2, 1, 3).reshape(128, 64, 16)
        bc[j, 2] = np.asarray(inp["ssm_c_re"])[j].transpose(0, 3, 1, 2).reshape(128, 64, 16)
        bc[j, 3] = np.asarray(inp["ssm_c_im"])[j].transpose(0, 3, 1, 2).reshape(128, 64, 16)
        dpp[j] = np.tile(np.asarray(inp["ssm_d"])[j].reshape(64, 16).T, (8, 1))
    shared["ssm_small"] = small; shared["ssm_bc"] = bc; shared["ssm_dpp"] = dpp
    xp = np.asarray(inp["x_prompt"]); xs = np.asarray(inp["x_sample"])
    cc = np.asarray(inp["c"]); cctx = np.asarray(inp["c_ctx"])
    sre = np.asarray(inp["state_ssm_re"]); sim = np.asarray(inp["state_ssm_im"])
    maps = []
    for c in range(8):
        segs = _core_segments(c)
        x = np.concatenate([xs[s[1], s[2] * 256:(s[2] + 1) * 256] if s[0] == "s" else xp[s[1]] for s in segs], axis=0)
        cond = np.stack([cc[s[1]] if s[0] == "s" else cctx for s in segs], axis=1)
        m = dict(shared)
        m["xT"] = f(x.T)
        m["condT"] = f(cond)
        is_s = c < 4
        m["flag"] = np.full((1, 1), 1.0 if is_s else 0.0, np.float32)
        lf = np.zeros(6, np.float32); lb = np.zeros(6, np.float32)
        if is_s:
            lf[1:4] = 1.0; lb[0:3] = 1.0
        m["link"] = np.concatenate([lf, lb]).reshape(1, 12)
        h0 = np.zeros((2, 2, 128, 64), np.float32)
        if is_s:
            for j in range(2):
                h0[j, 0, :64] = sre[c, j, 0].T; h0[j, 0, 64:] = sre[c, j, 1].T
                h0[j, 1, :64] = sim[c, j, 0].T; h0[j, 1, 64:] = sim[c, j, 1].T
        m["h0"] = h0
        maps.append(m)
    return maps


def assemble(results):
    yp = np.zeros((32, 256, D), np.float32)
    ys = np.zeros((4, 1024, D), np.float32)
    nre = np.zeros((32, 2, 2, 64, 64), np.float32)
    nim = np.zeros((32, 2, 2, 64, 64), np.float32)
    for c in range(8):
        y = np.asarray(results[c]["yT"]).T
        st = np.asarray(results[c]["st"]).reshape(2, 2, 2, 64, 64, NSEG)
        for si, s in enumerate(_core_segments(c)):
            blk = y[si * 256:(si + 1) * 256]
            if s[0] == "s":
                ys[s[1], s[2] * 256:(s[2] + 1) * 256] = blk
            else:
                yp[s[1]] = blk
                nre[s[1]] = st[:, 0, :, :, :, si].transpose(0, 1, 3, 2)
                nim[s[1]] = st[:, 1, :, :, :, si].transpose(0, 1, 3, 2)
    return yp, ys, nre, nim


_NC_CACHE = {}


def kernel(**inputs):
    maps = make_in_maps(inputs)
    if "nc" not in _NC_CACHE:
        _NC_CACHE["nc"] = build_nc()
    nc = _NC_CACHE["nc"]
    res = run_bass_kernel_spmd(nc, maps, core_ids=list(range(8)))
    return assemble(res.results)
```

```python
import contextlib
import math
import numpy as np
import concourse.bass as bass
import concourse.mybir as mybir
from concourse.bass_utils import run_bass_kernel_spmd

F32 = mybir.dt.float32
BF16 = mybir.dt.bfloat16
I32 = mybir.dt.int32
F32R = mybir.dt.float32r
AF = mybir.ActivationFunctionType
ALU = mybir.AluOpType

D = 1024
NSEG = 6
SEGL = 256
NTOK = NSEG * SEGL
NTB = NTOK // 512
TCH = 8
NCH = SEGL // TCH
NCOL = NSEG * NCH
NSLOT = NCH + 2
NG = 4
QLEN = 200
EPS = 1e-6
PREFETCH = True


class Res:
    def __init__(self, name):
        self.name = name
        self.whole_w = None
        self.whole_r = []
        self.sub = {}

    def __getitem__(self, key):
        return (self, key)


def _norm(r):
    if isinstance(r, Res):
        return (r, None)
    return r


class Op:
    __slots__ = ("eng", "fn", "deps", "dma", "signal", "idx", "ticket", "sem")

    def __init__(self, eng, fn, dma):
        self.eng = eng
        self.fn = fn
        self.deps = set()
        self.dma = dma
        self.signal = False
        self.ticket = None
        self.sem = None


class Sched:
    ENGS = ("pe", "act", "dve", "pool", "sp")

    def __init__(self, nc, ndma_sems=8):
        self.nc = nc
        self.ops = []
        self.ndma = ndma_sems
        self.last = {}
        self.pending_dma = []
        self.marks = []

    def mark(self, name):
        self.marks.append((name, sum(1 for o in self.ops if o.eng == "pe" and not o.dma)))

    def _collect(self, op, r, is_write):
        res, key = _norm(r)
        deps = op.deps
        if res.whole_w is not None:
            deps.add(res.whole_w)
        if is_write:
            deps.update(res.whole_r)
        if key is None:
            for (w, rd) in res.sub.values():
                if w is not None:
                    deps.add(w)
                if is_write:
                    deps.update(rd)
        else:
            st = res.sub.get(key)
            if st is not None:
                if st[0] is not None:
                    deps.add(st[0])
                if is_write:
                    deps.update(st[1])

    def _commit(self, op, r, is_write):
        res, key = _norm(r)
        if key is None:
            if is_write:
                res.whole_w = op.idx
                res.whole_r = []
                res.sub = {}
            else:
                res.whole_r.append(op.idx)
        else:
            st = res.sub.get(key)
            if st is None:
                st = [None, []]
                res.sub[key] = st
            if is_write:
                st[0] = op.idx
                st[1] = []
            else:
                st[1].append(op.idx)

    def op(self, eng, fn, reads=(), writes=(), dma=False):
        o = Op(eng, fn, dma)
        o.idx = len(self.ops)
        for r in reads:
            self._collect(o, r, False)
        for r in writes:
            self._collect(o, r, True)
        for r in reads:
            self._commit(o, r, False)
        for r in writes:
            self._commit(o, r, True)
        o.deps.discard(o.idx)
        self.ops.append(o)
        if dma:
            self.pending_dma.append(o.idx)
        else:
            self.last[eng] = o.idx
        return o

    def barrier(self):
        deps = set(self.last.values()) | set(self.pending_dma)
        self.pending_dma = []
        for e in self.ENGS:
            o = Op(e, (lambda eng: eng.nop()), False)
            o.idx = len(self.ops)
            o.deps = set(deps)
            self.ops.append(o)

    def emit(self):
        nc = self.nc
        ops = self.ops

        def needs(o, dop):
            return dop.dma or o.dma or dop.eng != o.eng or o.eng != "pe"

        for o in ops:
            for d in o.deps:
                if needs(o, ops[d]):
                    ops[d].signal = True
        for o in ops:
            if o.dma:
                o.signal = True
        with contextlib.ExitStack() as es:
            esem = {e: es.enter_context(nc.semaphore("s_" + e)) for e in self.ENGS}
            dsem = {}
            for e in ("sp", "act"):
                dsem[e] = [es.enter_context(nc.semaphore("d_%s_%d" % (e, i))) for i in range(self.ndma if e == "sp" else 2)]
            ecount = {e: 0 for e in self.ENGS}
            dcount = {e: [0] * len(dsem[e]) for e in dsem}
            drr = {e: 0 for e in dsem}
            for o in ops:
                if o.dma:
                    k = drr[o.eng] % len(dsem[o.eng])
                    drr[o.eng] += 1
                    dcount[o.eng][k] += 1
                    o.sem = dsem[o.eng][k]
                    o.ticket = 16 * dcount[o.eng][k]
                elif o.signal:
                    ecount[o.eng] += 1
                    o.sem = esem[o.eng]
                    o.ticket = ecount[o.eng]
            per_eng = {e: [o for o in ops if o.eng == e] for e in self.ENGS}
            block = es.enter_context(nc.Block())

            def make(e):
                def body(eng):
                    waited = {}
                    for o in per_eng[e]:
                        waits = {}
                        for d in o.deps:
                            dop = ops[d]
                            if not needs(o, dop):
                                continue
                            key = id(dop.sem)
                            if waits.get(key, (None, 0))[1] < dop.ticket:
                                waits[key] = (dop.sem, dop.ticket)
                        if o.dma and o.ticket > 16:
                            key = id(o.sem)
                            if waits.get(key, (None, 0))[1] < o.ticket - 16:
                                waits[key] = (o.sem, o.ticket - 16)
                        for key, (sem, val) in waits.items():
                            if waited.get(key, 0) >= val:
                                continue
                            eng.wait_ge(sem, val)
                            waited[key] = val
                        ins = o.fn(eng)
                        if o.signal:
                            ins.then_inc(o.sem, 16 if o.dma else 1)
                    if e == "sp":
                        for q in dsem:
                            for k in range(len(dsem[q])):
                                if dcount[q][k] > 0:
                                    eng.wait_ge(dsem[q][k], 16 * dcount[q][k])

                return body

            block.tensor(make("pe"))
            block.scalar(make("act"))
            block.vector(make("dve"))
            block.gpsimd(make("pool"))
            block.sync(make("sp"))
        return nc


class T:
    def __init__(self, t, name):
        self.t = t
        self.r = Res(name)

    def __getitem__(self, k):
        return self.t[k]


def build_nc(layers=(0, 1, 2, 3), do_final=True):
    nc = bass.Bass("TRN2", target_bir_lowering=False)
    S = Sched(nc)

    def din(name, shape):
        return nc.dram_tensor(name, list(shape), F32, kind="ExternalInput").ap()

    def dout(name, shape):
        return nc.dram_tensor(name, list(shape), F32, kind="ExternalOutput").ap()

    xT_d = din("xT", [D, NTOK])
    condT_d = din("condT", [D, NSEG])
    flag_d = din("flag", [1, 1])
    link_d = din("link", [1, 2 * NSEG])
    h0_d = din("h0", [2, 2, 128, 64])
    w_mod_d = din("w_mod", [4, D, 6 * D])
    b_modT_d = din("b_modT", [4, 128, 48])
    g_mixT_d = din("g_mixT", [4, 128, 8])
    g_ffnT_d = din("g_ffnT", [4, 128, 8])
    g_finT_d = din("g_finT", [128, 8])
    ffn_w1_d = din("ffn_w1", [4, D, 4 * D])
    ffn_w2_d = din("ffn_w2", [4, 4 * D, D])
    ssm_w_in_d = din("ssm_w_in", [2, D, D])
    ssm_w_out_d = din("ssm_w_out", [2, D, 2 * D])
    ssm_small_d = din("ssm_small", [2, 3, 128, 64])
    ssm_bc_d = din("ssm_bc", [2, 4, 128, 64, 16])
    ssm_dpp_d = din("ssm_dpp", [2, 128, 64])
    gm_w_in_d = din("gm_w_in", [D, 4 * D])
    gm_wsT_d = din("gm_wsT", [128, 16, 128])
    gm_bs_d = din("gm_bs", [1, 16 * 128])
    gm_w_out_d = din("gm_w_out", [2 * D, D])
    cv_w_in_d = din("cv_w_in", [D, 3 * D])
    cv_wT_d = din("cv_wT", [128, 3, 8])
    cv_w_out_d = din("cv_w_out", [D, D])
    consts_d = din("consts", [5, 128, 128])
    band_d = din("band", [128, 8 * 240])
    yT_d = dout("yT", [D, NTOK])
    st_d = dout("st", [2, 2, 128, 64 * NSEG])

    es = contextlib.ExitStack()
    base = (int(nc.sbuf_base) + 63) // 64 * 64
    top = 229344
    arena = es.enter_context(nc.sbuf_tensor("arena", [128, (top - base) // 4 - 64], F32))
    cur = [base]

    def alloc(name, shape, dt, off=None):
        nbytes = int(np.prod(shape[1:])) * (4 if dt in (F32, I32, F32R) else 2)
        nbytes = (nbytes + 31) // 32 * 32
        if off is None:
            off = cur[0]
            cur[0] += nbytes
        assert off + nbytes <= top - 256, (name, off, nbytes)
        t = nc.alloc_sbuf_tensor_at(name, list(shape), dt, offset=off)
        return T(t, name), off + nbytes

    def palloc(name, shape, dt):
        return alloc(name, shape, dt)[0]

    XT = palloc("XT", [128, 8, NTOK], F32)
    HT = palloc("HT", [128, 8, NTOK], BF16)
    MODSL = [palloc("MODSa", [128, 48, NSEG], F32), palloc("MODSb", [128, 48, NSEG], F32)]
    MS = [MODSL[0]]
    A1 = palloc("A1", [128, 8, NSEG], F32)
    A2 = palloc("A2", [128, 8, NSEG], F32)
    BMOD = palloc("BMOD", [128, 4, 48], F32)
    GMIX = palloc("GMIX", [128, 4, 8], F32)
    GFFN = palloc("GFFN", [128, 4, 8], F32)
    GFIN = palloc("GFIN", [128, 8], F32)
    CONDS = palloc("CONDS", [128, 8, NSEG], F32)
    CONDSB = palloc("CONDSB", [128, 8, NSEG], BF16)
    IDENT = palloc("IDENT", [128, 128], F32)
    MASKL = palloc("MASKL", [128, 128], F32)
    MASKU = palloc("MASKU", [128, 128], F32)
    SGF = palloc("SGF", [128, 128], F32)
    ONES = palloc("ONES", [128, 128], BF16)
    BAND = palloc("BAND", [128, 8, 240], BF16)
    FLAG = palloc("FLAG", [128, 1], F32)
    LINK = palloc("LINK", [128, 2 * NSEG], F32)
    EPSB = palloc("EPSB", [128, 1], F32)
    CVW = palloc("CVW", [128, 3, 8], F32)
    NL = palloc("NL", [128, NSEG], F32)
    NCVW = palloc("NCVW", [128, 3, 8], F32)
    CB = palloc("CB", [128, 8], F32)
    ST1 = palloc("ST1", [128, 32], F32)
    ARENA0 = cur[0]

    off = ARENA0
    WB, _ = alloc("WB", [128, 4, 4096], BF16, off)
    STG, off = alloc("STG", [128, 3, 2048], F32, off + 24576)
    BIG_OFF = off
    BIG, off = alloc("BIG", [128, 16, NTOK], BF16, off)
    RSTD_OFF = off
    RSTD, off = alloc("RSTD", [128, NTOK], F32, off)
    XN, off = alloc("XN", [128, NTOK], F32, off)
    TMP_OFF = off
    TMP, off = alloc("TMP", [128, 3, 512], F32, off)
    GEN_END = off
    WBr = [Res("WB%d" % i) for i in range(4)]
    STGr = [Res("STG%d" % i) for i in range(6)]
    stg_slots = [0, 1, 2]
    STG2 = nc.alloc_sbuf_tensor_at("STG2", [128, 3, 2048], F32, offset=BIG_OFF + 8 * NTOK * 2)

    def stg_ap(s, n):
        return STG[:, s, 0:n] if s < 3 else STG2[:, s - 3, 0:n]

    def next_stg():
        s = stg_slots[stg_rr[0] % len(stg_slots)]
        stg_rr[0] += 1
        return s
    TMPr = [Res("TMP%d" % i) for i in range(3)]

    PS = [T(es.enter_context(nc.psum_tensor("ps%d" % i, [128, 512], F32)), "ps%d" % i) for i in range(8)]
    bank_rr = [0]

    def next_bank():
        b = PS[bank_rr[0] % 8]
        bank_rr[0] += 1
        return b

    dmaq_rr = [0]

    def dma(out_ap, in_ap, reads=(), writes=(), q=None):
        if q is None:
            q = "sp"
        S.op(q, lambda e: e.dma_start(out=out_ap, in_=in_ap), reads=reads, writes=writes, dma=True)

    dma(IDENT[:], consts_d[0], writes=[IDENT.r])
    dma(MASKL[:], consts_d[1], writes=[MASKL.r])
    dma(MASKU[:], consts_d[2], writes=[MASKU.r])
    dma(SGF[:], consts_d[3], writes=[SGF.r])
    dma(STG[:, 2, 0:1920], band_d, writes=[STGr[2]])
    S.op("dve", lambda e: e.tensor_copy(out=BAND[:].rearrange("p a b -> p (a b)"), in_=STG[:, 2, 0:1920]), reads=[STGr[2]], writes=[BAND.r])
    S.op("dve", lambda e: e.memset(ONES[:], 1.0 / D), writes=[ONES.r])
    S.op("dve", lambda e: e.memset(EPSB[:], EPS), writes=[EPSB.r])
    dma(FLAG[:], flag_d.partition_broadcast(128), writes=[FLAG.r])
    dma(LINK[:], link_d.partition_broadcast(128), writes=[LINK.r])
    dma(BMOD[:], b_modT_d.rearrange("i p o -> p i o"), writes=[BMOD.r])
    dma(GMIX[:], g_mixT_d.rearrange("i p o -> p i o"), writes=[GMIX.r])
    dma(GFFN[:], g_ffnT_d.rearrange("i p o -> p i o"), writes=[GFFN.r])
    dma(GFIN[:], g_finT_d, writes=[GFIN.r])
    dma(CVW[:], cv_wT_d, writes=[CVW.r])
    dma(CONDS[:], condT_d.rearrange("(kt p) s -> p kt s", p=128), writes=[CONDS.r])
    xT_v = xT_d.rearrange("(kt p) n -> p kt n", p=128)
    for kt in range(8):
        dma(XT[:, kt, :], xT_v[:, kt, :], writes=[XT.r[kt]])
    S.op("act", lambda e: e.activation(out=CONDSB[:], in_=CONDS[:], func=AF.Silu), reads=[CONDS.r], writes=[CONDSB.r])

    rcI = nc.alloc_sbuf_tensor_at("rcI", [128, 2, 1024], I32, offset=BIG_OFF + 20480)
    rcF = nc.alloc_sbuf_tensor_at("rcF", [128, 2, 1024], F32, offset=BIG_OFF + 20480 + 8192)
    angF = nc.alloc_sbuf_tensor_at("angF", [128, 1024], F32, offset=BIG_OFF + 20480 + 16384)
    angI = nc.alloc_sbuf_tensor_at("angI", [128, 1024], I32, offset=BIG_OFF + 20480 + 20480)
    angK = nc.alloc_sbuf_tensor_at("angK", [128, 1024], F32, offset=BIG_OFF + 20480 + 24576)
    PEr = Res("pe_scratch")
    S.op("pool", lambda e: e.iota(rcI[:, 0, :], pattern=[[1, 16], [0, 64]], base=0, channel_multiplier=0), writes=[PEr])
    S.op("pool", lambda e: e.iota(rcI[:, 1, :], pattern=[[0, 16], [1, 64]], base=0, channel_multiplier=0), writes=[PEr])
    S.op("dve", lambda e: e.tensor_copy(out=rcF[:], in_=rcI[:]), reads=[PEr], writes=[PEr])
    TWO_PI = 2.0 * math.pi
    for tile in range(8):
        which = tile // 4
        is_cos = (tile // 2) % 2
        fcol = 1 + (tile % 2)
        shift = 0.75 if is_cos else 0.5
        S.op("dve", lambda e, which=which, fcol=fcol, shift=shift: e.tensor_scalar(
            out=angF[:], in0=rcF[:, which, :], scalar1=SGF[:, fcol:fcol + 1], scalar2=shift, op0=ALU.mult, op1=ALU.add),
            reads=[PEr, SGF.r], writes=[PEr])
        S.op("dve", lambda e: e.tensor_copy(out=angI[:], in_=angF[:]), reads=[PEr], writes=[PEr])
        S.op("dve", lambda e: e.tensor_copy(out=angK[:], in_=angI[:]), reads=[PEr], writes=[PEr])
        S.op("dve", lambda e: e.tensor_tensor(out=angF[:], in0=angF[:], in1=angK[:], op=ALU.subtract), reads=[PEr], writes=[PEr])
        S.op("dve", lambda e: e.tensor_scalar(out=angK[:], in0=angF[:], scalar1=0.0, scalar2=None, op0=ALU.is_lt), reads=[PEr], writes=[PEr])
        S.op("dve", lambda e: e.tensor_tensor(out=angF[:], in0=angF[:], in1=angK[:], op=ALU.add), reads=[PEr], writes=[PEr])
        S.op("dve", lambda e: e.tensor_scalar(out=angF[:], in0=angF[:], scalar1=TWO_PI, scalar2=-math.pi, op0=ALU.mult, op1=ALU.add), reads=[PEr], writes=[PEr])
        S.op("dve", lambda e: e.tensor_scalar(out=angF[:], in0=angF[:], scalar1=-math.pi, scalar2=math.pi, op0=ALU.max, op1=ALU.min), reads=[PEr], writes=[PEr])
        S.op("act", lambda e: e.activation(out=angK[:], in_=angF[:], func=AF.Sin), reads=[PEr], writes=[PEr])
        S.op("dve", lambda e, tile=tile: e.scalar_tensor_tensor(
            out=XT[:, tile, 0:1024], in0=angK[:], scalar=FLAG[:, 0:1], in1=XT[:, tile, 0:1024], op0=ALU.mult, op1=ALU.add),
            reads=[PEr, FLAG.r, XT.r[tile]], writes=[XT.r[tile]])

    stg_rr = [0]
    wb_rr = [0]
    cast_rr = [0]

    def load_chunk(src_aps, slot=None):
        if slot is None:
            slot = wb_rr[0] % 2
            wb_rr[0] += 1
        for i, sap in enumerate(src_aps):
            a, b = sap.shape[1], sap.shape[2]
            n = a * b
            assert n <= 2048
            s = next_stg()
            dma(STG[:, s, 0:n].rearrange("p (a b) -> p a b", a=a), sap, writes=[STGr[s]])
            ce = "act"
            cast_rr[0] += 1
            if ce == "pool":
                S.op("pool", lambda e, s=s, slot=slot, i=i, n=n: e.tensor_copy(out=WB[:, slot, i * 2048:i * 2048 + n], in_=STG[:, s, 0:n]),
                     reads=[STGr[s]], writes=[WBr[slot]])
            else:
                S.op("act", lambda e, s=s, slot=slot, i=i, n=n: e.activation(out=WB[:, slot, i * 2048:i * 2048 + n], in_=STG[:, s, 0:n], func=AF.Copy),
                     reads=[STGr[s]], writes=[WBr[slot]])
        return slot

    def wview(w2d, kt0, nkt, c0, ncols):
        return w2d.rearrange("(kt p) n -> p kt n", p=128)[:, kt0:kt0 + nkt, c0:c0 + ncols]

    def proj_chunk(slot, nkt, n_ot, src, src_reads, consumer, kt_layout_cols):
        wv = WB[:, slot, 0:nkt * kt_layout_cols].rearrange("p (k c) -> p k c", k=nkt)
        for ot in range(n_ot):
            banks = [next_bank() for _ in range(NTB)]
            for kt in range(nkt):
                for tb in range(NTB):
                    S.op("pe", lambda e, ot=ot, kt=kt, tb=tb, bk=banks[tb]: e.matmul(
                        bk[:], lhsT=wv[:, kt, ot * 128:(ot + 1) * 128], rhs=src(kt, tb), start=(kt == 0), stop=(kt == nkt - 1)),
                        reads=[WBr[slot]] + list(src_reads(kt)), writes=[banks[tb].r])
            for tb in range(NTB):
                consumer(ot, tb, banks[tb])

    loaded = {}

    def next_wb():
        for _ in range(4):
            slot = wb_rr[0] % 3
            wb_rr[0] += 1
            if slot not in loaded.values():
                return slot
        raise AssertionError("no free WB slot")

    def load_chunk(key, w2d, K, cols):
        if key in loaded:
            return
        nkt = K // 128
        ncol = 128 * len(cols)
        assert nkt * ncol <= 4096
        per_kt = ncol
        kpp = 1
        for k in range(1, nkt + 1):
            if nkt % k == 0 and k * per_kt <= 2048:
                kpp = k
        slot = next_wb()
        npieces = nkt // kpp
        runs = []
        for j, c0 in enumerate(cols):
            if runs and runs[-1][1] + runs[-1][2] == c0:
                runs[-1][2] += 128
            else:
                runs.append([j, c0, 128])
        for p in range(npieces):
            s = next_stg()
            n = kpp * per_kt
            sap = stg_ap(s, n)
            stv = sap.rearrange("p (k c) -> p k c", k=kpp)
            for (j, c0, wdt) in runs:
                dma(stv[:, :, j * 128:j * 128 + wdt], wview(w2d, p * kpp, kpp, c0, wdt), writes=[STGr[s]])
            ce = "act"
            cast_rr[0] += 1
            dst = WB[:, slot, p * n:(p + 1) * n]
            if ce == "pool":
                S.op("pool", lambda e, sap=sap, dst=dst: e.tensor_copy(out=dst, in_=sap), reads=[STGr[s]], writes=[WBr[slot]])
            else:
                S.op("act", lambda e, sap=sap, dst=dst: e.activation(out=dst, in_=sap, func=AF.Copy), reads=[STGr[s]], writes=[WBr[slot]])
        loaded[key] = slot

    def proj(w2d, K, cols_list, src, src_reads, consumer, name=None, hint=None, after=None):
        nkt = K // 128
        if name is None:
            name = ("anon", len(S.ops))
        for ci, cols in enumerate(cols_list):
            load_chunk((name, ci), w2d, K, cols)
            if PREFETCH:
                if ci + 1 < len(cols_list):
                    load_chunk((name, ci + 1), w2d, K, cols_list[ci + 1])
                elif hint is not None:
                    hint()
            slot = loaded.pop((name, ci))
            proj_chunk(slot, nkt, len(cols), src, src_reads, lambda ot, tb, bk, ci=ci: consumer(ci, ot, tb, bk), 128 * len(cols))
            if after is not None:
                after(ci)

    def adaln_load(i, cb):
        load_chunk(("ada", i, cb), w_mod_d[i], D, [cb * 512 + o * 128 for o in range(4)])

    XNr2 = [Res("XNa0"), Res("XNa1")]

    def adaln_mm(i, cb, Mdst):
        key = ("ada", i, cb)
        load_chunk(key, w_mod_d[i], D, [cb * 512 + o * 128 for o in range(4)])
        slot = loaded.pop(key)
        wv = WB[:, slot, 0:4096].rearrange("p (k c) -> p k c", k=8)
        bk = next_bank()
        for kt in range(8):
            S.op("pe", lambda e, kt=kt: e.matmul(bk[0:NSEG, :], lhsT=CONDSB[:, kt, :], rhs=wv[:, kt, :], start=(kt == 0), stop=(kt == 7)),
                 reads=[WBr[slot], CONDSB.r], writes=[bk.r])
        x0 = (cb % 2) * 512
        S.op("act", lambda e: e.activation(out=XN[0:NSEG, x0:x0 + 512], in_=bk[0:NSEG, :], func=AF.Copy), reads=[bk.r], writes=[XN.r, XNr2[cb % 2]])

    def adaln_tr(i, cb, Mdst):
        x0 = (cb % 2) * 512
        bkT = next_bank()
        for c in range(4):
            S.op("pe", lambda e, c=c: e.transpose(bkT[:, c * 8:c * 8 + NSEG], XN[0:NSEG, x0 + c * 128:x0 + (c + 1) * 128], IDENT[0:NSEG, 0:NSEG]),
                 reads=[XNr2[cb % 2], IDENT.r], writes=[bkT.r])
        bb = BMOD[:, i, 4 * cb:4 * cb + 4].unsqueeze(2).to_broadcast([128, 4, NSEG])
        S.op("dve", lambda e: e.tensor_tensor(out=Mdst[:, 4 * cb:4 * cb + 4, :], in0=bkT[:, 0:32].rearrange("p (c s) -> p c s", c=4)[:, :, 0:NSEG], in1=bb, op=ALU.add),
             reads=[bkT.r, BMOD.r], writes=[Mdst.r[cb]])

    def adaln_block(i, cb, Mdst):
        adaln_mm(i, cb, Mdst)
        adaln_tr(i, cb, Mdst)

    def merge_cols(cols):
        return cols

    def norm_mod(Atile, Bsel):
        for dt in range(8):
            S.op("act", lambda e, dt=dt: e.activation(out=HT[:, dt, :], in_=XT[:, dt, :], func=AF.Square), reads=[XT.r[dt]], writes=[HT.r[dt]])
        for tb in range(NTB):
            bk = next_bank()
            for dt in range(8):
                S.op("pe", lambda e, dt=dt, tb=tb, bk=bk: e.matmul(bk[:], lhsT=ONES[:], rhs=HT[:, dt, tb * 512:(tb + 1) * 512], start=(dt == 0), stop=(dt == 7)),
                     reads=[ONES.r, HT.r[dt]], writes=[bk.r])
            S.op("act", lambda e, tb=tb, bk=bk: e.activation(out=RSTD[:, tb * 512:(tb + 1) * 512], in_=bk[:], func=AF.Sqrt, bias=EPSB[:, 0:1], scale=1.0),
                 reads=[bk.r, EPSB.r], writes=[RSTD.r[tb]])
            S.op("dve", lambda e, tb=tb: e.reciprocal(out=RSTD[:, tb * 512:(tb + 1) * 512], in_=RSTD[:, tb * 512:(tb + 1) * 512]),
                 reads=[RSTD.r[tb]], writes=[RSTD.r[tb]])
        TMPf = TMP[:].rearrange("p a b -> p (a b)")
        for dt in range(8):
            if dt % 2 == 0:
                xb, xr = XN[:], [XN.r]
            else:
                xb, xr = TMPf, list(TMPr)
            S.op("dve", lambda e, dt=dt, xb=xb: e.tensor_tensor(out=xb, in0=XT[:, dt, :], in1=RSTD[:], op=ALU.mult),
                 reads=[XT.r[dt], RSTD.r], writes=xr)
            for sg in range(NSEG):
                if Bsel is not None:
                    bap = Bsel(dt)[:, sg:sg + 1]
                    S.op("act", lambda e, dt=dt, sg=sg, bap=bap, xb=xb: e.activation(out=HT[:, dt, sg * SEGL:(sg + 1) * SEGL], in_=xb[:, sg * SEGL:(sg + 1) * SEGL],
                                                                      func=AF.Identity, scale=Atile[:, dt, sg:sg + 1], bias=bap),
                         reads=xr + [Atile.r, MS[0].r], writes=[HT.r[dt]])

    def x_update(gate_ot0):
        def cons(dt, tb, bk):
            Mt = MS[0]
            for h in range(2):
                sg = tb * 2 + h
                S.op("dve", lambda e, dt=dt, sg=sg, h=h, bk=bk, Mt=Mt: e.scalar_tensor_tensor(
                    out=XT[:, dt, sg * SEGL:(sg + 1) * SEGL], in0=bk[:, h * SEGL:(h + 1) * SEGL], scalar=Mt[:, gate_ot0 + dt, sg:sg + 1],
                    in1=XT[:, dt, sg * SEGL:(sg + 1) * SEGL], op0=ALU.mult, op1=ALU.add),
                    reads=[bk.r, Mt.r, XT.r[dt]], writes=[XT.r[dt]])
        return cons

    def ht_src(kt, tb):
        return HT[:, kt, tb * 512:(tb + 1) * 512]

    def ht_reads(kt):
        return [HT.r[kt]]

    BIGr = BIG.r

    def big_src(kt, tb):
        return BIG[:, kt, tb * 512:(tb + 1) * 512]

    def big_reads(kt):
        return [BIGr[kt]]

    S.op("dve", lambda e: e.tensor_scalar(out=NL[:], in0=LINK[:, 0:NSEG], scalar1=-1.0, scalar2=1.0, op0=ALU.mult, op1=ALU.add), reads=[LINK.r], writes=[NL.r])
    S.op("dve", lambda e: e.tensor_scalar(out=NCVW[:], in0=CVW[:], scalar1=-1.0, scalar2=None, op0=ALU.mult), reads=[CVW.r], writes=[NCVW.r])

    def conv_layer(i):
        def cons(ci, ot, tb, bk):
            dt = ci
            sl = slice(tb * 512, (tb + 1) * 512)
            if ot == 0:
                S.op("act", lambda e: e.activation(out=BIG[:, dt, sl], in_=bk[:], func=AF.Copy), reads=[bk.r], writes=[BIGr[dt]])
            elif ot == 1:
                S.op("act", lambda e: e.activation(out=TMP[:, tb, :], in_=bk[:], func=AF.Copy), reads=[bk.r], writes=[TMPr[tb]])
            else:
                S.op("dve", lambda e: e.tensor_tensor(out=BIG[:, 8 + dt, sl], in0=bk[:], in1=TMP[:, tb, :], op=ALU.mult),
                     reads=[bk.r, TMPr[tb]], writes=[BIGr[8 + dt]])
        def conv_dt(dt):
            V = BIG[:, 8 + dt, :]
            rd = [BIGr[8 + dt], CVW.r, NCVW.r, NL.r]
            S.op("dve", lambda e, V=V, dt=dt: e.tensor_scalar(out=XN[:], in0=V, scalar1=CVW[:, 1, dt:dt + 1], scalar2=None, op0=ALU.mult), reads=rd, writes=[XN.r])
            S.op("dve", lambda e, V=V, dt=dt: e.scalar_tensor_tensor(out=XN[:, 1:NTOK], in0=V[:, 0:NTOK - 1], scalar=CVW[:, 0, dt:dt + 1], in1=XN[:, 1:NTOK], op0=ALU.mult, op1=ALU.add),
                 reads=rd + [XN.r], writes=[XN.r])
            S.op("dve", lambda e, V=V, dt=dt: e.scalar_tensor_tensor(out=XN[:, 0:NTOK - 1], in0=V[:, 1:NTOK], scalar=CVW[:, 2, dt:dt + 1], in1=XN[:, 0:NTOK - 1], op0=ALU.mult, op1=ALU.add),
                 reads=rd + [XN.r], writes=[XN.r])
            S.op("dve", lambda e, V=V: e.tensor_tensor(out=CB[:, 0:5], in0=V[:, SEGL - 1:NTOK - 1:SEGL], in1=NL[:, 1:NSEG], op=ALU.mult), reads=rd, writes=[CB.r])
            S.op("dve", lambda e, dt=dt: e.scalar_tensor_tensor(out=XN[:, SEGL:NTOK:SEGL], in0=CB[:, 0:5], scalar=NCVW[:, 0, dt:dt + 1], in1=XN[:, SEGL:NTOK:SEGL], op0=ALU.mult, op1=ALU.add),
                 reads=rd + [CB.r, XN.r], writes=[XN.r])
            S.op("dve", lambda e, V=V: e.tensor_tensor(out=CB[:, 0:5], in0=V[:, SEGL:NTOK:SEGL], in1=NL[:, 1:NSEG], op=ALU.mult), reads=rd + [CB.r], writes=[CB.r])
            S.op("dve", lambda e, dt=dt: e.scalar_tensor_tensor(out=XN[:, SEGL - 1:NTOK - 1:SEGL], in0=CB[:, 0:5], scalar=NCVW[:, 2, dt:dt + 1], in1=XN[:, SEGL - 1:NTOK - 1:SEGL], op0=ALU.mult, op1=ALU.add),
                 reads=rd + [CB.r, XN.r], writes=[XN.r])
            S.op("dve", lambda e, V=V, dt=dt: e.tensor_tensor(out=V, in0=BIG[:, dt, :], in1=XN[:], op=ALU.mult), reads=[BIGr[dt], XN.r], writes=[BIGr[8 + dt]])

        proj(cv_w_in_d, D, [[dt * 128, D + dt * 128, 2 * D + dt * 128] for dt in range(8)], ht_src, ht_reads, cons, after=conv_dt, name=("mix", i))
        upd = x_update(16)
        proj(cv_w_out_d, D, [[c * 512 + o * 128 for o in range(4)] for c in range(2)],
             lambda kt, tb: BIG[:, 8 + kt, tb * 512:(tb + 1) * 512], lambda kt: [BIGr[8 + kt]],
             lambda ci, ot, tb, bk: upd(ci * 4 + ot, tb, bk), hint=lambda: load_chunk((("f1", i, 0), 0), ffn_w1_d[i], D, [o * 128 for o in range(4)]))

    def gmlp_layer(i):
        def consu(ci, ot, tb, bk):
            S.op("act", lambda e: e.activation(out=BIG[:, ci * 4 + ot, tb * 512:(tb + 1) * 512], in_=bk[:], func=AF.Gelu_apprx_tanh),
                 reads=[bk.r], writes=[BIGr[ci * 4 + ot]])
        proj(gm_w_in_d, D, [[c * 512 + o * 128 for o in range(4)] for c in range(4)], ht_src, ht_reads, consu, name=("mix", i))
        VTs = [nc.alloc_sbuf_tensor_at("VT%d" % k, [128, 2048], BF16, offset=RSTD_OFF + 4096 * k) for k in range(2)]
        VNs = [nc.alloc_sbuf_tensor_at("VN0", [128, 2048], BF16, offset=RSTD_OFF + 8192),
               nc.alloc_sbuf_tensor_at("VN1", [128, 2048], BF16, offset=GEN_END)]
        VTrs, VNrs = [Res("VT0"), Res("VT1")], [Res("VN0"), Res("VN1")]
        WST = nc.alloc_sbuf_tensor_at("WST", [128, 16, 128], BF16, offset=TMP_OFF + 2048)
        WSTr = Res("WST")
        S.barrier()
        stg_slots[:] = [1, 2]
        for sl4 in range(4):
            for pc in range(2):
                s = next_stg()
                stv = STG[:, s, :].rearrange("p (k c) -> p k c", k=4)
                dma(stv, wview(gm_w_in_d, pc * 4, 4, 2 * D + sl4 * 512, 512), writes=[STGr[s]])
                S.op("act", lambda e, s=s, sl4=sl4, pc=pc: e.activation(out=WB[:, sl4, pc * 2048:(pc + 1) * 2048], in_=STG[:, s, :], func=AF.Copy), reads=[STGr[s]], writes=[WBr[sl4]])
        dma(STG[:, 1, :].rearrange("p (g q) -> p g q", g=16), gm_wsT_d, writes=[STGr[1]])
        S.op("pool", lambda e: e.tensor_copy(out=WST[:].rearrange("p g q -> p (g q)"), in_=STG[:, 1, :]), reads=[STGr[1]], writes=[WSTr])
        dma(STG[:, 2, :], gm_bs_d.partition_broadcast(128), writes=[STGr[2]])
        VB = {}

        def vmm_pe(tt):
            tsl = slice(tt * 128, (tt + 1) * 128)
            bks = []
            for cb in range(4):
                bk = next_bank()
                bks.append(bk)
                for kt in range(8):
                    S.op("pe", lambda e, kt=kt, cb=cb, bk=bk, tsl=tsl: e.matmul(bk[:], lhsT=HT[:, kt, tsl], rhs=WB[:, cb, kt * 512:(kt + 1) * 512], start=(kt == 0), stop=(kt == 7)),
                         reads=[HT.r[kt], WBr[cb]], writes=[bk.r])
            VB[tt] = bks

        def vmm_evac(tt):
            VT, VTr = VTs[tt % 2], VTrs[tt % 2]
            c0 = 16 * (tt % 2)
            for cb, bk in enumerate(VB.pop(tt)):
                S.op("act", lambda e, cb=cb, bk=bk: e.activation(out=VT[:, cb * 512:(cb + 1) * 512], in_=bk[:], func=AF.Gelu_apprx_tanh, accum_out=ST1[:, c0 + cb:c0 + cb + 1]),
                     reads=[bk.r], writes=[VTr, ST1.r[tt % 2]])

        def chain(tt):
            VT, VN, VTr, VNr = VTs[tt % 2], VNs[tt % 2], VTrs[tt % 2], VNrs[tt % 2]
            c0 = 16 * (tt % 2)
            SR = ST1.r[tt % 2]
            cs = lambda a_, b_: ST1[:, c0 + a_:c0 + b_]
            S.op("act", lambda e: e.activation(out=VN[:], in_=VT[:], func=AF.Square, accum_out=cs(4, 5)), reads=[VTr], writes=[VNr, SR])
            S.op("dve", lambda e: e.tensor_tensor(out=cs(5, 7), in0=cs(0, 2), in1=cs(2, 4), op=ALU.add), reads=[SR], writes=[SR])
            S.op("dve", lambda e: e.tensor_tensor(out=cs(7, 8), in0=cs(5, 6), in1=cs(6, 7), op=ALU.add), reads=[SR], writes=[SR])
            S.op("dve", lambda e: e.tensor_scalar(out=cs(8, 9), in0=cs(7, 8), scalar1=1.0 / 2048, scalar2=None, op0=ALU.mult), reads=[SR], writes=[SR])
            S.op("dve", lambda e: e.tensor_tensor(out=cs(9, 10), in0=cs(8, 9), in1=cs(8, 9), op=ALU.mult), reads=[SR], writes=[SR])
            S.op("dve", lambda e: e.scalar_tensor_tensor(out=cs(10, 11), in0=cs(4, 5), scalar=1.0 / 2048, in1=cs(9, 10), op0=ALU.mult, op1=ALU.subtract), reads=[SR], writes=[SR])
            S.op("act", lambda e: e.activation(out=cs(11, 12), in_=cs(10, 11), func=AF.Sqrt, bias=EPSB[:, 0:1], scale=1.0), reads=[SR, EPSB.r], writes=[SR])
            S.op("dve", lambda e: e.reciprocal(out=cs(12, 13), in_=cs(11, 12)), reads=[SR], writes=[SR])
            S.op("dve", lambda e: e.tensor_scalar(out=VN[:], in0=VT[:], scalar1=cs(8, 9), scalar2=cs(12, 13), op0=ALU.subtract, op1=ALU.mult),
                 reads=[VTr, SR], writes=[VNr])

        def smm(tt):
            tsl = slice(tt * 128, (tt + 1) * 128)
            VN, VNr = VNs[tt % 2], VNrs[tt % 2]
            for b4 in range(4):
                bk = next_bank()
                for g4 in range(4):
                    g = b4 * 4 + g4
                    S.op("pe", lambda e, g=g, g4=g4, bk=bk: e.matmul(bk[:, g4 * 128:(g4 + 1) * 128], lhsT=VN[:, g * 128:(g + 1) * 128], rhs=WST[:, g, :], start=True, stop=True),
                         reads=[VNr, WSTr], writes=[bk.r])
                S.op("dve", lambda e, b4=b4, bk=bk: e.tensor_tensor(out=TMP[:, 0, :], in0=bk[:], in1=STG[:, 2, b4 * 512:(b4 + 1) * 512], op=ALU.add),
                     reads=[bk.r, STGr[2]], writes=[TMPr[0]])
                S.op("dve", lambda e, b4=b4: e.tensor_tensor(out=BIG[:, b4 * 4:(b4 + 1) * 4, tsl], in0=TMP[:, 0, :].rearrange("p (g q) -> p g q", g=4),
                                                      in1=BIG[:, b4 * 4:(b4 + 1) * 4, tsl], op=ALU.mult),
                     reads=[TMPr[0]] + [BIGr[b4 * 4 + k] for k in range(4)], writes=[BIGr[b4 * 4 + k] for k in range(4)])
        vmm_pe(0)
        vmm_evac(0)
        for tt in range(12):
            if tt + 1 < 12:
                vmm_pe(tt + 1)
            chain(tt)
            if tt + 1 < 12:
                vmm_evac(tt + 1)
            smm(tt)
        S.barrier()
        stg_slots[:] = [0, 1, 2]
        upd = x_update(16)
        proj(gm_w_out_d, 2 * D, [[c * 256, c * 256 + 128] for c in range(4)], big_src, big_reads,
             lambda ci, ot, tb, bk: upd(ci * 2 + ot, tb, bk), hint=lambda: load_chunk((("f1", i, 0), 0), ffn_w1_d[i], D, [o * 128 for o in range(4)]))

    def ssm_layer(i, j):
        def consu(ci, ot, tb, bk):
            S.op("act", lambda e: e.activation(out=BIG[:, ci * 4 + ot, tb * 512:(tb + 1) * 512], in_=bk[:], func=AF.Copy),
                 reads=[bk.r], writes=[BIGr[ci * 4 + ot]])
        proj(ssm_w_in_d[j], D, [[c * 512 + o * 128 for o in range(4)] for c in range(2)], ht_src, ht_reads, consu, name=("mix", i))
        oa = [ARENA0]
        ob = [BIG_OFF + 8 * NTOK * 2]

        def sa(name, shape, dt, reg=oa):
            nb = int(np.prod(shape[1:])) * (4 if dt in (F32, I32) else 2)
            nb = (nb + 31) // 32 * 32
            t = nc.alloc_sbuf_tensor_at("%s_%d" % (name, i), list(shape), dt, offset=reg[0])
            reg[0] += nb
            return t

        N4 = NG * NSEG * NSLOT
        Xre, Xim, Gre, Gim, T1, T2 = [sa(n, [128, NG, NSEG, NSLOT], F32) for n in ("Xre", "Xim", "Gre", "Gim", "T1", "T2")]
        PTAB = sa("PTAB", [128, NG, 2, QLEN], F32)
        QTAB = sa("QTAB", [128, NG, 2, QLEN], F32)
        Pre, Pim, Qre, Qim = PTAB[:, :, 0, :], PTAB[:, :, 1, :], QTAB[:, :, 0, :], QTAB[:, :, 1, :]
        COEF = sa("COEF", [128, NG, NSEG, NSLOT], F32)
        Bsre, Bsim = [sa(n, [128, NG, 128], F32) for n in ("Bsre", "Bsim")]
        BCT = sa("BCT", [128, 4, NG, 16], F32)
        bfn = ("Bbre", "Bbim", "Csre", "Csni", "Cfre", "Cfni", "Cbre", "Cbni", "W1fr", "W1fi", "W1br", "W1bi", "W2")
        Bbre, Bbim, Csre, Csni, Cfre, Cfni, Cbre, Cbni, W1fr, W1fi, W1br, W1bi, W2 = [sa(n, [128, NG, 128], BF16, ob) for n in bfn]
        Ub = [sa("U%d" % k, [128, NG, NSEG, NSLOT], BF16) for k in range(2)]
        Hre = [sa("Hre%d" % k, [128, NG, NSEG, NSLOT], BF16, ob) for k in range(2)]
        Him = [sa("Him%d" % k, [128, NG, NSEG, NSLOT], BF16, ob) for k in range(2)]
        Ysb = sa("Ysb", [128, 8, NCOL], BF16)
        assert oa[0] <= BIG_OFF, (oa[0], BIG_OFF)
        pwBr, pwBi, pwCr, pwCi, wpr = [sa(n, [128, 64, 8], F32, ob) for n in ("pwBr", "pwBi", "pwCr", "pwCi", "wpr")]
        WIP = sa("WIP", [128, 64, 8, 2], F32, ob)
        nwpi, wpi = WIP[:, :, :, 0], WIP[:, :, :, 1]
        g64 = {}
        for n in ("LR", "LI", "DTt", "ANG", "LRDT", "Cc", "Sn", "EP", "EN", "SSg", "nur", "nui", "mur", "mui", "abr", "abi", "fr", "fi",
                  "ta", "tb", "tc", "mu8r", "mu8i", "k1r", "k1i", "k3r", "k3i", "rho8", "spr", "spi", "stPr", "stPi", "stQr", "stQi", "h0r", "h0i", "dpp"):
            g64[n] = sa(n, [128, 64], F32, ob)
        angI = sa("angI", [128, 64], I32, ob)
        Er, Ei = [sa(n, [128, 64, NSEG], F32, ob) for n in ("Er", "Ei")]
        STOr, STOi = [sa(n, [128, 64, NSEG], F32, ob) for n in ("STOr", "STOi")]
        NSG = sa("NSG", [128, 1], F32, ob)
        RG = Res("ssmgen%d" % i)
        RW = Res("ssmw%d" % i)
        RU = [Res("ssmU%d_%d" % (i, k)) for k in range(2)]
        RH = [Res("ssmH%d_%d" % (i, k)) for k in range(2)]
        RY = Res("ssmY%d" % i)
        G = g64

        def dv(fn, reads=(), writes=()):
            S.op("dve", fn, reads=[RG] + list(reads), writes=[RG] + list(writes))

        def tt(out, a, b, op, **kw):
            dv(lambda e: e.tensor_tensor(out=out, in0=a, in1=b, op=op), **kw)

        def cmul(o_re, o_im, a_re, a_im, b_re, b_im, t1, t2, **kw):
            tt(t1, a_re, b_re, ALU.mult, **kw)
            tt(t2, a_im, b_im, ALU.mult, **kw)
            dv(lambda e: e.tensor_tensor(out=t2, in0=t1, in1=t2, op=ALU.subtract), **kw)
            tt(t1, a_re, b_im, ALU.mult, **kw)
            dv(lambda e: e.tensor_tensor(out=o_im, in0=a_im, in1=b_re, op=ALU.mult), **kw)
            dv(lambda e: e.tensor_tensor(out=o_im, in0=o_im, in1=t1, op=ALU.add), **kw)
            dv(lambda e: e.tensor_copy(out=o_re, in_=t2), **kw)

        sm = ssm_small_d[j]
        nd = [HT.r, XN.r, RSTD.r] + list(TMPr)
        dma(G["LR"][:], sm[0], reads=nd, writes=[RG]); dma(G["LI"][:], sm[1], writes=[RG]); dma(G["DTt"][:], sm[2], writes=[RG])
        dma(G["h0r"][:], h0_d[j, 0], writes=[RG]); dma(G["h0i"][:], h0_d[j, 1], writes=[RG]); dma(G["dpp"][:], ssm_dpp_d[j], writes=[RG])
        dv(lambda e: e.tensor_scalar(out=NSG[:], in0=SGF[:, 0:1], scalar1=-1.0, scalar2=None, op0=ALU.mult), reads=[SGF.r])
        S.op("act", lambda e: e.activation(out=G["DTt"][:], in_=G["DTt"][:], func=AF.Exp), reads=[RG], writes=[RG])
        tt(G["ANG"][:], G["LI"][:], G["DTt"][:], ALU.mult)
        tt(G["LRDT"][:], G["LR"][:], G["DTt"][:], ALU.mult)
        TWO_PI_ = 2.0 * math.pi

        def sinlike(out, shift):
            dv(lambda e: e.tensor_scalar(out=G["ta"][:], in0=G["ANG"][:], scalar1=1.0 / TWO_PI_, scalar2=shift, op0=ALU.mult, op1=ALU.add))
            dv(lambda e: e.tensor_copy(out=angI[:], in_=G["ta"][:]))
            dv(lambda e: e.tensor_copy(out=G["tb"][:], in_=angI[:]))
            tt(G["ta"][:], G["ta"][:], G["tb"][:], ALU.subtract)
            dv(lambda e: e.tensor_scalar(out=G["tb"][:], in0=G["ta"][:], scalar1=0.0, scalar2=None, op0=ALU.is_lt))
            tt(G["ta"][:], G["ta"][:], G["tb"][:], ALU.add)
            dv(lambda e: e.tensor_scalar(out=G["ta"][:], in0=G["ta"][:], scalar1=TWO_PI_, scalar2=-math.pi, op0=ALU.mult, op1=ALU.add))
            dv(lambda e: e.tensor_scalar(out=G["ta"][:], in0=G["ta"][:], scalar1=-math.pi, scalar2=math.pi, op0=ALU.max, op1=ALU.min))
            S.op("act", lambda e: e.activation(out=out, in_=G["ta"][:], func=AF.Sin), reads=[RG], writes=[RG])
        sinlike(G["Sn"][:], 0.5)
        sinlike(G["Cc"][:], 0.75)
        S.op("act", lambda e: e.activation(out=G["EP"][:], in_=G["LRDT"][:], func=AF.Exp, scale=SGF[:, 0:1]), reads=[RG, SGF.r], writes=[RG])
        S.op("act", lambda e: e.activation(out=G["EN"][:], in_=G["LRDT"][:], func=AF.Exp, scale=NSG[:, 0:1]), reads=[RG], writes=[RG])
        S.op("act", lambda e: e.activation(out=G["ta"][:], in_=G["LRDT"][:], func=AF.Exp), reads=[RG], writes=[RG])
        S.op("act", lambda e: e.activation(out=G["rho8"][:], in_=G["LRDT"][:], func=AF.Exp, scale=8.0), reads=[RG], writes=[RG])
        tt(G["abr"][:], G["ta"][:], G["Cc"][:], ALU.mult)
        tt(G["abi"][:], G["ta"][:], G["Sn"][:], ALU.mult)
        dv(lambda e: e.tensor_scalar(out=G["SSg"][:], in0=G["Sn"][:], scalar1=SGF[:, 0:1], scalar2=None, op0=ALU.mult), reads=[SGF.r])
        tt(G["nur"][:], G["EP"][:], G["Cc"][:], ALU.mult)
        tt(G["nui"][:], G["EP"][:], G["SSg"][:], ALU.mult)
        tt(G["mur"][:], G["EN"][:], G["Cc"][:], ALU.mult)
        tt(G["mui"][:], G["EN"][:], G["SSg"][:], ALU.mult)
        dv(lambda e: e.tensor_scalar(out=G["mui"][:], in0=G["mui"][:], scalar1=-1.0, scalar2=None, op0=ALU.mult))
        dv(lambda e: e.tensor_scalar(out=G["ta"][:], in0=G["abr"][:], scalar1=-1.0, scalar2=None, op0=ALU.add))
        tt(G["tb"][:], G["ta"][:], G["LR"][:], ALU.mult)
        tt(G["tc"][:], G["abi"][:], G["LI"][:], ALU.mult)
        tt(G["fr"][:], G["tb"][:], G["tc"][:], ALU.add)
        tt(G["tb"][:], G["abi"][:], G["LR"][:], ALU.mult)
        tt(G["tc"][:], G["ta"][:], G["LI"][:], ALU.mult)
        tt(G["fi"][:], G["tb"][:], G["tc"][:], ALU.subtract)
        tt(G["tb"][:], G["LR"][:], G["LR"][:], ALU.mult)
        tt(G["tc"][:], G["LI"][:], G["LI"][:], ALU.mult)
        tt(G["tb"][:], G["tb"][:], G["tc"][:], ALU.add)
        dv(lambda e: e.reciprocal(out=G["tb"][:], in_=G["tb"][:]))
        tt(G["fr"][:], G["fr"][:], G["tb"][:], ALU.mult)
        tt(G["fi"][:], G["fi"][:], G["tb"][:], ALU.mult)
        dv(lambda e: e.memset(pwCr[:, :, 0:1], 1.0)); dv(lambda e: e.memset(pwCi[:, :, 0:1], 0.0))
        dv(lambda e: e.memset(pwBr[:, :, 0:1], 1.0)); dv(lambda e: e.memset(pwBi[:, :, 0:1], 0.0))
        for t in range(1, 8):
            cmul(pwCr[:, :, t], pwCi[:, :, t], pwCr[:, :, t - 1], pwCi[:, :, t - 1], G["nur"][:], G["nui"][:], G["tb"][:], G["tc"][:])
            cmul(pwBr[:, :, t], pwBi[:, :, t], pwBr[:, :, t - 1], pwBi[:, :, t - 1], G["mur"][:], G["mui"][:], G["tb"][:], G["tc"][:])
        cmul(G["mu8r"][:], G["mu8i"][:], pwBr[:, :, 7], pwBi[:, :, 7], G["mur"][:], G["mui"][:], G["tb"][:], G["tc"][:])
        dv(lambda e: e.memset(G["k1r"][:], 1.0)); dv(lambda e: e.memset(G["k1i"][:], 0.0))
        dv(lambda e: e.tensor_copy(out=G["k1r"][0:64, :], in_=pwCr[0:64, :, 7])); dv(lambda e: e.tensor_copy(out=G["k1i"][0:64, :], in_=pwCi[0:64, :, 7]))
        dv(lambda e: e.tensor_copy(out=G["k3r"][0:64, :], in_=G["nur"][0:64, :])); dv(lambda e: e.tensor_copy(out=G["k3i"][0:64, :], in_=G["nui"][0:64, :]))
        dv(lambda e: e.tensor_copy(out=G["k3r"][64:128, :], in_=G["mu8r"][64:128, :])); dv(lambda e: e.tensor_copy(out=G["k3i"][64:128, :], in_=G["mu8i"][64:128, :]))
        for hh in range(2):
            t1v = Er[:].rearrange("p g s -> p (g s)")[:, 0:256].rearrange("p (g t) -> p g t", g=64)
            t2v = Ei[:].rearrange("p g s -> p (g s)")[:, 0:256].rearrange("p (g t) -> p g t", g=64)
            frb = G["fr"][:].unsqueeze(2).to_broadcast([128, 64, 4]); fib = G["fi"][:].unsqueeze(2).to_broadcast([128, 64, 4])
            sl_ = slice(4 * hh, 4 * hh + 4)
            cmul(pwBr[:, :, sl_], pwBi[:, :, sl_], pwBr[:, :, sl_], pwBi[:, :, sl_], frb, fib, t1v, t2v)
        dv(lambda e: e.tensor_copy(out=wpr[:, :, 0], in_=G["Cc"][:])); dv(lambda e: e.tensor_copy(out=wpi[:, :, 0], in_=G["Sn"][:]))
        for _ in range(3):
            cmul(wpr[:, :, 0], wpi[:, :, 0], wpr[:, :, 0], wpi[:, :, 0], wpr[:, :, 0], wpi[:, :, 0], G["tb"][:], G["tc"][:])
        dv(lambda e: e.tensor_scalar(out=wpi[:, :, 0], in0=wpi[:, :, 0], scalar1=NSG[:, 0:1], scalar2=None, op0=ALU.mult))
        for k in range(1, 8):
            cmul(wpr[:, :, k], wpi[:, :, k], wpr[:, :, k - 1], wpi[:, :, k - 1], wpr[:, :, k - 1], wpi[:, :, k - 1], G["tb"][:], G["tc"][:])
        dv(lambda e: e.tensor_scalar(out=nwpi, in0=wpi, scalar1=-1.0, scalar2=None, op0=ALU.mult))
        dv(lambda e: e.tensor_copy(out=G["spr"][0:64, :], in_=wpr[0:64, :, 0])); dv(lambda e: e.tensor_copy(out=G["spi"][0:64, :], in_=nwpi[0:64, :, 0]))
        dv(lambda e: e.tensor_copy(out=G["spr"][64:128, :], in_=wpr[64:128, :, 7])); dv(lambda e: e.tensor_copy(out=G["spi"][64:128, :], in_=nwpi[64:128, :, 7]))
        cmul(G["stPr"][:], G["stPi"][:], G["k1r"][:], G["k1i"][:], G["spr"][:], G["spi"][:], G["tb"][:], G["tc"][:])
        dv(lambda e: e.tensor_scalar(out=G["ta"][:], in0=G["spi"][:], scalar1=-1.0, scalar2=None, op0=ALU.mult))
        cmul(G["stQr"][:], G["stQi"][:], G["k3r"][:], G["k3i"][:], G["spr"][:], G["ta"][:], G["tb"][:], G["tc"][:])
        dv(lambda e: e.tensor_copy(out=Er[0:64, :, 0], in_=wpr[0:64, :, 5])); dv(lambda e: e.tensor_copy(out=Ei[0:64, :, 0], in_=nwpi[0:64, :, 5]))
        dv(lambda e: e.tensor_copy(out=Er[64:128, :, 0], in_=wpr[64:128, :, 7])); dv(lambda e: e.tensor_copy(out=Ei[64:128, :, 0], in_=wpi[64:128, :, 7]))
        for sg_ in range(1, NSEG):
            cmul(Er[:, :, sg_], Ei[:, :, sg_], Er[:, :, sg_ - 1], Ei[:, :, sg_ - 1], wpr[:, :, 5], nwpi[:, :, 5], G["tb"][:], G["tc"][:])
        S.barrier()

        flat = lambda t_: t_[:].rearrange("p a b c -> p (a b c)")
        PT1 = sa("PT1", [128, NG, 2, 64], F32)
        PT2 = sa("PT2", [128, NG, 2, 64], F32, ob)
        assert oa[0] <= BIG_OFF, (oa[0], BIG_OFF)
        assert ob[0] <= top - 256, (ob[0], top)
        R_T1, R_T2, R_X, R_Gs, R_P, R_Q, R_CO = [Res("ssm_%s_%d" % (n, i)) for n in ("T1", "T2", "X", "Gs", "P", "Q", "CO")]
        R_Bs, R_Cs, R_Bb, R_W1, R_W2, R_STO, R_BCT, R_PT = [Res("ssm_%s_%d" % (n, i)) for n in ("Bs", "Cs", "Bb", "W1", "W2", "STO", "BCT", "PT")]

        for tl in (Cfre, Cfni, Cbre, Cbni):
            S.op("pool", lambda e, tl=tl: e.memset(tl[:], 0.0), writes=[R_Cs])
        for tl in (W1fr, W1fi, W1br, W1bi):
            S.op("pool", lambda e, tl=tl: e.memset(tl[:], 0.0), writes=[R_W1])

        def Dv(fn, r=(), w=()):
            S.op("dve", fn, reads=list(r), writes=list(w))

        def Pl(fn, r=(), w=()):
            S.op("pool", fn, reads=list(r), writes=list(w))

        def TT(eng, out, a_, b_, op, r=(), w=()):
            S.op(eng, lambda e: e.tensor_tensor(out=out, in0=a_, in1=b_, op=op), reads=list(r), writes=list(w))

        def gen_table(blk, which, part="all"):
            g0 = blk * NG
            gs = slice(g0, g0 + NG)
            (TB, s_r, s_i, Rt) = ((PTAB, G["stPr"], G["stPi"], R_P), (QTAB, G["stQr"], G["stQi"], R_Q))[which]
            if part in ("all", "lo"):
                Pl(lambda e: e.tensor_copy(out=TB[:, :, 0, 0], in_=s_r[:, gs]), r=[RG], w=[Rt])
                Pl(lambda e: e.tensor_copy(out=TB[:, :, 1, 0], in_=s_i[:, gs]), r=[RG], w=[Rt])
            L = 1
            k = 0
            while L < QLEN:
                ntot = min(L, QLEN - L)
                on_dve = (L >= 64) and part != "all"
                do = (part == "all") or (part == "hi" and on_dve) or (part == "lo" and not on_dve)
                o0 = 0
                while do and o0 < ntot:
                    n_ = min(64, ntot - o0)
                    wr_b = wpr[:, gs, k:k + 1].unsqueeze(3).to_broadcast([128, NG, 2, n_])
                    wsel = WIP[:, gs, k, :] if which == 0 else WIP[:, gs, k, ::-1]
                    wi_b = wsel.unsqueeze(3).to_broadcast([128, NG, 2, n_])
                    a_ = TB[:, :, :, o0:o0 + n_]
                    a_sw = TB[:, :, ::-1, o0:o0 + n_]
                    o_ = TB[:, :, :, L + o0:L + o0 + n_]
                    if on_dve:
                        x1 = flat(T1)[:, 0:NG * 2 * n_].rearrange("p (g c m) -> p g c m", g=NG, c=2)
                        x2 = flat(T2)[:, 0:NG * 2 * n_].rearrange("p (g c m) -> p g c m", g=NG, c=2)
                        TT("dve", x1, a_, wr_b, ALU.mult, r=[Rt, RG], w=[R_T1])
                        TT("dve", x2, a_sw, wi_b, ALU.mult, r=[Rt, RG], w=[R_T2])
                        TT("dve", o_, x1, x2, ALU.add, r=[R_T1, R_T2], w=[Rt])
                    else:
                        x1, x2 = PT1[:, :, :, 0:n_], PT2[:, :, :, 0:n_]
                        TT("pool", x1, a_, wr_b, ALU.mult, r=[Rt, RG], w=[R_PT["1"]])
                        TT("pool", x2, a_sw, wi_b, ALU.mult, r=[Rt, RG], w=[R_PT["2"]])
                        TT("pool", o_, x1, x2, ALU.add, r=[R_PT], w=[Rt])
                    o0 += n_
                L += ntot
                k += 1

        def gen_coef(blk):
            g0 = blk * NG
            gs = slice(g0, g0 + NG)
            rb = G["rho8"][:, gs].unsqueeze(2).unsqueeze(3).to_broadcast([128, NG, NSEG, NSLOT])

            def Ac(out, in_, **kw):
                S.op("act", lambda e: e.activation(out=out, in_=in_, func=AF.Copy, **kw), reads=[RG, LINK.r], writes=[R_CO])
            Ac(COEF[:], rb)
            Ac(COEF[0:64, :, :, 0:1], COEF[0:64, :, :, 2:3], scale=0.0, bias=1.0)
            Ac(COEF[64:128, :, :, NSLOT - 1:NSLOT], COEF[64:128, :, :, 2:3], scale=0.0, bias=1.0)
            Ac(COEF[0:64, :, 0:1, 0:1], COEF[0:64, :, 0:1, 2:3], scale=0.0)
            Ac(COEF[64:128, :, NSEG - 1:NSEG, NSLOT - 1:NSLOT], COEF[64:128, :, NSEG - 1:NSEG, 2:3], scale=0.0)
            lf = LINK[0:64, 0:NSEG].unsqueeze(1).unsqueeze(3).to_broadcast([64, NG, NSEG, 1])
            lb = LINK[64:128, NSEG:2 * NSEG].unsqueeze(1).unsqueeze(3).to_broadcast([64, NG, NSEG, 1])
            Ac(COEF[0:64, :, :, 1:2], lf)
            Ac(COEF[64:128, :, :, NSLOT - 2:NSLOT - 1], lb)

        def gen_Bs(blk):
            g0 = blk * NG
            gs = slice(g0, g0 + NG)
            dma(BCT[:], ssm_bc_d[j].rearrange("k p g q -> p k g q")[:, :, gs, :], writes=[R_BCT])

            def outer(o_re, o_im, pr, pi, vr, vi, neg_im, Rout):
                prb = pr[:, gs, :].unsqueeze(3).to_broadcast([128, NG, 8, 16]); pib = pi[:, gs, :].unsqueeze(3).to_broadcast([128, NG, 8, 16])
                vrb = vr.unsqueeze(2).to_broadcast([128, NG, 8, 16]); vib = vi.unsqueeze(2).to_broadcast([128, NG, 8, 16])
                t1_ = flat(T1)[:, 0:512].rearrange("p (g t q) -> p g t q", g=NG, t=8)
                t2_ = flat(T2)[:, 0:512].rearrange("p (g t q) -> p g t q", g=NG, t=8)
                ore = o_re[:].rearrange("p g (t q) -> p g t q", t=8); oim = o_im[:].rearrange("p g (t q) -> p g t q", t=8)
                rin = [RG, R_BCT]
                TT("dve", t1_, prb, vrb, ALU.mult, r=rin, w=[R_T1]); TT("dve", t2_, pib, vib, ALU.mult, r=rin, w=[R_T2])
                TT("dve", ore, t1_, t2_, ALU.subtract, r=[R_T1, R_T2], w=[Rout])
                TT("dve", t1_, prb, vib, ALU.mult, r=rin, w=[R_T1]); TT("dve", t2_, pib, vrb, ALU.mult, r=rin, w=[R_T2])
                if neg_im:
                    Dv(lambda e: e.scalar_tensor_tensor(out=oim, in0=t1_, scalar=-1.0, in1=t2_, op0=ALU.mult, op1=ALU.subtract), r=[R_T1, R_T2], w=[Rout])
                else:
                    TT("dve", oim, t1_, t2_, ALU.add, r=[R_T1, R_T2], w=[Rout])
            outer(Bsre, Bsim, pwBr, pwBi, BCT[:, 0], BCT[:, 1], False, R_Bs)
            for (src, df, db) in ((Bsre, W1fr, W1br), (Bsim, W1fi, W1bi)):
                bk = next_bank()
                for g in range(NG):
                    S.op("pe", lambda e, src=src, g=g, bk=bk: e.transpose(bk[:, g * 128:(g + 1) * 128], src[:, g, :], IDENT[:]), reads=[R_Bs, IDENT.r], writes=[bk.r])
                bv = bk[:].rearrange("p (g m) -> p g m", g=NG)
                S.op("act", lambda e, bv=bv, df=df: e.activation(out=df[:, :, 0:64], in_=bv[:, :, 0:64], func=AF.Copy), reads=[bk.r], writes=[R_W1])
                S.op("act", lambda e, bv=bv, db=db: e.activation(out=db[:, :, 64:128], in_=bv[:, :, 64:128], func=AF.Copy), reads=[bk.r], writes=[R_W1])

        def gen_Cs(blk):
            g0 = blk * NG
            gs = slice(g0, g0 + NG)

            def outer(o_re, o_im, pr, pi, vr, vi, neg_im, Rout):
                prb = pr[:, gs, :].unsqueeze(3).to_broadcast([128, NG, 8, 16]); pib = pi[:, gs, :].unsqueeze(3).to_broadcast([128, NG, 8, 16])
                vrb = vr.unsqueeze(2).to_broadcast([128, NG, 8, 16]); vib = vi.unsqueeze(2).to_broadcast([128, NG, 8, 16])
                t1_ = flat(T1)[:, 0:512].rearrange("p (g t q) -> p g t q", g=NG, t=8)
                t2_ = flat(T2)[:, 0:512].rearrange("p (g t q) -> p g t q", g=NG, t=8)
                ore = o_re[:].rearrange("p g (t q) -> p g t q", t=8); oim = o_im[:].rearrange("p g (t q) -> p g t q", t=8)
                rin = [RG, R_BCT]
                TT("dve", t1_, prb, vrb, ALU.mult, r=rin, w=[R_T1]); TT("dve", t2_, pib, vib, ALU.mult, r=rin, w=[R_T2])
                TT("dve", ore, t1_, t2_, ALU.subtract, r=[R_T1, R_T2], w=[Rout])
                TT("dve", t1_, prb, vib, ALU.mult, r=rin, w=[R_T1]); TT("dve", t2_, pib, vrb, ALU.mult, r=rin, w=[R_T2])
                if neg_im:
                    Dv(lambda e: e.scalar_tensor_tensor(out=oim, in0=t1_, scalar=-1.0, in1=t2_, op0=ALU.mult, op1=ALU.subtract), r=[R_T1, R_T2], w=[Rout])
                else:
                    TT("dve", oim, t1_, t2_, ALU.add, r=[R_T1, R_T2], w=[Rout])
            outer(Csre, Csni, pwCr, pwCi, BCT[:, 2], BCT[:, 3], True, R_Cs)
            S.op("act", lambda e: e.activation(out=Bbre[:], in_=Bsre[:], func=AF.Copy), reads=[R_Bs], writes=[R_Bb])
            S.op("act", lambda e: e.activation(out=Bbim[:], in_=Bsim[:], func=AF.Copy), reads=[R_Bs], writes=[R_Bb])
            for (dst, src, lo, hi) in ((Cfre, Csre, 0, 64), (Cfni, Csni, 0, 64), (Cbre, Csre, 64, 128), (Cbni, Csni, 64, 128)):
                S.op("act", lambda e, dst=dst, src=src, lo=lo, hi=hi: e.activation(out=dst[lo:hi], in_=src[lo:hi], func=AF.Copy), reads=[R_Cs], writes=[R_Cs["m"]])
            bkf, bkb = next_bank(), next_bank()
            for g in range(NG):
                for (bk_, cr, cn) in ((bkf, Cfre, Cfni), (bkb, Cbre, Cbni)):
                    S.op("pe", lambda e, g=g, bk_=bk_, cr=cr: e.matmul(bk_[:, g * 128:(g + 1) * 128], lhsT=Bbre[:, g, :], rhs=cr[:, g, :], start=True, stop=False), reads=[R_Bb, R_Cs], writes=[bk_.r])
                    S.op("pe", lambda e, g=g, bk_=bk_, cn=cn: e.matmul(bk_[:, g * 128:(g + 1) * 128], lhsT=Bbim[:, g, :], rhs=cn[:, g, :], start=False, stop=True), reads=[R_Bb, R_Cs], writes=[bk_.r])
            mLb = MASKL[:].unsqueeze(1).to_broadcast([128, NG, 128]); mUb = MASKU[:].unsqueeze(1).to_broadcast([128, NG, 128])
            t1w = flat(T1)[:, 0:512].rearrange("p (g m) -> p g m", g=NG); t2w = flat(T2)[:, 0:512].rearrange("p (g m) -> p g m", g=NG)
            TT("dve", t1w, bkf[:].rearrange("p (g m) -> p g m", g=NG), mLb, ALU.mult, r=[bkf.r, MASKL.r], w=[R_T1])
            TT("dve", t2w, bkb[:].rearrange("p (g m) -> p g m", g=NG), mUb, ALU.mult, r=[bkb.r, MASKU.r], w=[R_T2])
            TT("dve", t1w, t1w, t2w, ALU.add, r=[R_T1, R_T2], w=[R_T1])
            for g in range(NG):
                Dv(lambda e, g=g: e.scalar_tensor_tensor(out=W2[:, g, :], in0=IDENT[:], scalar=G["dpp"][:, g0 + g:g0 + g + 1], in1=t1w[:, g, :], op0=ALU.mult, op1=ALU.add),
                   r=[IDENT.r, RG, R_T1], w=[R_W2[g]])

        SBK = {}
        ssm_rr = [0]

        def next_bank():
            b_ = PS[4 + ssm_rr[0] % 4]
            ssm_rr[0] += 1
            return b_

        def usel(blk):
            g0 = blk * NG
            ct = blk // 2
            par = blk % 2
            gs = slice(g0, g0 + NG)
            U = Ub[par]
            Hr, Hi = Hre[par], Him[par]
            ubanks = [next_bank(), next_bank()]
            uv = BIG[:, ct, :].rearrange("p (c t) -> p c t", t=TCH)
            for g in range(NG):
                gl = (g0 + g) % 8
                bk = ubanks[g // 2]
                for t_ in range(8):
                    S.op("pe", lambda e, g=g, gl=gl, t_=t_, bk=bk: e.matmul(bk[:, (g % 2) * NCOL:(g % 2 + 1) * NCOL], lhsT=BAND[:, gl, 112 - 16 * t_:240 - 16 * t_],
                                                                            rhs=uv[:, :, t_], start=(t_ == 0), stop=(t_ == 7)),
                         reads=[BAND.r, BIGr[ct]], writes=[bk.r])
            for h in range(2):
                S.op("act", lambda e, h=h: e.activation(out=U[:, 2 * h:2 * h + 2, :, 0:NCH], in_=ubanks[h][:, 0:2 * NCOL].rearrange("p (g s k) -> p g s k", g=2, s=NSEG), func=AF.Copy),
                     reads=[ubanks[h].r], writes=[RU[par]])

        def sprime(blk):
            g0 = blk * NG
            ct = blk // 2
            par = blk % 2
            gs = slice(g0, g0 + NG)
            U = Ub[par]
            Hr, Hi = Hre[par], Him[par]
            sb_re = [PS[0], PS[1]]
            sb_im = [PS[2], PS[3]]
            for g in range(NG):
                for (bks, wf, wb_) in ((sb_re, W1fr, W1br), (sb_im, W1fi, W1bi)):
                    bk = bks[g // 2]
                    ov = bk[:, (g % 2) * 204:(g % 2 + 1) * 204].rearrange("p (s k) -> p s k", s=NSEG)
                    uin = U[:, g, :, 0:NCH]
                    for s_ in range(NSEG):
                        S.op("pe", lambda e, g=g, ov=ov, uin=uin, wf=wf, s_=s_: e.matmul(ov[:, s_, 2:34], lhsT=wf[:, g, :], rhs=uin[:, s_, :], start=True, stop=False, skip_group_check=True), reads=[R_W1, RU[par]], writes=[bk.r])
                        S.op("pe", lambda e, g=g, ov=ov, uin=uin, wb_=wb_, s_=s_: e.matmul(ov[:, s_, 0:32], lhsT=wb_[:, g, :], rhs=uin[:, s_, :], start=False, stop=True, skip_group_check=True), reads=[R_W1, RU[par]], writes=[bk.r])
            SBK[blk] = (sb_re, sb_im)

        def core(blk):
            g0 = blk * NG
            ct = blk // 2
            par = blk % 2
            gs = slice(g0, g0 + NG)
            U = Ub[par]
            Hr, Hi = Hre[par], Him[par]
            sb_re, sb_im = SBK.pop(blk)
            for h in range(2):
                sre = sb_re[h][:, 0:408].rearrange("p (g s k) -> p g s k", g=2, s=NSEG)
                sim_ = sb_im[h][:, 0:408].rearrange("p (g s k) -> p g s k", g=2, s=NSEG)

                def win(Tt):
                    base_ap = Tt[:, 2 * h:2 * h + 2, 0:NSLOT]
                    return bass.AP(tensor=base_ap.tensor, offset=base_ap.offset, ap=[list(base_ap.ap[0]), list(base_ap.ap[1]), [NCH, NSEG], [1, NSLOT]])
                pr_, pi_ = win(Pre), win(Pim)
                xr, xi = Xre[:, 2 * h:2 * h + 2], Xim[:, 2 * h:2 * h + 2]
                a1, a2 = T1[:, 2 * h:2 * h + 2], T2[:, 2 * h:2 * h + 2]
                rdb = [sb_re[h].r, sb_im[h].r, R_P]
                TT("dve", a1, sre, pr_, ALU.mult, r=rdb, w=[R_T1[h]]); TT("dve", a2, sim_, pi_, ALU.mult, r=rdb, w=[R_T2[h]])
                TT("dve", xr, a1, a2, ALU.subtract, r=[R_T1[h], R_T2[h]], w=[R_X["r%d" % h]])
                TT("dve", a1, sim_, pr_, ALU.mult, r=rdb, w=[R_T1[h]]); TT("dve", a2, sre, pi_, ALU.mult, r=rdb, w=[R_T2[h]])
                TT("dve", xi, a1, a2, ALU.add, r=[R_T1[h], R_T2[h]], w=[R_X["i%d" % h]])
            if blk + 1 < 16:
                gen_table(blk + 1, 0)
            Dv(lambda e: e.tensor_copy(out=Xre[0:64, :, 0, 1], in_=G["h0r"][0:64, gs]), r=[RG, R_X], w=[R_X])
            Dv(lambda e: e.tensor_copy(out=Xim[0:64, :, 0, 1], in_=G["h0i"][0:64, gs]), r=[RG], w=[R_X["hi0"]])
            Dv(lambda e: e.tensor_copy(out=Xre[64:128, :, 3, 32], in_=G["h0r"][64:128, gs]), r=[RG], w=[R_X["hr1"]])
            Dv(lambda e: e.tensor_copy(out=Xim[64:128, :, 3, 32], in_=G["h0i"][64:128, gs]), r=[RG], w=[R_X["hi1"]])
            for (Xs, Gs, nm) in ((Xre, Gre, "r"), (Xim, Gim, "i")):
                xf, gf, cf = flat(Xs), flat(Gs), flat(COEF)
                Dv(lambda e, xf=xf, gf=gf, cf=cf: e.tensor_tensor_scan(out=gf[0:64, :], data0=cf[0:64, :], data1=xf[0:64, :], initial=0.0, op0=ALU.mult, op1=ALU.add),
                   r=[R_X, R_CO], w=[R_Gs[nm + "f"]])
                Dv(lambda e, xf=xf, gf=gf, cf=cf: e.tensor_tensor_scan(out=gf[64:128, ::-1], data0=cf[64:128, ::-1], data1=xf[64:128, ::-1], initial=0.0, op0=ALU.mult, op1=ALU.add),
                   r=[R_X, R_CO], w=[R_Gs[nm + "b"]])
            if blk + 1 < 16:
                gen_coef(blk + 1)

            def win4(Tt):
                base_ap = Tt[:, :, 0:NSLOT]
                return bass.AP(tensor=base_ap.tensor, offset=base_ap.offset, ap=[list(base_ap.ap[0]), list(base_ap.ap[1]), [NCH, NSEG], [1, NSLOT]])
            qr_, qi_ = win4(Qre), win4(Qim)
            TT("dve", T1[:], Gre[:], qr_, ALU.mult, r=[R_Gs, R_Q], w=[R_T1]); TT("dve", T2[:], Gim[:], qi_, ALU.mult, r=[R_Gs, R_Q], w=[R_T2])
            TT("dve", Hr[:], T1[:], T2[:], ALU.subtract, r=[R_T1, R_T2], w=[RH[par]["r"]])
            TT("dve", T1[:], Gim[:], qr_, ALU.mult, r=[R_Gs, R_Q], w=[R_T1]); TT("dve", T2[:], Gre[:], qi_, ALU.mult, r=[R_Gs, R_Q], w=[R_T2])
            TT("dve", Hi[:], T1[:], T2[:], ALU.add, r=[R_T1, R_T2], w=[RH[par]["i"]])
            if blk + 1 < 16:
                gen_table(blk + 1, 1)
            for (lo, hi, sl_) in ((0, 64, NSLOT - 1), (64, 128, 0)):
                S.op("act", lambda e, lo=lo, hi=hi, sl_=sl_: e.activation(out=STOr[lo:hi, gs, :], in_=Gre[lo:hi, :, :, sl_], func=AF.Copy),
                     reads=[R_Gs], writes=[R_STO[(blk, lo, 0)]])
                S.op("act", lambda e, lo=lo, hi=hi, sl_=sl_: e.activation(out=STOi[lo:hi, gs, :], in_=Gim[lo:hi, :, :, sl_], func=AF.Copy),
                     reads=[R_Gs], writes=[R_STO[(blk, lo, 1)]])

        def ymm(blk):
            g0 = blk * NG
            ct = blk // 2
            par = blk % 2
            gs = slice(g0, g0 + NG)
            U = Ub[par]
            Hr, Hi = Hre[par], Him[par]
            ybanks = [next_bank(), next_bank()]
            for g in range(NG):
                bk = ybanks[g // 2]
                ov = bk[:, (g % 2) * 204:(g % 2 + 1) * 204].rearrange("p (s k) -> p s k", s=NSEG)[:, :, 0:NCH]
                for s_ in range(NSEG):
                    S.op("pe", lambda e, g=g, ov=ov, s_=s_: e.matmul(ov[:, s_, :], lhsT=W2[:, g, :], rhs=U[:, g, s_, 0:NCH], start=True, stop=False), reads=[R_W2, RU[par]], writes=[bk.r])
                    S.op("pe", lambda e, g=g, ov=ov, s_=s_: e.matmul(ov[:, s_, :], lhsT=Csre[:, g, :], rhs=Hr[:, g, s_, 1:33], start=False, stop=False), reads=[R_Cs, RH[par]], writes=[bk.r])
                    S.op("pe", lambda e, g=g, ov=ov, s_=s_: e.matmul(ov[:, s_, :], lhsT=Csni[:, g, :], rhs=Hi[:, g, s_, 1:33], start=False, stop=True), reads=[R_Cs, RH[par]], writes=[bk.r])
            for h in range(2):
                S.op("act", lambda e, h=h: e.activation(out=Ysb[:, par * NG + 2 * h:par * NG + 2 * h + 2, :].rearrange("p g (s k) -> p g s k", s=NSEG), in_=ybanks[h][:, 0:408].rearrange("p (g s k) -> p g s k", g=2, s=NSEG)[:, :, :, 0:NCH], func=AF.Copy),
                     reads=[ybanks[h].r], writes=[RY[(par, h)]])

        def selback(blk):
            g0 = blk * NG
            ct = blk // 2
            par = blk % 2
            gs = slice(g0, g0 + NG)
            U = Ub[par]
            Hr, Hi = Hre[par], Him[par]
            if par == 1:
                zv = HT[:, ct, :].rearrange("p (c t) -> p c t", t=TCH)
                for tp in range(4):
                    bk = next_bank()
                    for t2_ in range(2):
                        t_ = tp * 2 + t2_
                        for gl in range(8):
                            S.op("pe", lambda e, t_=t_, t2_=t2_, gl=gl, bk=bk: e.matmul(bk[:, t2_ * NCOL:(t2_ + 1) * NCOL], lhsT=BAND[:, t_, 112 - 16 * gl:240 - 16 * gl],
                                                                                        rhs=Ysb[:, gl, :], start=(gl == 0), stop=(gl == 7)),
                                 reads=[BAND.r, RY], writes=[bk.r])
                        S.op("act", lambda e, t_=t_, t2_=t2_, bk=bk: e.activation(out=zv[:, :, t_], in_=bk[:, t2_ * NCOL:(t2_ + 1) * NCOL], func=AF.Gelu_apprx_tanh),
                             reads=[bk.r], writes=[HT.r[ct]])
        S.mark("L%d ssmcore" % i)
        gen_table(0, 0)
        gen_coef(0)
        gen_table(0, 1)
        usel(0)
        gen_Bs(0)
        sprime(0)
        gen_Cs(0)
        for blk_ in range(16):
            core(blk_)
            nxt = blk_ + 1 < 16
            if nxt:
                usel(blk_ + 1)
            ymm(blk_)
            if nxt:
                gen_Bs(blk_ + 1)
                sprime(blk_ + 1)
            selback(blk_)
            if nxt:
                gen_Cs(blk_ + 1)
        sv = lambda tl: flat(tl)[:, 0:64 * NSEG].rearrange("p (g s) -> p g s", g=64)
        x1_, x2_ = sv(T1), sv(T2)
        y1_ = flat(Xre)[:, 0:64 * NSEG].rearrange("p (g s) -> p g s", g=64)
        TT("dve", x1_, STOr[:], Er[:], ALU.mult, r=[R_STO, RG], w=[R_T1]); TT("dve", x2_, STOi[:], Ei[:], ALU.mult, r=[R_STO, RG], w=[R_T2])
        TT("dve", y1_, x1_, x2_, ALU.subtract, r=[R_T1, R_T2], w=[R_X])
        TT("dve", x1_, STOr[:], Ei[:], ALU.mult, r=[R_STO, RG], w=[R_T1]); TT("dve", x2_, STOi[:], Er[:], ALU.mult, r=[R_STO, RG], w=[R_T2])
        TT("dve", STOi[:], x1_, x2_, ALU.add, r=[R_T1, R_T2], w=[R_STO])
        Dv(lambda e: e.tensor_copy(out=STOr[:], in_=y1_), r=[R_X], w=[R_STO])
        dma(st_d[j, 0], STOr[:].rearrange("p g s -> p (g s)"), reads=[R_STO])
        dma(st_d[j, 1], STOi[:].rearrange("p g s -> p (g s)"), reads=[R_STO])
        S.barrier()
        S.mark("L%d ssmout" % i)
        def conso(ci, ot, tb, bk):
            dt = ci * 2 + ot // 2
            if ot % 2 == 0:
                pend[tb] = bk
            else:
                bka = pend.pop(tb)
                Mt = MS[0]
                S.op("act", lambda e: e.activation(out=TMP[:, tb, :], in_=bk[:], func=AF.Sigmoid), reads=[bk.r], writes=[TMPr[tb]])
                S.op("dve", lambda e: e.tensor_tensor(out=TMP[:, tb, :], in0=bka[:], in1=TMP[:, tb, :], op=ALU.mult), reads=[bka.r, TMPr[tb]], writes=[TMPr[tb]])
                for h in range(2):
                    sg_ = tb * 2 + h
                    S.op("dve", lambda e, h=h, sg_=sg_: e.scalar_tensor_tensor(
                        out=XT[:, dt, sg_ * SEGL:(sg_ + 1) * SEGL], in0=TMP[:, tb, h * SEGL:(h + 1) * SEGL], scalar=Mt[:, 16 + dt, sg_:sg_ + 1],
                        in1=XT[:, dt, sg_ * SEGL:(sg_ + 1) * SEGL], op0=ALU.mult, op1=ALU.add),
                        reads=[TMPr[tb], Mt.r, XT.r[dt]], writes=[XT.r[dt]])
        pend = {}
        proj(ssm_w_out_d[j], D, [[c * 256, D + c * 256, c * 256 + 128, D + c * 256 + 128] for c in range(4)], ht_src, ht_reads, conso, hint=lambda: load_chunk((("f1", i, 0), 0), ffn_w1_d[i], D, [o * 128 for o in range(4)]))

    for li, i in enumerate(layers):
        kind, j = i % 3, i // 3
        Mc = MODSL[li % 2]
        MS[0] = Mc
        if li == 0:
            adaln_load(i, 0)
            for cb in range(12):
                if cb + 1 < 12:
                    adaln_load(i, cb + 1)
                adaln_block(i, cb, Mc)
        for (At, Gt, o0) in ((A1, GMIX, 8), (A2, GFFN, 32)):
            for dt in range(8):
                S.op("dve", lambda e, At=At, Gt=Gt, o0=o0, dt=dt, i=i, Mc=Mc: e.tensor_scalar(
                    out=At[:, dt, :], in0=Mc[:, o0 + dt, :], scalar1=1.0, scalar2=Gt[:, i, dt:dt + 1], op0=ALU.add, op1=ALU.mult),
                    reads=[Mc.r, Gt.r], writes=[At.r])
        S.mark("L%d norm1" % i)
        norm_mod(A1, lambda dt, Mc=Mc: Mc[:, 0 + dt, :])
        S.mark("L%d mixer" % i)
        if kind == 2:
            conv_layer(i)
        elif kind == 1:
            gmlp_layer(i)
        else:
            ssm_layer(i, j)
        S.barrier()
        S.mark("L%d norm2" % i)
        norm_mod(A2, lambda dt, Mc=Mc: Mc[:, 24 + dt, :])
        S.mark("L%d ffn" % i)
        upd2 = x_update(40)
        mix_hint = None
        if li + 1 < len(layers):
            ni = layers[li + 1]
            nk, nj = ni % 3, ni // 3
            if nk == 2:
                mix_hint = lambda ni=ni: load_chunk((("mix", ni), 0), cv_w_in_d, D, [0, D, 2 * D])
            elif nk == 1:
                mix_hint = lambda ni=ni: load_chunk((("mix", ni), 0), gm_w_in_d, D, [o * 128 for o in range(4)])
            else:
                mix_hint = lambda ni=ni, nj=nj: load_chunk((("mix", ni), 0), ssm_w_in_d[nj], D, [o * 128 for o in range(4)])
        stg_slots[:] = [0, 1, 2, 3, 4, 5]
        ada_q = [(layers[li + 1], cb, MODSL[(li + 1) % 2]) for cb in range(12)] if li + 1 < len(layers) else []
        ada_ld = []
        ada_tr = []

        def ada_step():
            if ada_tr:
                adaln_tr(*ada_tr.pop(0))
            if ada_ld:
                blk_ = ada_ld.pop(0)
                adaln_mm(*blk_)
                ada_tr.append(blk_)
            if ada_q:
                nx = ada_q.pop(0)
                adaln_load(nx[0], nx[1])
                ada_ld.append(nx)
        for fc in range(8):
            hb = fc % 2

            def cons1(ci, ot, tb, bk, hb=hb):
                tslot = (ot * NTB + tb) % 3
                S.op("act", lambda e: e.activation(out=TMP[:, tslot, :], in_=bk[:], func=AF.Relu), reads=[bk.r], writes=[TMPr[tslot]])
                S.op("act", lambda e: e.activation(out=BIG[:, hb * 4 + ot, tb * 512:(tb + 1) * 512], in_=TMP[:, tslot, :], func=AF.Square),
                     reads=[TMPr[tslot]], writes=[BIGr[hb * 4 + ot]])
            w1cols = lambda f: [f * 512 + o * 128 for o in range(4)]
            w2cols = [o * 128 for o in range(8)]
            w2v = lambda f: ffn_w2_d[i][f * 512:(f + 1) * 512, :]
            proj(ffn_w1_d[i], D, [w1cols(fc)], ht_src, ht_reads, cons1, name=("f1", i, fc),
                 hint=lambda fc=fc: load_chunk((("f2", i, fc), 0), w2v(fc), 512, w2cols))
            ada_step()
            proj(w2v(fc), 512, [w2cols],
                 lambda kt, tb, hb=hb: BIG[:, hb * 4 + kt, tb * 512:(tb + 1) * 512], lambda kt, hb=hb: [BIGr[hb * 4 + kt]],
                 lambda ci, ot, tb, bk: upd2(ot, tb, bk), name=("f2", i, fc),
                 hint=(lambda fc=fc: load_chunk((("f1", i, fc + 1), 0), ffn_w1_d[i], D, w1cols(fc + 1))) if fc < 7 else mix_hint)
            ada_step()
        while ada_ld or ada_q or ada_tr:
            ada_step()
        stg_slots[:] = [0, 1, 2]
        S.barrier()

    S.mark("final")
    if do_final:
        for dt in range(8):
            S.op("act", lambda e, dt=dt: e.activation(out=HT[:, dt, :], in_=XT[:, dt, :], func=AF.Square), reads=[XT.r[dt]], writes=[HT.r[dt]])
        for tb in range(NTB):
            bk = next_bank()
            for dt in range(8):
                S.op("pe", lambda e, dt=dt, tb=tb, bk=bk: e.matmul(bk[:], lhsT=ONES[:], rhs=HT[:, dt, tb * 512:(tb + 1) * 512], start=(dt == 0), stop=(dt == 7)),
                     reads=[ONES.r, HT.r[dt]], writes=[bk.r])
            S.op("act", lambda e, tb=tb, bk=bk: e.activation(out=RSTD[:, tb * 512:(tb + 1) * 512], in_=bk[:], func=AF.Sqrt, bias=EPSB[:, 0:1], scale=1.0),
                 reads=[bk.r, EPSB.r], writes=[RSTD.r[tb]])
            S.op("dve", lambda e, tb=tb: e.reciprocal(out=RSTD[:, tb * 512:(tb + 1) * 512], in_=RSTD[:, tb * 512:(tb + 1) * 512]),
                 reads=[RSTD.r[tb]], writes=[RSTD.r[tb]])
        yT_v = yT_d.rearrange("(kt p) n -> p kt n", p=128)
        for dt in range(8):
            S.op("dve", lambda e, dt=dt: e.scalar_tensor_tensor(out=XT[:, dt, :], in0=XT[:, dt, :], scalar=GFIN[:, dt:dt + 1], in1=RSTD[:], op0=ALU.mult, op1=ALU.mult),
                 reads=[XT.r[dt], RSTD.r, GFIN.r], writes=[XT.r[dt]])
            dma(yT_v[:, dt, :], XT[:, dt, :], reads=[XT.r[dt]])
    S.emit()
    es.close()
    nc._marks = S.marks
    return nc


def _consts():
    c = np.zeros((5, 128, 128), np.float32)
    c[0] = np.eye(128, dtype=np.float32)
    k = np.arange(128)
    tk = k // 16
    c[1] = (tk[None, :] >= tk[:, None]).astype(np.float32)
    c[2] = (tk[:, None] >= tk[None, :]).astype(np.float32)
    c[3, :64, 0] = 1.0
    c[3, 64:, 0] = -1.0
    freq = 1.0 / (10000.0 ** (np.arange(256, dtype=np.float64) / 256.0))
    c[3, :, 1] = (freq[:128] / (2 * np.pi)).astype(np.float32)
    c[3, :, 2] = (freq[128:] / (2 * np.pi)).astype(np.float32)
    band = np.zeros((128, 8, 240), np.float32)
    for a in range(8):
        for kk in range(16 * a, 16 * a + 16):
            band[kk, a, kk - 16 * a + 112] = 1.0
    return c, band.reshape(128, 8 * 240)


def _core_segments(c):
    if c < 4:
        return [("s", c, s) for s in range(4)] + [("p", 2 * c), ("p", 2 * c + 1)]
    return [("p", 8 + 6 * (c - 4) + s) for s in range(6)]


def make_in_maps(inp):
    f = lambda a: np.ascontiguousarray(np.asarray(a, dtype=np.float32))
    consts, band = _consts()
    shared = {
        "w_mod": f(inp["w_mod"]),
        "b_modT": f(np.asarray(inp["b_mod"]).reshape(4, 48, 128).transpose(0, 2, 1)),
        "g_mixT": f(np.asarray(inp["g_mix"]).reshape(4, 8, 128).transpose(0, 2, 1)),
        "g_ffnT": f(np.asarray(inp["g_ffn"]).reshape(4, 8, 128).transpose(0, 2, 1)),
        "g_finT": f(np.asarray(inp["g_final"]).reshape(8, 128).T),
        "ffn_w1": f(inp["ffn_w1"]), "ffn_w2": f(inp["ffn_w2"]),
        "ssm_w_in": f(inp["ssm_w_in"]), "ssm_w_out": f(inp["ssm_w_out"]),
        "gm_w_in": f(np.asarray(inp["gmlp_w_in"])[0]),
        "gm_wsT": f(np.asarray(inp["gmlp_w_s"])[0].transpose(2, 0, 1)),
        "gm_bs": f(np.asarray(inp["gmlp_b_s"])[0].reshape(1, 2048)),
        "gm_w_out": f(np.asarray(inp["gmlp_w_out"])[0]),
        "cv_w_in": f(np.asarray(inp["conv_w_in"])[0]),
        "cv_wT": f(np.asarray(inp["conv_w"])[0].reshape(3, 8, 128).transpose(2, 0, 1)),
        "cv_w_out": f(np.asarray(inp["conv_w_out"])[0]),
        "consts": consts, "band": band,
    }
    lr = np.asarray(inp["ssm_lam_re"]); li = np.asarray(inp["ssm_lam_im"]); ld = np.asarray(inp["ssm_log_dt"])
    small = np.zeros((2, 3, 128, 64), np.float32)
    bc = np.zeros((2, 4, 128, 64, 16), np.float32)
    dpp = np.zeros((2, 128, 64), np.float32)
    for j in range(2):
        small[j, 0] = lr[j].transpose(0, 2, 1).reshape(128, 64)
        small[j, 1] = li[j].transpose(0, 2, 1).reshape(128, 64)
        small[j, 2] = np.broadcast_to(ld[j][:, None, :], (2, 64, 64)).reshape(128, 64)
        bc[j, 0] = np.asarray(inp["ssm_b_re"])[j].transpose(0, 2, 1, 3).reshape(128, 64, 16)
        bc[j, 1] = np.asarray(inp["ssm_b_im"])[j].transpose(0, 2, 1, 3).reshape(128, 64, 16)
        bc[j, 2] = np.asarray(inp["ssm_c_re"])[j].transpose(0, 3, 1, 2).reshape(128, 64, 16)
        bc[j, 3] = np.asarray(inp["ssm_c_im"])[j].transpose(0, 3, 1, 2).reshape(128, 64, 16)
        dpp[j] = np.tile(np.asarray(inp["ssm_d"])[j].reshape(64, 16).T, (8, 1))
    shared["ssm_small"] = small; shared["ssm_bc"] = bc; shared["ssm_dpp"] = dpp
    xp = np.asarray(inp["x_prompt"]); xs = np.asarray(inp["x_sample"])
    cc = np.asarray(inp["c"]); cctx = np.asarray(inp["c_ctx"])
    sre = np.asarray(inp["state_ssm_re"]); sim = np.asarray(inp["state_ssm_im"])
    maps = []
    for c in range(8):
        segs = _core_segments(c)
        x = np.concatenate([xs[s[1], s[2] * 256:(s[2] + 1) * 256] if s[0] == "s" else xp[s[1]] for s in segs], axis=0)
        cond = np.stack([cc[s[1]] if s[0] == "s" else cctx for s in segs], axis=1)
        m = dict(shared)
        m["xT"] = f(x.T)
        m["condT"] = f(cond)
        is_s = c < 4
        m["flag"] = np.full((1, 1), 1.0 if is_s else 0.0, np.float32)
        lf = np.zeros(6, np.float32); lb = np.zeros(6, np.float32)
        if is_s:
            lf[1:4] = 1.0; lb[0:3] = 1.0
        m["link"] = np.concatenate([lf, lb]).reshape(1, 12)
        h0 = np.zeros((2, 2, 128, 64), np.float32)
        if is_s:
            for j in range(2):
                h0[j, 0, :64] = sre[c, j, 0].T; h0[j, 0, 64:] = sre[c, j, 1].T
                h0[j, 1, :64] = sim[c, j, 0].T; h0[j, 1, 64:] = sim[c, j, 1].T
        m["h0"] = h0
        maps.append(m)
    return maps


def assemble(results):
    yp = np.zeros((32, 256, D), np.float32)
    ys = np.zeros((4, 1024, D), np.float32)
    nre = np.zeros((32, 2, 2, 64, 64), np.float32)
    nim = np.zeros((32, 2, 2, 64, 64), np.float32)
    for c in range(8):
        y = np.asarray(results[c]["yT"]).T
        st = np.asarray(results[c]["st"]).reshape(2, 2, 2, 64, 64, NSEG)
        for si, s in enumerate(_core_segments(c)):
            blk = y[si * 256:(si + 1) * 256]
            if s[0] == "s":
                ys[s[1], s[2] * 256:(s[2] + 1) * 256] = blk
            else:
                yp[s[1]] = blk
                nre[s[1]] = st[:, 0, :, :, :, si].transpose(0, 1, 3, 2)
                nim[s[1]] = st[:, 1, :, :, :, si].transpose(0, 1, 3, 2)
    return yp, ys, nre, nim


_NC_CACHE = {}


def kernel(**inputs):
    maps = make_in_maps(inputs)
    if "nc" not in _NC_CACHE:
        _NC_CACHE["nc"] = build_nc()
    nc = _NC_CACHE["nc"]
    res = run_bass_kernel_spmd(nc, maps, core_ids=list(range(8)))
    return assemble(res.results)
```

```python
import contextlib
import math
import numpy as np
import concourse.bass as bass
import concourse.mybir as mybir
from concourse.bass_utils import run_bass_kernel_spmd

F32 = mybir.dt.float32
BF16 = mybir.dt.bfloat16
I32 = mybir.dt.int32
F32R = mybir.dt.float32r
AF = mybir.ActivationFunctionType
ALU = mybir.AluOpType

D = 1024
NSEG = 6
SEGL = 256
NTOK = NSEG * SEGL
NTB = NTOK // 512
TCH = 8
NCH = SEGL // TCH
NCOL = NSEG * NCH
NSLOT = NCH + 2
NG = 4
QLEN = 200
EPS = 1e-6
PREFETCH = True


class Res:
    def __init__(self, name):
        self.name = name
        self.whole_w = None
        self.whole_r = []
        self.sub = {}

    def __getitem__(self, key):
        return (self, key)


def _norm(r):
    if isinstance(r, Res):
        return (r, None)
    return r


class Op:
    __slots__ = ("eng", "fn", "deps", "dma", "signal", "idx", "ticket", "sem")

    def __init__(self, eng, fn, dma):
        self.eng = eng
        self.fn = fn
        self.deps = set()
        self.dma = dma
        self.signal = False
        self.ticket = None
        self.sem = None


class Sched:
    ENGS = ("pe", "act", "dve", "pool", "sp")

    def __init__(self, nc, ndma_sems=8):
        self.nc = nc
        self.ops = []
        self.ndma = ndma_sems
        self.last = {}
        self.pending_dma = []
        self.marks = []

    def mark(self, name):
        self.marks.append((name, sum(1 for o in self.ops if o.eng == "pe" and not o.dma)))

    def _collect(self, op, r, is_write):
        res, key = _norm(r)
        deps = op.deps
        if res.whole_w is not None:
            deps.add(res.whole_w)
        if is_write:
            deps.update(res.whole_r)
        if key is None:
            for (w, rd) in res.sub.values():
                if w is not None:
                    deps.add(w)
                if is_write:
                    deps.update(rd)
        else:
            st = res.sub.get(key)
            if st is not None:
                if st[0] is not None:
                    deps.add(st[0])
                if is_write:
                    deps.update(st[1])

    def _commit(self, op, r, is_write):
        res, key = _norm(r)
        if key is None:
            if is_write:
                res.whole_w = op.idx
                res.whole_r = []
                res.sub = {}
            else:
                res.whole_r.append(op.idx)
        else:
            st = res.sub.get(key)
            if st is None:
                st = [None, []]
                res.sub[key] = st
            if is_write:
                st[0] = op.idx
                st[1] = []
            else:
                st[1].append(op.idx)

    def op(self, eng, fn, reads=(), writes=(), dma=False):
        o = Op(eng, fn, dma)
        o.idx = len(self.ops)
        for r in reads:
            self._collect(o, r, False)
        for r in writes:
            self._collect(o, r, True)
        for r in reads:
            self._commit(o, r, False)
        for r in writes:
            self._commit(o, r, True)
        o.deps.discard(o.idx)
        self.ops.append(o)
        if dma:
            self.pending_dma.append(o.idx)
        else:
            self.last[eng] = o.idx
        return o

    def barrier(self):
        deps = set(self.last.values()) | set(self.pending_dma)
        self.pending_dma = []
        for e in self.ENGS:
            o = Op(e, (lambda eng: eng.nop()), False)
            o.idx = len(self.ops)
            o.deps = set(deps)
            self.ops.append(o)

    def emit(self):
        nc = self.nc
        ops = self.ops

        def needs(o, dop):
            return dop.dma or o.dma or dop.eng != o.eng or o.eng == "dve"

        for o in ops:
            for d in o.deps:
                if needs(o, ops[d]):
                    ops[d].signal = True
        for o in ops:
            if o.dma:
                o.signal = True
        with contextlib.ExitStack() as es:
            esem = {e: es.enter_context(nc.semaphore("s_" + e)) for e in self.ENGS}
            dsem = {}
            for e in ("sp", "act"):
                dsem[e] = [es.enter_context(nc.semaphore("d_%s_%d" % (e, i))) for i in range(self.ndma if e == "sp" else 2)]
            ecount = {e: 0 for e in self.ENGS}
            dcount = {e: [0] * len(dsem[e]) for e in dsem}
            drr = {e: 0 for e in dsem}
            for o in ops:
                if o.dma:
                    k = drr[o.eng] % len(dsem[o.eng])
                    drr[o.eng] += 1
                    dcount[o.eng][k] += 1
                    o.sem = dsem[o.eng][k]
                    o.ticket = 16 * dcount[o.eng][k]
                elif o.signal:
                    ecount[o.eng] += 1
                    o.sem = esem[o.eng]
                    o.ticket = ecount[o.eng]
            per_eng = {e: [o for o in ops if o.eng == e] for e in self.ENGS}
            block = es.enter_context(nc.Block())

            def make(e):
                def body(eng):
                    waited = {}
                    for o in per_eng[e]:
                        waits = {}
                        for d in o.deps:
                            dop = ops[d]
                            if not needs(o, dop):
                                continue
                            key = id(dop.sem)
                            if waits.get(key, (None, 0))[1] < dop.ticket:
                                waits[key] = (dop.sem, dop.ticket)
                        if o.dma and o.ticket > 16:
                            key = id(o.sem)
                            if waits.get(key, (None, 0))[1] < o.ticket - 16:
                                waits[key] = (o.sem, o.ticket - 16)
                        for key, (sem, val) in waits.items():
                            if waited.get(key, 0) >= val:
                                continue
                            eng.wait_ge(sem, val)
                            waited[key] = val
                        ins = o.fn(eng)
                        if o.signal:
                            ins.then_inc(o.sem, 16 if o.dma else 1)
                    if e == "sp":
                        for q in dsem:
                            for k in range(len(dsem[q])):
                                if dcount[q][k] > 0:
                                    eng.wait_ge(dsem[q][k], 16 * dcount[q][k])

                return body

            block.tensor(make("pe"))
            block.scalar(make("act"))
            block.vector(make("dve"))
            block.gpsimd(make("pool"))
            block.sync(make("sp"))
        return nc


class T:
    def __init__(self, t, name):
        self.t = t
        self.r = Res(name)

    def __getitem__(self, k):
        return self.t[k]


def build_nc(layers=(0, 1, 2, 3), do_final=True):
    nc = bass.Bass("TRN2", target_bir_lowering=False)
    S = Sched(nc)

    def din(name, shape):
        return nc.dram_tensor(name, list(shape), F32, kind="ExternalInput").ap()

    def dout(name, shape):
        return nc.dram_tensor(name, list(shape), F32, kind="ExternalOutput").ap()

    xT_d = din("xT", [D, NTOK])
    condT_d = din("condT", [D, NSEG])
    flag_d = din("flag", [1, 1])
    link_d = din("link", [1, 2 * NSEG])
    h0_d = din("h0", [2, 2, 128, 64])
    w_mod_d = din("w_mod", [4, D, 6 * D])
    b_modT_d = din("b_modT", [4, 128, 48])
    g_mixT_d = din("g_mixT", [4, 128, 8])
    g_ffnT_d = din("g_ffnT", [4, 128, 8])
    g_finT_d = din("g_finT", [128, 8])
    ffn_w1_d = din("ffn_w1", [4, D, 4 * D])
    ffn_w2_d = din("ffn_w2", [4, 4 * D, D])
    ssm_w_in_d = din("ssm_w_in", [2, D, D])
    ssm_w_out_d = din("ssm_w_out", [2, D, 2 * D])
    ssm_small_d = din("ssm_small", [2, 3, 128, 64])
    ssm_bc_d = din("ssm_bc", [2, 4, 128, 64, 16])
    ssm_dpp_d = din("ssm_dpp", [2, 128, 64])
    gm_w_in_d = din("gm_w_in", [D, 4 * D])
    gm_wsT_d = din("gm_wsT", [128, 16, 128])
    gm_bs_d = din("gm_bs", [1, 16 * 128])
    gm_w_out_d = din("gm_w_out", [2 * D, D])
    cv_w_in_d = din("cv_w_in", [D, 3 * D])
    cv_wT_d = din("cv_wT", [128, 3, 8])
    cv_w_out_d = din("cv_w_out", [D, D])
    consts_d = din("consts", [5, 128, 128])
    band_d = din("band", [128, 8 * 240])
    yT_d = dout("yT", [D, NTOK])
    st_d = dout("st", [2, 2, 128, 64 * NSEG])

    es = contextlib.ExitStack()
    base = (int(nc.sbuf_base) + 63) // 64 * 64
    top = 229344
    arena = es.enter_context(nc.sbuf_tensor("arena", [128, (top - base) // 4 - 64], F32))
    cur = [base]

    def alloc(name, shape, dt, off=None):
        nbytes = int(np.prod(shape[1:])) * (4 if dt in (F32, I32, F32R) else 2)
        nbytes = (nbytes + 31) // 32 * 32
        if off is None:
            off = cur[0]
            cur[0] += nbytes
        assert off + nbytes <= top - 256, (name, off, nbytes)
        t = nc.alloc_sbuf_tensor_at(name, list(shape), dt, offset=off)
        return T(t, name), off + nbytes

    def palloc(name, shape, dt):
        return alloc(name, shape, dt)[0]

    XT = palloc("XT", [128, 8, NTOK], F32)
    HT = palloc("HT", [128, 8, NTOK], BF16)
    MODSL = [palloc("MODSa", [128, 48, NSEG], F32), palloc("MODSb", [128, 48, NSEG], F32)]
    MS = [MODSL[0]]
    A1 = palloc("A1", [128, 8, NSEG], F32)
    A2 = palloc("A2", [128, 8, NSEG], F32)
    BMOD = palloc("BMOD", [128, 4, 48], F32)
    GMIX = palloc("GMIX", [128, 4, 8], F32)
    GFFN = palloc("GFFN", [128, 4, 8], F32)
    GFIN = palloc("GFIN", [128, 8], F32)
    CONDS = palloc("CONDS", [128, 8, NSEG], F32)
    CONDSB = palloc("CONDSB", [128, 8, NSEG], BF16)
    IDENT = palloc("IDENT", [128, 128], F32)
    MASKL = palloc("MASKL", [128, 128], F32)
    MASKU = palloc("MASKU", [128, 128], F32)
    SGF = palloc("SGF", [128, 128], F32)
    ONES = palloc("ONES", [128, 128], BF16)
    BAND = palloc("BAND", [128, 8, 240], BF16)
    FLAG = palloc("FLAG", [128, 1], F32)
    LINK = palloc("LINK", [128, 2 * NSEG], F32)
    EPSB = palloc("EPSB", [128, 1], F32)
    CVW = palloc("CVW", [128, 3, 8], F32)
    NL = palloc("NL", [128, NSEG], F32)
    NCVW = palloc("NCVW", [128, 3, 8], F32)
    CB = palloc("CB", [128, 8], F32)
    ST1 = palloc("ST1", [128, 32], F32)
    ARENA0 = cur[0]

    off = ARENA0
    WB, _ = alloc("WB", [128, 4, 4096], BF16, off)
    STG, off = alloc("STG", [128, 3, 2048], F32, off + 24576)
    BIG_OFF = off
    BIG, off = alloc("BIG", [128, 16, NTOK], BF16, off)
    RSTD_OFF = off
    RSTD, off = alloc("RSTD", [128, NTOK], F32, off)
    XN, off = alloc("XN", [128, NTOK], F32, off)
    TMP_OFF = off
    TMP, off = alloc("TMP", [128, 3, 512], F32, off)
    GEN_END = off
    WBr = [Res("WB%d" % i) for i in range(4)]
    STGr = [Res("STG%d" % i) for i in range(6)]
    stg_slots = [0, 1, 2]
    STG2 = nc.alloc_sbuf_tensor_at("STG2", [128, 3, 2048], F32, offset=BIG_OFF + 8 * NTOK * 2)

    def stg_ap(s, n):
        return STG[:, s, 0:n] if s < 3 else STG2[:, s - 3, 0:n]

    def next_stg():
        s = stg_slots[stg_rr[0] % len(stg_slots)]
        stg_rr[0] += 1
        return s
    TMPr = [Res("TMP%d" % i) for i in range(3)]

    PS = [T(es.enter_context(nc.psum_tensor("ps%d" % i, [128, 512], F32)), "ps%d" % i) for i in range(8)]
    bank_rr = [0]

    def next_bank():
        b = PS[bank_rr[0] % 8]
        bank_rr[0] += 1
        return b

    dmaq_rr = [0]

    def dma(out_ap, in_ap, reads=(), writes=(), q=None):
        if q is None:
            q = "sp"
        S.op(q, lambda e: e.dma_start(out=out_ap, in_=in_ap), reads=reads, writes=writes, dma=True)

    dma(IDENT[:], consts_d[0], writes=[IDENT.r])
    dma(MASKL[:], consts_d[1], writes=[MASKL.r])
    dma(MASKU[:], consts_d[2], writes=[MASKU.r])
    dma(SGF[:], consts_d[3], writes=[SGF.r])
    dma(STG[:, 2, 0:1920], band_d, writes=[STGr[2]])
    S.op("dve", lambda e: e.tensor_copy(out=BAND[:].rearrange("p a b -> p (a b)"), in_=STG[:, 2, 0:1920]), reads=[STGr[2]], writes=[BAND.r])
    S.op("dve", lambda e: e.memset(ONES[:], 1.0 / D), writes=[ONES.r])
    S.op("dve", lambda e: e.memset(EPSB[:], EPS), writes=[EPSB.r])
    dma(FLAG[:], flag_d.partition_broadcast(128), writes=[FLAG.r])
    dma(LINK[:], link_d.partition_broadcast(128), writes=[LINK.r])
    dma(BMOD[:], b_modT_d.rearrange("i p o -> p i o"), writes=[BMOD.r])
    dma(GMIX[:], g_mixT_d.rearrange("i p o -> p i o"), writes=[GMIX.r])
    dma(GFFN[:], g_ffnT_d.rearrange("i p o -> p i o"), writes=[GFFN.r])
    dma(GFIN[:], g_finT_d, writes=[GFIN.r])
    dma(CVW[:], cv_wT_d, writes=[CVW.r])
    dma(CONDS[:], condT_d.rearrange("(kt p) s -> p kt s", p=128), writes=[CONDS.r])
    xT_v = xT_d.rearrange("(kt p) n -> p kt n", p=128)
    for kt in range(8):
        dma(XT[:, kt, :], xT_v[:, kt, :], writes=[XT.r[kt]])
    S.op("act", lambda e: e.activation(out=CONDSB[:], in_=CONDS[:], func=AF.Silu), reads=[CONDS.r], writes=[CONDSB.r])

    rcI = nc.alloc_sbuf_tensor_at("rcI", [128, 2, 1024], I32, offset=BIG_OFF + 20480)
    rcF = nc.alloc_sbuf_tensor_at("rcF", [128, 2, 1024], F32, offset=BIG_OFF + 20480 + 8192)
    angF = nc.alloc_sbuf_tensor_at("angF", [128, 1024], F32, offset=BIG_OFF + 20480 + 16384)
    angI = nc.alloc_sbuf_tensor_at("angI", [128, 1024], I32, offset=BIG_OFF + 20480 + 20480)
    angK = nc.alloc_sbuf_tensor_at("angK", [128, 1024], F32, offset=BIG_OFF + 20480 + 24576)
    PEr = Res("pe_scratch")
    S.op("pool", lambda e: e.iota(rcI[:, 0, :], pattern=[[1, 16], [0, 64]], base=0, channel_multiplier=0), writes=[PEr])
    S.op("pool", lambda e: e.iota(rcI[:, 1, :], pattern=[[0, 16], [1, 64]], base=0, channel_multiplier=0), writes=[PEr])
    S.op("dve", lambda e: e.tensor_copy(out=rcF[:], in_=rcI[:]), reads=[PEr], writes=[PEr])
    TWO_PI = 2.0 * math.pi
    for tile in range(8):
        which = tile // 4
        is_cos = (tile // 2) % 2
        fcol = 1 + (tile % 2)
        shift = 0.75 if is_cos else 0.5
        S.op("dve", lambda e, which=which, fcol=fcol, shift=shift: e.tensor_scalar(
            out=angF[:], in0=rcF[:, which, :], scalar1=SGF[:, fcol:fcol + 1], scalar2=shift, op0=ALU.mult, op1=ALU.add),
            reads=[PEr, SGF.r], writes=[PEr])
        S.op("dve", lambda e: e.tensor_copy(out=angI[:], in_=angF[:]), reads=[PEr], writes=[PEr])
        S.op("dve", lambda e: e.tensor_copy(out=angK[:], in_=angI[:]), reads=[PEr], writes=[PEr])
        S.op("dve", lambda e: e.tensor_tensor(out=angF[:], in0=angF[:], in1=angK[:], op=ALU.subtract), reads=[PEr], writes=[PEr])
        S.op("dve", lambda e: e.tensor_scalar(out=angK[:], in0=angF[:], scalar1=0.0, scalar2=None, op0=ALU.is_lt), reads=[PEr], writes=[PEr])
        S.op("dve", lambda e: e.tensor_tensor(out=angF[:], in0=angF[:], in1=angK[:], op=ALU.add), reads=[PEr], writes=[PEr])
        S.op("dve", lambda e: e.tensor_scalar(out=angF[:], in0=angF[:], scalar1=TWO_PI, scalar2=-math.pi, op0=ALU.mult, op1=ALU.add), reads=[PEr], writes=[PEr])
        S.op("dve", lambda e: e.tensor_scalar(out=angF[:], in0=angF[:], scalar1=-math.pi, scalar2=math.pi, op0=ALU.max, op1=ALU.min), reads=[PEr], writes=[PEr])
        S.op("act", lambda e: e.activation(out=angK[:], in_=angF[:], func=AF.Sin), reads=[PEr], writes=[PEr])
        S.op("dve", lambda e, tile=tile: e.scalar_tensor_tensor(
            out=XT[:, tile, 0:1024], in0=angK[:], scalar=FLAG[:, 0:1], in1=XT[:, tile, 0:1024], op0=ALU.mult, op1=ALU.add),
            reads=[PEr, FLAG.r, XT.r[tile]], writes=[XT.r[tile]])

    stg_rr = [0]
    wb_rr = [0]
    cast_rr = [0]

    def load_chunk(src_aps, slot=None):
        if slot is None:
            slot = wb_rr[0] % 2
            wb_rr[0] += 1
        for i, sap in enumerate(src_aps):
            a, b = sap.shape[1], sap.shape[2]
            n = a * b
            assert n <= 2048
            s = next_stg()
            dma(STG[:, s, 0:n].rearrange("p (a b) -> p a b", a=a), sap, writes=[STGr[s]])
            ce = "act" if (cast_rr[0] % 6) != 5 else "pool"
            cast_rr[0] += 1
            if ce == "pool":
                S.op("pool", lambda e, s=s, slot=slot, i=i, n=n: e.tensor_copy(out=WB[:, slot, i * 2048:i * 2048 + n], in_=STG[:, s, 0:n]),
                     reads=[STGr[s]], writes=[WBr[slot]])
            else:
                S.op("act", lambda e, s=s, slot=slot, i=i, n=n: e.activation(out=WB[:, slot, i * 2048:i * 2048 + n], in_=STG[:, s, 0:n], func=AF.Copy),
                     reads=[STGr[s]], writes=[WBr[slot]])
        return slot

    def wview(w2d, kt0, nkt, c0, ncols):
        return w2d.rearrange("(kt p) n -> p kt n", p=128)[:, kt0:kt0 + nkt, c0:c0 + ncols]

    def proj_chunk(slot, nkt, n_ot, src, src_reads, consumer, kt_layout_cols):
        wv = WB[:, slot, 0:nkt * kt_layout_cols].rearrange("p (k c) -> p k c", k=nkt)
        for ot in range(n_ot):
            banks = [next_bank() for _ in range(NTB)]
            for kt in range(nkt):
                for tb in range(NTB):
                    S.op("pe", lambda e, ot=ot, kt=kt, tb=tb, bk=banks[tb]: e.matmul(
                        bk[:], lhsT=wv[:, kt, ot * 128:(ot + 1) * 128], rhs=src(kt, tb), start=(kt == 0), stop=(kt == nkt - 1)),
                        reads=[WBr[slot]] + list(src_reads(kt)), writes=[banks[tb].r])
            for tb in range(NTB):
                consumer(ot, tb, banks[tb])

    loaded = {}

    def next_wb():
        for _ in range(4):
            slot = wb_rr[0] % 3
            wb_rr[0] += 1
            if slot not in loaded.values():
                return slot
        raise AssertionError("no free WB slot")

    def load_chunk(key, w2d, K, cols):
        if key in loaded:
            return
        nkt = K // 128
        ncol = 128 * len(cols)
        assert nkt * ncol <= 4096
        per_kt = ncol
        kpp = 1
        for k in range(1, nkt + 1):
            if nkt % k == 0 and k * per_kt <= 2048:
                kpp = k
        slot = next_wb()
        npieces = nkt // kpp
        runs = []
        for j, c0 in enumerate(cols):
            if runs and runs[-1][1] + runs[-1][2] == c0:
                runs[-1][2] += 128
            else:
                runs.append([j, c0, 128])
        for p in range(npieces):
            s = next_stg()
            n = kpp * per_kt
            sap = stg_ap(s, n)
            stv = sap.rearrange("p (k c) -> p k c", k=kpp)
            for (j, c0, wdt) in runs:
                dma(stv[:, :, j * 128:j * 128 + wdt], wview(w2d, p * kpp, kpp, c0, wdt), writes=[STGr[s]])
            ce = "act" if (cast_rr[0] % 6) != 5 else "pool"
            cast_rr[0] += 1
            dst = WB[:, slot, p * n:(p + 1) * n]
            if ce == "pool":
                S.op("pool", lambda e, sap=sap, dst=dst: e.tensor_copy(out=dst, in_=sap), reads=[STGr[s]], writes=[WBr[slot]])
            else:
                S.op("act", lambda e, sap=sap, dst=dst: e.activation(out=dst, in_=sap, func=AF.Copy), reads=[STGr[s]], writes=[WBr[slot]])
        loaded[key] = slot

    def proj(w2d, K, cols_list, src, src_reads, consumer, name=None, hint=None, after=None):
        nkt = K // 128
        if name is None:
            name = ("anon", len(S.ops))
        for ci, cols in enumerate(cols_list):
            load_chunk((name, ci), w2d, K, cols)
            if PREFETCH:
                if ci + 1 < len(cols_list):
                    load_chunk((name, ci + 1), w2d, K, cols_list[ci + 1])
                elif hint is not None:
                    hint()
            slot = loaded.pop((name, ci))
            proj_chunk(slot, nkt, len(cols), src, src_reads, lambda ot, tb, bk, ci=ci: consumer(ci, ot, tb, bk), 128 * len(cols))
            if after is not None:
                after(ci)

    def adaln_load(i, cb):
        load_chunk(("ada", i, cb), w_mod_d[i], D, [cb * 512 + o * 128 for o in range(4)])

    XNr2 = [Res("XNa0"), Res("XNa1")]

    def adaln_mm(i, cb, Mdst):
        key = ("ada", i, cb)
        load_chunk(key, w_mod_d[i], D, [cb * 512 + o * 128 for o in range(4)])
        slot = loaded.pop(key)
        wv = WB[:, slot, 0:4096].rearrange("p (k c) -> p k c", k=8)
        bk = next_bank()
        for kt in range(8):
            S.op("pe", lambda e, kt=kt: e.matmul(bk[0:NSEG, :], lhsT=CONDSB[:, kt, :], rhs=wv[:, kt, :], start=(kt == 0), stop=(kt == 7)),
                 reads=[WBr[slot], CONDSB.r], writes=[bk.r])
        x0 = (cb % 2) * 512
        S.op("act", lambda e: e.activation(out=XN[0:NSEG, x0:x0 + 512], in_=bk[0:NSEG, :], func=AF.Copy), reads=[bk.r], writes=[XN.r, XNr2[cb % 2]])

    def adaln_tr(i, cb, Mdst):
        x0 = (cb % 2) * 512
        bkT = next_bank()
        for c in range(4):
            S.op("pe", lambda e, c=c: e.transpose(bkT[:, c * 8:c * 8 + NSEG], XN[0:NSEG, x0 + c * 128:x0 + (c + 1) * 128], IDENT[0:NSEG, 0:NSEG]),
                 reads=[XNr2[cb % 2], IDENT.r], writes=[bkT.r])
        bb = BMOD[:, i, 4 * cb:4 * cb + 4].unsqueeze(2).to_broadcast([128, 4, NSEG])
        S.op("dve", lambda e: e.tensor_tensor(out=Mdst[:, 4 * cb:4 * cb + 4, :], in0=bkT[:, 0:32].rearrange("p (c s) -> p c s", c=4)[:, :, 0:NSEG], in1=bb, op=ALU.add),
             reads=[bkT.r, BMOD.r], writes=[Mdst.r[cb]])

    def adaln_block(i, cb, Mdst):
        adaln_mm(i, cb, Mdst)
        adaln_tr(i, cb, Mdst)

    def merge_cols(cols):
        return cols

    def norm_mod(Atile, Bsel):
        for dt in range(8):
            S.op("act", lambda e, dt=dt: e.activation(out=HT[:, dt, :], in_=XT[:, dt, :], func=AF.Square), reads=[XT.r[dt]], writes=[HT.r[dt]])
        for tb in range(NTB):
            bk = next_bank()
            for dt in range(8):
                S.op("pe", lambda e, dt=dt, tb=tb, bk=bk: e.matmul(bk[:], lhsT=ONES[:], rhs=HT[:, dt, tb * 512:(tb + 1) * 512], start=(dt == 0), stop=(dt == 7)),
                     reads=[ONES.r, HT.r[dt]], writes=[bk.r])
            S.op("act", lambda e, tb=tb, bk=bk: e.activation(out=RSTD[:, tb * 512:(tb + 1) * 512], in_=bk[:], func=AF.Sqrt, bias=EPSB[:, 0:1], scale=1.0),
                 reads=[bk.r, EPSB.r], writes=[RSTD.r[tb]])
            S.op("dve", lambda e, tb=tb: e.reciprocal(out=RSTD[:, tb * 512:(tb + 1) * 512], in_=RSTD[:, tb * 512:(tb + 1) * 512]),
                 reads=[RSTD.r[tb]], writes=[RSTD.r[tb]])
        TMPf = TMP[:].rearrange("p a b -> p (a b)")
        for dt in range(8):
            if dt % 2 == 0:
                xb, xr = XN[:], [XN.r]
            else:
                xb, xr = TMPf, list(TMPr)
            S.op("dve", lambda e, dt=dt, xb=xb: e.tensor_tensor(out=xb, in0=XT[:, dt, :], in1=RSTD[:], op=ALU.mult),
                 reads=[XT.r[dt], RSTD.r], writes=xr)
            for sg in range(NSEG):
                if Bsel is not None:
                    bap = Bsel(dt)[:, sg:sg + 1]
                    S.op("act", lambda e, dt=dt, sg=sg, bap=bap, xb=xb: e.activation(out=HT[:, dt, sg * SEGL:(sg + 1) * SEGL], in_=xb[:, sg * SEGL:(sg + 1) * SEGL],
                                                                      func=AF.Identity, scale=Atile[:, dt, sg:sg + 1], bias=bap),
                         reads=xr + [Atile.r, MS[0].r], writes=[HT.r[dt]])

    def x_update(gate_ot0):
        def cons(dt, tb, bk):
            Mt = MS[0]
            for h in range(2):
                sg = tb * 2 + h
                S.op("dve", lambda e, dt=dt, sg=sg, h=h, bk=bk, Mt=Mt: e.scalar_tensor_tensor(
                    out=XT[:, dt, sg * SEGL:(sg + 1) * SEGL], in0=bk[:, h * SEGL:(h + 1) * SEGL], scalar=Mt[:, gate_ot0 + dt, sg:sg + 1],
                    in1=XT[:, dt, sg * SEGL:(sg + 1) * SEGL], op0=ALU.mult, op1=ALU.add),
                    reads=[bk.r, Mt.r, XT.r[dt]], writes=[XT.r[dt]])
        return cons

    def ht_src(kt, tb):
        return HT[:, kt, tb * 512:(tb + 1) * 512]

    def ht_reads(kt):
        return [HT.r[kt]]

    BIGr = BIG.r

    def big_src(kt, tb):
        return BIG[:, kt, tb * 512:(tb + 1) * 512]

    def big_reads(kt):
        return [BIGr[kt]]

    S.op("dve", lambda e: e.tensor_scalar(out=NL[:], in0=LINK[:, 0:NSEG], scalar1=-1.0, scalar2=1.0, op0=ALU.mult, op1=ALU.add), reads=[LINK.r], writes=[NL.r])
    S.op("dve", lambda e: e.tensor_scalar(out=NCVW[:], in0=CVW[:], scalar1=-1.0, scalar2=None, op0=ALU.mult), reads=[CVW.r], writes=[NCVW.r])

    def conv_layer(i):
        def cons(ci, ot, tb, bk):
            dt = ci
            sl = slice(tb * 512, (tb + 1) * 512)
            if ot == 0:
                S.op("act", lambda e: e.activation(out=BIG[:, dt, sl], in_=bk[:], func=AF.Copy), reads=[bk.r], writes=[BIGr[dt]])
            elif ot == 1:
                S.op("act", lambda e: e.activation(out=TMP[:, tb, :], in_=bk[:], func=AF.Copy), reads=[bk.r], writes=[TMPr[tb]])
            else:
                S.op("dve", lambda e: e.tensor_tensor(out=BIG[:, 8 + dt, sl], in0=bk[:], in1=TMP[:, tb, :], op=ALU.mult),
                     reads=[bk.r, TMPr[tb]], writes=[BIGr[8 + dt]])
        def conv_dt(dt):
            V = BIG[:, 8 + dt, :]
            rd = [BIGr[8 + dt], CVW.r, NCVW.r, NL.r]
            S.op("dve", lambda e, V=V, dt=dt: e.tensor_scalar(out=XN[:], in0=V, scalar1=CVW[:, 1, dt:dt + 1], scalar2=None, op0=ALU.mult), reads=rd, writes=[XN.r])
            S.op("dve", lambda e, V=V, dt=dt: e.scalar_tensor_tensor(out=XN[:, 1:NTOK], in0=V[:, 0:NTOK - 1], scalar=CVW[:, 0, dt:dt + 1], in1=XN[:, 1:NTOK], op0=ALU.mult, op1=ALU.add),
                 reads=rd + [XN.r], writes=[XN.r])
            S.op("dve", lambda e, V=V, dt=dt: e.scalar_tensor_tensor(out=XN[:, 0:NTOK - 1], in0=V[:, 1:NTOK], scalar=CVW[:, 2, dt:dt + 1], in1=XN[:, 0:NTOK - 1], op0=ALU.mult, op1=ALU.add),
                 reads=rd + [XN.r], writes=[XN.r])
            S.op("dve", lambda e, V=V: e.tensor_tensor(out=CB[:, 0:5], in0=V[:, SEGL - 1:NTOK - 1:SEGL], in1=NL[:, 1:NSEG], op=ALU.mult), reads=rd, writes=[CB.r])
            S.op("dve", lambda e, dt=dt: e.scalar_tensor_tensor(out=XN[:, SEGL:NTOK:SEGL], in0=CB[:, 0:5], scalar=NCVW[:, 0, dt:dt + 1], in1=XN[:, SEGL:NTOK:SEGL], op0=ALU.mult, op1=ALU.add),
                 reads=rd + [CB.r, XN.r], writes=[XN.r])
            S.op("dve", lambda e, V=V: e.tensor_tensor(out=CB[:, 0:5], in0=V[:, SEGL:NTOK:SEGL], in1=NL[:, 1:NSEG], op=ALU.mult), reads=rd + [CB.r], writes=[CB.r])
            S.op("dve", lambda e, dt=dt: e.scalar_tensor_tensor(out=XN[:, SEGL - 1:NTOK - 1:SEGL], in0=CB[:, 0:5], scalar=NCVW[:, 2, dt:dt + 1], in1=XN[:, SEGL - 1:NTOK - 1:SEGL], op0=ALU.mult, op1=ALU.add),
                 reads=rd + [CB.r, XN.r], writes=[XN.r])
            S.op("dve", lambda e, V=V, dt=dt: e.tensor_tensor(out=V, in0=BIG[:, dt, :], in1=XN[:], op=ALU.mult), reads=[BIGr[dt], XN.r], writes=[BIGr[8 + dt]])

        proj(cv_w_in_d, D, [[dt * 128, D + dt * 128, 2 * D + dt * 128] for dt in range(8)], ht_src, ht_reads, cons, after=conv_dt, name=("mix", i))
        upd = x_update(16)
        proj(cv_w_out_d, D, [[c * 512 + o * 128 for o in range(4)] for c in range(2)],
             lambda kt, tb: BIG[:, 8 + kt, tb * 512:(tb + 1) * 512], lambda kt: [BIGr[8 + kt]],
             lambda ci, ot, tb, bk: upd(ci * 4 + ot, tb, bk), hint=lambda: load_chunk((("f1", i, 0), 0), ffn_w1_d[i], D, [o * 128 for o in range(4)]))

    def gmlp_layer(i):
        def consu(ci, ot, tb, bk):
            S.op("act", lambda e: e.activation(out=BIG[:, ci * 4 + ot, tb * 512:(tb + 1) * 512], in_=bk[:], func=AF.Gelu_apprx_tanh),
                 reads=[bk.r], writes=[BIGr[ci * 4 + ot]])
        proj(gm_w_in_d, D, [[c * 512 + o * 128 for o in range(4)] for c in range(4)], ht_src, ht_reads, consu, name=("mix", i))
        VTs = [nc.alloc_sbuf_tensor_at("VT%d" % k, [128, 2048], BF16, offset=RSTD_OFF + 4096 * k) for k in range(2)]
        VNs = [nc.alloc_sbuf_tensor_at("VN0", [128, 2048], BF16, offset=RSTD_OFF + 8192),
               nc.alloc_sbuf_tensor_at("VN1", [128, 2048], BF16, offset=GEN_END)]
        VTrs, VNrs = [Res("VT0"), Res("VT1")], [Res("VN0"), Res("VN1")]
        WST = nc.alloc_sbuf_tensor_at("WST", [128, 16, 128], BF16, offset=TMP_OFF + 2048)
        WSTr = Res("WST")
        S.barrier()
        stg_slots[:] = [1, 2]
        for sl4 in range(4):
            for pc in range(2):
                s = next_stg()
                stv = STG[:, s, :].rearrange("p (k c) -> p k c", k=4)
                dma(stv, wview(gm_w_in_d, pc * 4, 4, 2 * D + sl4 * 512, 512), writes=[STGr[s]])
                S.op("act", lambda e, s=s, sl4=sl4, pc=pc: e.activation(out=WB[:, sl4, pc * 2048:(pc + 1) * 2048], in_=STG[:, s, :], func=AF.Copy), reads=[STGr[s]], writes=[WBr[sl4]])
        dma(STG[:, 1, :].rearrange("p (g q) -> p g q", g=16), gm_wsT_d, writes=[STGr[1]])
        S.op("pool", lambda e: e.tensor_copy(out=WST[:].rearrange("p g q -> p (g q)"), in_=STG[:, 1, :]), reads=[STGr[1]], writes=[WSTr])
        dma(STG[:, 2, :], gm_bs_d.partition_broadcast(128), writes=[STGr[2]])
        VB = {}

        def vmm_pe(tt):
            tsl = slice(tt * 128, (tt + 1) * 128)
            bks = []
            for cb in range(4):
                bk = next_bank()
                bks.append(bk)
                for kt in range(8):
                    S.op("pe", lambda e, kt=kt, cb=cb, bk=bk, tsl=tsl: e.matmul(bk[:], lhsT=HT[:, kt, tsl], rhs=WB[:, cb, kt * 512:(kt + 1) * 512], start=(kt == 0), stop=(kt == 7)),
                         reads=[HT.r[kt], WBr[cb]], writes=[bk.r])
            VB[tt] = bks

        def vmm_evac(tt):
            VT, VTr = VTs[tt % 2], VTrs[tt % 2]
            c0 = 16 * (tt % 2)
            for cb, bk in enumerate(VB.pop(tt)):
                S.op("act", lambda e, cb=cb, bk=bk: e.activation(out=VT[:, cb * 512:(cb + 1) * 512], in_=bk[:], func=AF.Gelu_apprx_tanh, accum_out=ST1[:, c0 + cb:c0 + cb + 1]),
                     reads=[bk.r], writes=[VTr, ST1.r[tt % 2]])

        def chain(tt):
            VT, VN, VTr, VNr = VTs[tt % 2], VNs[tt % 2], VTrs[tt % 2], VNrs[tt % 2]
            c0 = 16 * (tt % 2)
            SR = ST1.r[tt % 2]
            cs = lambda a_, b_: ST1[:, c0 + a_:c0 + b_]
            S.op("act", lambda e: e.activation(out=VN[:], in_=VT[:], func=AF.Square, accum_out=cs(4, 5)), reads=[VTr], writes=[VNr, SR])
            S.op("dve", lambda e: e.tensor_tensor(out=cs(5, 7), in0=cs(0, 2), in1=cs(2, 4), op=ALU.add), reads=[SR], writes=[SR])
            S.op("dve", lambda e: e.tensor_tensor(out=cs(7, 8), in0=cs(5, 6), in1=cs(6, 7), op=ALU.add), reads=[SR], writes=[SR])
            S.op("dve", lambda e: e.tensor_scalar(out=cs(8, 9), in0=cs(7, 8), scalar1=1.0 / 2048, scalar2=None, op0=ALU.mult), reads=[SR], writes=[SR])
            S.op("dve", lambda e: e.tensor_tensor(out=cs(9, 10), in0=cs(8, 9), in1=cs(8, 9), op=ALU.mult), reads=[SR], writes=[SR])
            S.op("dve", lambda e: e.scalar_tensor_tensor(out=cs(10, 11), in0=cs(4, 5), scalar=1.0 / 2048, in1=cs(9, 10), op0=ALU.mult, op1=ALU.subtract), reads=[SR], writes=[SR])
            S.op("act", lambda e: e.activation(out=cs(11, 12), in_=cs(10, 11), func=AF.Sqrt, bias=EPSB[:, 0:1], scale=1.0), reads=[SR, EPSB.r], writes=[SR])
            S.op("dve", lambda e: e.reciprocal(out=cs(12, 13), in_=cs(11, 12)), reads=[SR], writes=[SR])
            S.op("dve", lambda e: e.tensor_scalar(out=VN[:], in0=VT[:], scalar1=cs(8, 9), scalar2=cs(12, 13), op0=ALU.subtract, op1=ALU.mult),
                 reads=[VTr, SR], writes=[VNr])

        def smm(tt):
            tsl = slice(tt * 128, (tt + 1) * 128)
            VN, VNr = VNs[tt % 2], VNrs[tt % 2]
            for b4 in range(4):
                bk = next_bank()
                for g4 in range(4):
                    g = b4 * 4 + g4
                    S.op("pe", lambda e, g=g, g4=g4, bk=bk: e.matmul(bk[:, g4 * 128:(g4 + 1) * 128], lhsT=VN[:, g * 128:(g + 1) * 128], rhs=WST[:, g, :], start=True, stop=True),
                         reads=[VNr, WSTr], writes=[bk.r])
                S.op("dve", lambda e, b4=b4, bk=bk: e.tensor_tensor(out=TMP[:, 0, :], in0=bk[:], in1=STG[:, 2, b4 * 512:(b4 + 1) * 512], op=ALU.add),
                     reads=[bk.r, STGr[2]], writes=[TMPr[0]])
                S.op("dve", lambda e, b4=b4: e.tensor_tensor(out=BIG[:, b4 * 4:(b4 + 1) * 4, tsl], in0=TMP[:, 0, :].rearrange("p (g q) -> p g q", g=4),
                                                      in1=BIG[:, b4 * 4:(b4 + 1) * 4, tsl], op=ALU.mult),
                     reads=[TMPr[0]] + [BIGr[b4 * 4 + k] for k in range(4)], writes=[BIGr[b4 * 4 + k] for k in range(4)])
        vmm_pe(0)
        vmm_evac(0)
        for tt in range(12):
            if tt + 1 < 12:
                vmm_pe(tt + 1)
            chain(tt)
            if tt + 1 < 12:
                vmm_evac(tt + 1)
            smm(tt)
        S.barrier()
        stg_slots[:] = [0, 1, 2]
        upd = x_update(16)
        proj(gm_w_out_d, 2 * D, [[c * 256, c * 256 + 128] for c in range(4)], big_src, big_reads,
             lambda ci, ot, tb, bk: upd(ci * 2 + ot, tb, bk), hint=lambda: load_chunk((("f1", i, 0), 0), ffn_w1_d[i], D, [o * 128 for o in range(4)]))

    def ssm_layer(i, j):
        def consu(ci, ot, tb, bk):
            S.op("act", lambda e: e.activation(out=BIG[:, ci * 4 + ot, tb * 512:(tb + 1) * 512], in_=bk[:], func=AF.Copy),
                 reads=[bk.r], writes=[BIGr[ci * 4 + ot]])
        proj(ssm_w_in_d[j], D, [[c * 512 + o * 128 for o in range(4)] for c in range(2)], ht_src, ht_reads, consu, name=("mix", i))
        oa = [ARENA0]
        ob = [BIG_OFF + 8 * NTOK * 2]

        def sa(name, shape, dt, reg=oa):
            nb = int(np.prod(shape[1:])) * (4 if dt in (F32, I32) else 2)
            nb = (nb + 31) // 32 * 32
            t = nc.alloc_sbuf_tensor_at("%s_%d" % (name, i), list(shape), dt, offset=reg[0])
            reg[0] += nb
            return t

        N4 = NG * NSEG * NSLOT
        Xre, Xim, Gre, Gim, T1, T2 = [sa(n, [128, NG, NSEG, NSLOT], F32) for n in ("Xre", "Xim", "Gre", "Gim", "T1", "T2")]
        PTAB = sa("PTAB", [128, NG, 2, QLEN], F32)
        QTAB = sa("QTAB", [128, NG, 2, QLEN], F32)
        Pre, Pim, Qre, Qim = PTAB[:, :, 0, :], PTAB[:, :, 1, :], QTAB[:, :, 0, :], QTAB[:, :, 1, :]
        COEF = sa("COEF", [128, NG, NSEG, NSLOT], F32)
        Bsre, Bsim = [sa(n, [128, NG, 128], F32) for n in ("Bsre", "Bsim")]
        BCT = sa("BCT", [128, 4, NG, 16], F32)
        bfn = ("Bbre", "Bbim", "Csre", "Csni", "Cfre", "Cfni", "Cbre", "Cbni", "W1fr", "W1fi", "W1br", "W1bi", "W2")
        Bbre, Bbim, Csre, Csni, Cfre, Cfni, Cbre, Cbni, W1fr, W1fi, W1br, W1bi, W2 = [sa(n, [128, NG, 128], BF16, ob) for n in bfn]
        Ub = [sa("U%d" % k, [128, NG, NSEG, NSLOT], BF16) for k in range(2)]
        Hre = [sa("Hre%d" % k, [128, NG, NSEG, NSLOT], BF16, ob) for k in range(2)]
        Him = [sa("Him%d" % k, [128, NG, NSEG, NSLOT], BF16, ob) for k in range(2)]
        Ysb = sa("Ysb", [128, 8, NCOL], BF16)
        assert oa[0] <= BIG_OFF, (oa[0], BIG_OFF)
        pwBr, pwBi, pwCr, pwCi, wpr = [sa(n, [128, 64, 8], F32, ob) for n in ("pwBr", "pwBi", "pwCr", "pwCi", "wpr")]
        WIP = sa("WIP", [128, 64, 8, 2], F32, ob)
        nwpi, wpi = WIP[:, :, :, 0], WIP[:, :, :, 1]
        g64 = {}
        for n in ("LR", "LI", "DTt", "ANG", "LRDT", "Cc", "Sn", "EP", "EN", "SSg", "nur", "nui", "mur", "mui", "abr", "abi", "fr", "fi",
                  "ta", "tb", "tc", "mu8r", "mu8i", "k1r", "k1i", "k3r", "k3i", "rho8", "spr", "spi", "stPr", "stPi", "stQr", "stQi", "h0r", "h0i", "dpp"):
            g64[n] = sa(n, [128, 64], F32, ob)
        angI = sa("angI", [128, 64], I32, ob)
        Er, Ei = [sa(n, [128, 64, NSEG], F32, ob) for n in ("Er", "Ei")]
        STOr, STOi = [sa(n, [128, 64, NSEG], F32, ob) for n in ("STOr", "STOi")]
        NSG = sa("NSG", [128, 1], F32, ob)
        RG = Res("ssmgen%d" % i)
        RW = Res("ssmw%d" % i)
        RU = [Res("ssmU%d_%d" % (i, k)) for k in range(2)]
        RH = [Res("ssmH%d_%d" % (i, k)) for k in range(2)]
        RY = Res("ssmY%d" % i)
        G = g64

        def dv(fn, reads=(), writes=()):
            S.op("dve", fn, reads=[RG] + list(reads), writes=[RG] + list(writes))

        def tt(out, a, b, op, **kw):
            dv(lambda e: e.tensor_tensor(out=out, in0=a, in1=b, op=op), **kw)

        def cmul(o_re, o_im, a_re, a_im, b_re, b_im, t1, t2, **kw):
            tt(t1, a_re, b_re, ALU.mult, **kw)
            tt(t2, a_im, b_im, ALU.mult, **kw)
            dv(lambda e: e.tensor_tensor(out=t2, in0=t1, in1=t2, op=ALU.subtract), **kw)
            tt(t1, a_re, b_im, ALU.mult, **kw)
            dv(lambda e: e.tensor_tensor(out=o_im, in0=a_im, in1=b_re, op=ALU.mult), **kw)
            dv(lambda e: e.tensor_tensor(out=o_im, in0=o_im, in1=t1, op=ALU.add), **kw)
            dv(lambda e: e.tensor_copy(out=o_re, in_=t2), **kw)

        sm = ssm_small_d[j]
        nd = [HT.r, XN.r, RSTD.r] + list(TMPr)
        dma(G["LR"][:], sm[0], reads=nd, writes=[RG]); dma(G["LI"][:], sm[1], writes=[RG]); dma(G["DTt"][:], sm[2], writes=[RG])
        dma(G["h0r"][:], h0_d[j, 0], writes=[RG]); dma(G["h0i"][:], h0_d[j, 1], writes=[RG]); dma(G["dpp"][:], ssm_dpp_d[j], writes=[RG])
        dv(lambda e: e.tensor_scalar(out=NSG[:], in0=SGF[:, 0:1], scalar1=-1.0, scalar2=None, op0=ALU.mult), reads=[SGF.r])
        S.op("act", lambda e: e.activation(out=G["DTt"][:], in_=G["DTt"][:], func=AF.Exp), reads=[RG], writes=[RG])
        tt(G["ANG"][:], G["LI"][:], G["DTt"][:], ALU.mult)
        tt(G["LRDT"][:], G["LR"][:], G["DTt"][:], ALU.mult)
        TWO_PI_ = 2.0 * math.pi

        def sinlike(out, shift):
            dv(lambda e: e.tensor_scalar(out=G["ta"][:], in0=G["ANG"][:], scalar1=1.0 / TWO_PI_, scalar2=shift, op0=ALU.mult, op1=ALU.add))
            dv(lambda e: e.tensor_copy(out=angI[:], in_=G["ta"][:]))
            dv(lambda e: e.tensor_copy(out=G["tb"][:], in_=angI[:]))
            tt(G["ta"][:], G["ta"][:], G["tb"][:], ALU.subtract)
            dv(lambda e: e.tensor_scalar(out=G["tb"][:], in0=G["ta"][:], scalar1=0.0, scalar2=None, op0=ALU.is_lt))
            tt(G["ta"][:], G["ta"][:], G["tb"][:], ALU.add)
            dv(lambda e: e.tensor_scalar(out=G["ta"][:], in0=G["ta"][:], scalar1=TWO_PI_, scalar2=-math.pi, op0=ALU.mult, op1=ALU.add))
            dv(lambda e: e.tensor_scalar(out=G["ta"][:], in0=G["ta"][:], scalar1=-math.pi, scalar2=math.pi, op0=ALU.max, op1=ALU.min))
            S.op("act", lambda e: e.activation(out=out, in_=G["ta"][:], func=AF.Sin), reads=[RG], writes=[RG])
        sinlike(G["Sn"][:], 0.5)
        sinlike(G["Cc"][:], 0.75)
        S.op("act", lambda e: e.activation(out=G["EP"][:], in_=G["LRDT"][:], func=AF.Exp, scale=SGF[:, 0:1]), reads=[RG, SGF.r], writes=[RG])
        S.op("act", lambda e: e.activation(out=G["EN"][:], in_=G["LRDT"][:], func=AF.Exp, scale=NSG[:, 0:1]), reads=[RG], writes=[RG])
        S.op("act", lambda e: e.activation(out=G["ta"][:], in_=G["LRDT"][:], func=AF.Exp), reads=[RG], writes=[RG])
        S.op("act", lambda e: e.activation(out=G["rho8"][:], in_=G["LRDT"][:], func=AF.Exp, scale=8.0), reads=[RG], writes=[RG])
        tt(G["abr"][:], G["ta"][:], G["Cc"][:], ALU.mult)
        tt(G["abi"][:], G["ta"][:], G["Sn"][:], ALU.mult)
        dv(lambda e: e.tensor_scalar(out=G["SSg"][:], in0=G["Sn"][:], scalar1=SGF[:, 0:1], scalar2=None, op0=ALU.mult), reads=[SGF.r])
        tt(G["nur"][:], G["EP"][:], G["Cc"][:], ALU.mult)
        tt(G["nui"][:], G["EP"][:], G["SSg"][:], ALU.mult)
        tt(G["mur"][:], G["EN"][:], G["Cc"][:], ALU.mult)
        tt(G["mui"][:], G["EN"][:], G["SSg"][:], ALU.mult)
        dv(lambda e: e.tensor_scalar(out=G["mui"][:], in0=G["mui"][:], scalar1=-1.0, scalar2=None, op0=ALU.mult))
        dv(lambda e: e.tensor_scalar(out=G["ta"][:], in0=G["abr"][:], scalar1=-1.0, scalar2=None, op0=ALU.add))
        tt(G["tb"][:], G["ta"][:], G["LR"][:], ALU.mult)
        tt(G["tc"][:], G["abi"][:], G["LI"][:], ALU.mult)
        tt(G["fr"][:], G["tb"][:], G["tc"][:], ALU.add)
        tt(G["tb"][:], G["abi"][:], G["LR"][:], ALU.mult)
        tt(G["tc"][:], G["ta"][:], G["LI"][:], ALU.mult)
        tt(G["fi"][:], G["tb"][:], G["tc"][:], ALU.subtract)
        tt(G["tb"][:], G["LR"][:], G["LR"][:], ALU.mult)
        tt(G["tc"][:], G["LI"][:], G["LI"][:], ALU.mult)
        tt(G["tb"][:], G["tb"][:], G["tc"][:], ALU.add)
        dv(lambda e: e.reciprocal(out=G["tb"][:], in_=G["tb"][:]))
        tt(G["fr"][:], G["fr"][:], G["tb"][:], ALU.mult)
        tt(G["fi"][:], G["fi"][:], G["tb"][:], ALU.mult)
        dv(lambda e: e.memset(pwCr[:, :, 0:1], 1.0)); dv(lambda e: e.memset(pwCi[:, :, 0:1], 0.0))
        dv(lambda e: e.memset(pwBr[:, :, 0:1], 1.0)); dv(lambda e: e.memset(pwBi[:, :, 0:1], 0.0))
        for t in range(1, 8):
            cmul(pwCr[:, :, t], pwCi[:, :, t], pwCr[:, :, t - 1], pwCi[:, :, t - 1], G["nur"][:], G["nui"][:], G["tb"][:], G["tc"][:])
            cmul(pwBr[:, :, t], pwBi[:, :, t], pwBr[:, :, t - 1], pwBi[:, :, t - 1], G["mur"][:], G["mui"][:], G["tb"][:], G["tc"][:])
        cmul(G["mu8r"][:], G["mu8i"][:], pwBr[:, :, 7], pwBi[:, :, 7], G["mur"][:], G["mui"][:], G["tb"][:], G["tc"][:])
        dv(lambda e: e.memset(G["k1r"][:], 1.0)); dv(lambda e: e.memset(G["k1i"][:], 0.0))
        dv(lambda e: e.tensor_copy(out=G["k1r"][0:64, :], in_=pwCr[0:64, :, 7])); dv(lambda e: e.tensor_copy(out=G["k1i"][0:64, :], in_=pwCi[0:64, :, 7]))
        dv(lambda e: e.tensor_copy(out=G["k3r"][0:64, :], in_=G["nur"][0:64, :])); dv(lambda e: e.tensor_copy(out=G["k3i"][0:64, :], in_=G["nui"][0:64, :]))
        dv(lambda e: e.tensor_copy(out=G["k3r"][64:128, :], in_=G["mu8r"][64:128, :])); dv(lambda e: e.tensor_copy(out=G["k3i"][64:128, :], in_=G["mu8i"][64:128, :]))
        for hh in range(2):
            t1v = Er[:].rearrange("p g s -> p (g s)")[:, 0:256].rearrange("p (g t) -> p g t", g=64)
            t2v = Ei[:].rearrange("p g s -> p (g s)")[:, 0:256].rearrange("p (g t) -> p g t", g=64)
            frb = G["fr"][:].unsqueeze(2).to_broadcast([128, 64, 4]); fib = G["fi"][:].unsqueeze(2).to_broadcast([128, 64, 4])
            sl_ = slice(4 * hh, 4 * hh + 4)
            cmul(pwBr[:, :, sl_], pwBi[:, :, sl_], pwBr[:, :, sl_], pwBi[:, :, sl_], frb, fib, t1v, t2v)
        dv(lambda e: e.tensor_copy(out=wpr[:, :, 0], in_=G["Cc"][:])); dv(lambda e: e.tensor_copy(out=wpi[:, :, 0], in_=G["Sn"][:]))
        for _ in range(3):
            cmul(wpr[:, :, 0], wpi[:, :, 0], wpr[:, :, 0], wpi[:, :, 0], wpr[:, :, 0], wpi[:, :, 0], G["tb"][:], G["tc"][:])
        dv(lambda e: e.tensor_scalar(out=wpi[:, :, 0], in0=wpi[:, :, 0], scalar1=NSG[:, 0:1], scalar2=None, op0=ALU.mult))
        for k in range(1, 8):
            cmul(wpr[:, :, k], wpi[:, :, k], wpr[:, :, k - 1], wpi[:, :, k - 1], wpr[:, :, k - 1], wpi[:, :, k - 1], G["tb"][:], G["tc"][:])
        dv(lambda e: e.tensor_scalar(out=nwpi, in0=wpi, scalar1=-1.0, scalar2=None, op0=ALU.mult))
        dv(lambda e: e.tensor_copy(out=G["spr"][0:64, :], in_=wpr[0:64, :, 0])); dv(lambda e: e.tensor_copy(out=G["spi"][0:64, :], in_=nwpi[0:64, :, 0]))
        dv(lambda e: e.tensor_copy(out=G["spr"][64:128, :], in_=wpr[64:128, :, 7])); dv(lambda e: e.tensor_copy(out=G["spi"][64:128, :], in_=nwpi[64:128, :, 7]))
        cmul(G["stPr"][:], G["stPi"][:], G["k1r"][:], G["k1i"][:], G["spr"][:], G["spi"][:], G["tb"][:], G["tc"][:])
        dv(lambda e: e.tensor_scalar(out=G["ta"][:], in0=G["spi"][:], scalar1=-1.0, scalar2=None, op0=ALU.mult))
        cmul(G["stQr"][:], G["stQi"][:], G["k3r"][:], G["k3i"][:], G["spr"][:], G["ta"][:], G["tb"][:], G["tc"][:])
        dv(lambda e: e.tensor_copy(out=Er[0:64, :, 0], in_=wpr[0:64, :, 5])); dv(lambda e: e.tensor_copy(out=Ei[0:64, :, 0], in_=nwpi[0:64, :, 5]))
        dv(lambda e: e.tensor_copy(out=Er[64:128, :, 0], in_=wpr[64:128, :, 7])); dv(lambda e: e.tensor_copy(out=Ei[64:128, :, 0], in_=wpi[64:128, :, 7]))
        for sg_ in range(1, NSEG):
            cmul(Er[:, :, sg_], Ei[:, :, sg_], Er[:, :, sg_ - 1], Ei[:, :, sg_ - 1], wpr[:, :, 5], nwpi[:, :, 5], G["tb"][:], G["tc"][:])
        S.barrier()

        flat = lambda t_: t_[:].rearrange("p a b c -> p (a b c)")
        PT1 = sa("PT1", [128, NG, 2, 64], F32)
        PT2 = sa("PT2", [128, NG, 2, 64], F32, ob)
        assert oa[0] <= BIG_OFF, (oa[0], BIG_OFF)
        assert ob[0] <= top - 256, (ob[0], top)
        R_T1, R_T2, R_X, R_Gs, R_P, R_Q, R_CO = [Res("ssm_%s_%d" % (n, i)) for n in ("T1", "T2", "X", "Gs", "P", "Q", "CO")]
        R_Bs, R_Cs, R_Bb, R_W1, R_W2, R_STO, R_BCT, R_PT = [Res("ssm_%s_%d" % (n, i)) for n in ("Bs", "Cs", "Bb", "W1", "W2", "STO", "BCT", "PT")]

        for tl in (Cfre, Cfni, Cbre, Cbni):
            S.op("pool", lambda e, tl=tl: e.memset(tl[:], 0.0), writes=[R_Cs])
        for tl in (W1fr, W1fi, W1br, W1bi):
            S.op("pool", lambda e, tl=tl: e.memset(tl[:], 0.0), writes=[R_W1])

        def Dv(fn, r=(), w=()):
            S.op("dve", fn, reads=list(r), writes=list(w))

        def Pl(fn, r=(), w=()):
            S.op("pool", fn, reads=list(r), writes=list(w))

        def TT(eng, out, a_, b_, op, r=(), w=()):
            S.op(eng, lambda e: e.tensor_tensor(out=out, in0=a_, in1=b_, op=op), reads=list(r), writes=list(w))

        def gen_table(blk, which, part="all"):
            g0 = blk * NG
            gs = slice(g0, g0 + NG)
            (TB, s_r, s_i, Rt) = ((PTAB, G["stPr"], G["stPi"], R_P), (QTAB, G["stQr"], G["stQi"], R_Q))[which]
            if part in ("all", "lo"):
                Pl(lambda e: e.tensor_copy(out=TB[:, :, 0, 0], in_=s_r[:, gs]), r=[RG], w=[Rt])
                Pl(lambda e: e.tensor_copy(out=TB[:, :, 1, 0], in_=s_i[:, gs]), r=[RG], w=[Rt])
            L = 1
            k = 0
            while L < QLEN:
                ntot = min(L, QLEN - L)
                on_dve = (L >= 64) and part != "all"
                do = (part == "all") or (part == "hi" and on_dve) or (part == "lo" and not on_dve)
                o0 = 0
                while do and o0 < ntot:
                    n_ = min(64, ntot - o0)
                    wr_b = wpr[:, gs, k:k + 1].unsqueeze(3).to_broadcast([128, NG, 2, n_])
                    wsel = WIP[:, gs, k, :] if which == 0 else WIP[:, gs, k, ::-1]
                    wi_b = wsel.unsqueeze(3).to_broadcast([128, NG, 2, n_])
                    a_ = TB[:, :, :, o0:o0 + n_]
                    a_sw = TB[:, :, ::-1, o0:o0 + n_]
                    o_ = TB[:, :, :, L + o0:L + o0 + n_]
                    if on_dve:
                        x1 = flat(T1)[:, 0:NG * 2 * n_].rearrange("p (g c m) -> p g c m", g=NG, c=2)
                        x2 = flat(T2)[:, 0:NG * 2 * n_].rearrange("p (g c m) -> p g c m", g=NG, c=2)
                        TT("dve", x1, a_, wr_b, ALU.mult, r=[Rt, RG], w=[R_T1])
                        TT("dve", x2, a_sw, wi_b, ALU.mult, r=[Rt, RG], w=[R_T2])
                        TT("dve", o_, x1, x2, ALU.add, r=[R_T1, R_T2], w=[Rt])
                    else:
                        x1, x2 = PT1[:, :, :, 0:n_], PT2[:, :, :, 0:n_]
                        TT("pool", x1, a_, wr_b, ALU.mult, r=[Rt, RG], w=[R_PT["1"]])
                        TT("pool", x2, a_sw, wi_b, ALU.mult, r=[Rt, RG], w=[R_PT["2"]])
                        TT("pool", o_, x1, x2, ALU.add, r=[R_PT], w=[Rt])
                    o0 += n_
                L += ntot
                k += 1

        def gen_coef(blk):
            g0 = blk * NG
            gs = slice(g0, g0 + NG)
            rb = G["rho8"][:, gs].unsqueeze(2).unsqueeze(3).to_broadcast([128, NG, NSEG, NSLOT])

            def Ac(out, in_, **kw):
                S.op("act", lambda e: e.activation(out=out, in_=in_, func=AF.Copy, **kw), reads=[RG, LINK.r], writes=[R_CO])
            Ac(COEF[:], rb)
            Ac(COEF[0:64, :, :, 0:1], COEF[0:64, :, :, 2:3], scale=0.0, bias=1.0)
            Ac(COEF[64:128, :, :, NSLOT - 1:NSLOT], COEF[64:128, :, :, 2:3], scale=0.0, bias=1.0)
            Ac(COEF[0:64, :, 0:1, 0:1], COEF[0:64, :, 0:1, 2:3], scale=0.0)
            Ac(COEF[64:128, :, NSEG - 1:NSEG, NSLOT - 1:NSLOT], COEF[64:128, :, NSEG - 1:NSEG, 2:3], scale=0.0)
            lf = LINK[0:64, 0:NSEG].unsqueeze(1).unsqueeze(3).to_broadcast([64, NG, NSEG, 1])
            lb = LINK[64:128, NSEG:2 * NSEG].unsqueeze(1).unsqueeze(3).to_broadcast([64, NG, NSEG, 1])
            Ac(COEF[0:64, :, :, 1:2], lf)
            Ac(COEF[64:128, :, :, NSLOT - 2:NSLOT - 1], lb)

        def gen_Bs(blk):
            g0 = blk * NG
            gs = slice(g0, g0 + NG)
            dma(BCT[:], ssm_bc_d[j].rearrange("k p g q -> p k g q")[:, :, gs, :], writes=[R_BCT])

            def outer(o_re, o_im, pr, pi, vr, vi, neg_im, Rout):
                prb = pr[:, gs, :].unsqueeze(3).to_broadcast([128, NG, 8, 16]); pib = pi[:, gs, :].unsqueeze(3).to_broadcast([128, NG, 8, 16])
                vrb = vr.unsqueeze(2).to_broadcast([128, NG, 8, 16]); vib = vi.unsqueeze(2).to_broadcast([128, NG, 8, 16])
                t1_ = flat(T1)[:, 0:512].rearrange("p (g t q) -> p g t q", g=NG, t=8)
                t2_ = flat(T2)[:, 0:512].rearrange("p (g t q) -> p g t q", g=NG, t=8)
                ore = o_re[:].rearrange("p g (t q) -> p g t q", t=8); oim = o_im[:].rearrange("p g (t q) -> p g t q", t=8)
                rin = [RG, R_BCT]
                TT("dve", t1_, prb, vrb, ALU.mult, r=rin, w=[R_T1]); TT("dve", t2_, pib, vib, ALU.mult, r=rin, w=[R_T2])
                TT("dve", ore, t1_, t2_, ALU.subtract, r=[R_T1, R_T2], w=[Rout])
                TT("dve", t1_, prb, vib, ALU.mult, r=rin, w=[R_T1]); TT("dve", t2_, pib, vrb, ALU.mult, r=rin, w=[R_T2])
                if neg_im:
                    Dv(lambda e: e.scalar_tensor_tensor(out=oim, in0=t1_, scalar=-1.0, in1=t2_, op0=ALU.mult, op1=ALU.subtract), r=[R_T1, R_T2], w=[Rout])
                else:
                    TT("dve", oim, t1_, t2_, ALU.add, r=[R_T1, R_T2], w=[Rout])
            outer(Bsre, Bsim, pwBr, pwBi, BCT[:, 0], BCT[:, 1], False, R_Bs)
            for (src, df, db) in ((Bsre, W1fr, W1br), (Bsim, W1fi, W1bi)):
                bk = next_bank()
                for g in range(NG):
                    S.op("pe", lambda e, src=src, g=g, bk=bk: e.transpose(bk[:, g * 128:(g + 1) * 128], src[:, g, :], IDENT[:]), reads=[R_Bs, IDENT.r], writes=[bk.r])
                bv = bk[:].rearrange("p (g m) -> p g m", g=NG)
                S.op("act", lambda e, bv=bv, df=df: e.activation(out=df[:, :, 0:64], in_=bv[:, :, 0:64], func=AF.Copy), reads=[bk.r], writes=[R_W1])
                S.op("act", lambda e, bv=bv, db=db: e.activation(out=db[:, :, 64:128], in_=bv[:, :, 64:128], func=AF.Copy), reads=[bk.r], writes=[R_W1])

        def gen_Cs(blk):
            g0 = blk * NG
            gs = slice(g0, g0 + NG)

            def outer(o_re, o_im, pr, pi, vr, vi, neg_im, Rout):
                prb = pr[:, gs, :].unsqueeze(3).to_broadcast([128, NG, 8, 16]); pib = pi[:, gs, :].unsqueeze(3).to_broadcast([128, NG, 8, 16])
                vrb = vr.unsqueeze(2).to_broadcast([128, NG, 8, 16]); vib = vi.unsqueeze(2).to_broadcast([128, NG, 8, 16])
                t1_ = flat(T1)[:, 0:512].rearrange("p (g t q) -> p g t q", g=NG, t=8)
                t2_ = flat(T2)[:, 0:512].rearrange("p (g t q) -> p g t q", g=NG, t=8)
                ore = o_re[:].rearrange("p g (t q) -> p g t q", t=8); oim = o_im[:].rearrange("p g (t q) -> p g t q", t=8)
                rin = [RG, R_BCT]
                TT("dve", t1_, prb, vrb, ALU.mult, r=rin, w=[R_T1]); TT("dve", t2_, pib, vib, ALU.mult, r=rin, w=[R_T2])
                TT("dve", ore, t1_, t2_, ALU.subtract, r=[R_T1, R_T2], w=[Rout])
                TT("dve", t1_, prb, vib, ALU.mult, r=rin, w=[R_T1]); TT("dve", t2_, pib, vrb, ALU.mult, r=rin, w=[R_T2])
                if neg_im:
                    Dv(lambda e: e.scalar_tensor_tensor(out=oim, in0=t1_, scalar=-1.0, in1=t2_, op0=ALU.mult, op1=ALU.subtract), r=[R_T1, R_T2], w=[Rout])
                else:
                    TT("dve", oim, t1_, t2_, ALU.add, r=[R_T1, R_T2], w=[Rout])
            outer(Csre, Csni, pwCr, pwCi, BCT[:, 2], BCT[:, 3], True, R_Cs)
            S.op("act", lambda e: e.activation(out=Bbre[:], in_=Bsre[:], func=AF.Copy), reads=[R_Bs], writes=[R_Bb])
            S.op("act", lambda e: e.activation(out=Bbim[:], in_=Bsim[:], func=AF.Copy), reads=[R_Bs], writes=[R_Bb])
            for (dst, src, lo, hi) in ((Cfre, Csre, 0, 64), (Cfni, Csni, 0, 64), (Cbre, Csre, 64, 128), (Cbni, Csni, 64, 128)):
                S.op("act", lambda e, dst=dst, src=src, lo=lo, hi=hi: e.activation(out=dst[lo:hi], in_=src[lo:hi], func=AF.Copy), reads=[R_Cs], writes=[R_Cs["m"]])
            bkf, bkb = next_bank(), next_bank()
            for g in range(NG):
                for (bk_, cr, cn) in ((bkf, Cfre, Cfni), (bkb, Cbre, Cbni)):
                    S.op("pe", lambda e, g=g, bk_=bk_, cr=cr: e.matmul(bk_[:, g * 128:(g + 1) * 128], lhsT=Bbre[:, g, :], rhs=cr[:, g, :], start=True, stop=False), reads=[R_Bb, R_Cs], writes=[bk_.r])
                    S.op("pe", lambda e, g=g, bk_=bk_, cn=cn: e.matmul(bk_[:, g * 128:(g + 1) * 128], lhsT=Bbim[:, g, :], rhs=cn[:, g, :], start=False, stop=True), reads=[R_Bb, R_Cs], writes=[bk_.r])
            mLb = MASKL[:].unsqueeze(1).to_broadcast([128, NG, 128]); mUb = MASKU[:].unsqueeze(1).to_broadcast([128, NG, 128])
            t1w = flat(T1)[:, 0:512].rearrange("p (g m) -> p g m", g=NG); t2w = flat(T2)[:, 0:512].rearrange("p (g m) -> p g m", g=NG)
            TT("dve", t1w, bkf[:].rearrange("p (g m) -> p g m", g=NG), mLb, ALU.mult, r=[bkf.r, MASKL.r], w=[R_T1])
            TT("dve", t2w, bkb[:].rearrange("p (g m) -> p g m", g=NG), mUb, ALU.mult, r=[bkb.r, MASKU.r], w=[R_T2])
            TT("dve", t1w, t1w, t2w, ALU.add, r=[R_T1, R_T2], w=[R_T1])
            for g in range(NG):
                Dv(lambda e, g=g: e.scalar_tensor_tensor(out=W2[:, g, :], in0=IDENT[:], scalar=G["dpp"][:, g0 + g:g0 + g + 1], in1=t1w[:, g, :], op0=ALU.mult, op1=ALU.add),
                   r=[IDENT.r, RG, R_T1], w=[R_W2[g]])

        SBK = {}
        ssm_rr = [0]

        def next_bank():
            b_ = PS[4 + ssm_rr[0] % 4]
            ssm_rr[0] += 1
            return b_

        def usel(blk):
            g0 = blk * NG
            ct = blk // 2
            par = blk % 2
            gs = slice(g0, g0 + NG)
            U = Ub[par]
            Hr, Hi = Hre[par], Him[par]
            ubanks = [next_bank(), next_bank()]
            uv = BIG[:, ct, :].rearrange("p (c t) -> p c t", t=TCH)
            for g in range(NG):
                gl = (g0 + g) % 8
                bk = ubanks[g // 2]
                for t_ in range(8):
                    S.op("pe", lambda e, g=g, gl=gl, t_=t_, bk=bk: e.matmul(bk[:, (g % 2) * NCOL:(g % 2 + 1) * NCOL], lhsT=BAND[:, gl, 112 - 16 * t_:240 - 16 * t_],
                                                                            rhs=uv[:, :, t_], start=(t_ == 0), stop=(t_ == 7)),
                         reads=[BAND.r, BIGr[ct]], writes=[bk.r])
            for h in range(2):
                S.op("act", lambda e, h=h: e.activation(out=U[:, 2 * h:2 * h + 2, :, 0:NCH], in_=ubanks[h][:, 0:2 * NCOL].rearrange("p (g s k) -> p g s k", g=2, s=NSEG), func=AF.Copy),
                     reads=[ubanks[h].r], writes=[RU[par]])

        def sprime(blk):
            g0 = blk * NG
            ct = blk // 2
            par = blk % 2
            gs = slice(g0, g0 + NG)
            U = Ub[par]
            Hr, Hi = Hre[par], Him[par]
            sb_re = [PS[0], PS[1]]
            sb_im = [PS[2], PS[3]]
            for g in range(NG):
                for (bks, wf, wb_) in ((sb_re, W1fr, W1br), (sb_im, W1fi, W1bi)):
                    bk = bks[g // 2]
                    ov = bk[:, (g % 2) * 204:(g % 2 + 1) * 204].rearrange("p (s k) -> p s k", s=NSEG)
                    uin = U[:, g, :, 0:NCH]
                    for s_ in range(NSEG):
                        S.op("pe", lambda e, g=g, ov=ov, uin=uin, wf=wf, s_=s_: e.matmul(ov[:, s_, 2:34], lhsT=wf[:, g, :], rhs=uin[:, s_, :], start=True, stop=False, skip_group_check=True), reads=[R_W1, RU[par]], writes=[bk.r])
                        S.op("pe", lambda e, g=g, ov=ov, uin=uin, wb_=wb_, s_=s_: e.matmul(ov[:, s_, 0:32], lhsT=wb_[:, g, :], rhs=uin[:, s_, :], start=False, stop=True, skip_group_check=True), reads=[R_W1, RU[par]], writes=[bk.r])
            SBK[blk] = (sb_re, sb_im)

        def core(blk):
            g0 = blk * NG
            ct = blk // 2
            par = blk % 2
            gs = slice(g0, g0 + NG)
            U = Ub[par]
            Hr, Hi = Hre[par], Him[par]
            sb_re, sb_im = SBK.pop(blk)
            for h in range(2):
                sre = sb_re[h][:, 0:408].rearrange("p (g s k) -> p g s k", g=2, s=NSEG)
                sim_ = sb_im[h][:, 0:408].rearrange("p (g s k) -> p g s k", g=2, s=NSEG)

                def win(Tt):
                    base_ap = Tt[:, 2 * h:2 * h + 2, 0:NSLOT]
                    return bass.AP(tensor=base_ap.tensor, offset=base_ap.offset, ap=[list(base_ap.ap[0]), list(base_ap.ap[1]), [NCH, NSEG], [1, NSLOT]])
                pr_, pi_ = win(Pre), win(Pim)
                xr, xi = Xre[:, 2 * h:2 * h + 2], Xim[:, 2 * h:2 * h + 2]
                a1, a2 = T1[:, 2 * h:2 * h + 2], T2[:, 2 * h:2 * h + 2]
                rdb = [sb_re[h].r, sb_im[h].r, R_P]
                TT("dve", a1, sre, pr_, ALU.mult, r=rdb, w=[R_T1[h]]); TT("dve", a2, sim_, pi_, ALU.mult, r=rdb, w=[R_T2[h]])
                TT("dve", xr, a1, a2, ALU.subtract, r=[R_T1[h], R_T2[h]], w=[R_X["r%d" % h]])
                TT("dve", a1, sim_, pr_, ALU.mult, r=rdb, w=[R_T1[h]]); TT("dve", a2, sre, pi_, ALU.mult, r=rdb, w=[R_T2[h]])
                TT("dve", xi, a1, a2, ALU.add, r=[R_T1[h], R_T2[h]], w=[R_X["i%d" % h]])
            if blk + 1 < 16:
                gen_table(blk + 1, 0)
            Dv(lambda e: e.tensor_copy(out=Xre[0:64, :, 0, 1], in_=G["h0r"][0:64, gs]), r=[RG, R_X], w=[R_X])
            Dv(lambda e: e.tensor_copy(out=Xim[0:64, :, 0, 1], in_=G["h0i"][0:64, gs]), r=[RG], w=[R_X["hi0"]])
            Dv(lambda e: e.tensor_copy(out=Xre[64:128, :, 3, 32], in_=G["h0r"][64:128, gs]), r=[RG], w=[R_X["hr1"]])
            Dv(lambda e: e.tensor_copy(out=Xim[64:128, :, 3, 32], in_=G["h0i"][64:128, gs]), r=[RG], w=[R_X["hi1"]])
            for (Xs, Gs, nm) in ((Xre, Gre, "r"), (Xim, Gim, "i")):
                xf, gf, cf = flat(Xs), flat(Gs), flat(COEF)
                Dv(lambda e, xf=xf, gf=gf, cf=cf: e.tensor_tensor_scan(out=gf[0:64, :], data0=cf[0:64, :], data1=xf[0:64, :], initial=0.0, op0=ALU.mult, op1=ALU.add),
                   r=[R_X, R_CO], w=[R_Gs[nm + "f"]])
                Dv(lambda e, xf=xf, gf=gf, cf=cf: e.tensor_tensor_scan(out=gf[64:128, ::-1], data0=cf[64:128, ::-1], data1=xf[64:128, ::-1], initial=0.0, op0=ALU.mult, op1=ALU.add),
                   r=[R_X, R_CO], w=[R_Gs[nm + "b"]])
            if blk + 1 < 16:
                gen_coef(blk + 1)

            def win4(Tt):
                base_ap = Tt[:, :, 0:NSLOT]
                return bass.AP(tensor=base_ap.tensor, offset=base_ap.offset, ap=[list(base_ap.ap[0]), list(base_ap.ap[1]), [NCH, NSEG], [1, NSLOT]])
            qr_, qi_ = win4(Qre), win4(Qim)
            TT("dve", T1[:], Gre[:], qr_, ALU.mult, r=[R_Gs, R_Q], w=[R_T1]); TT("dve", T2[:], Gim[:], qi_, ALU.mult, r=[R_Gs, R_Q], w=[R_T2])
            TT("dve", Hr[:], T1[:], T2[:], ALU.subtract, r=[R_T1, R_T2], w=[RH[par]["r"]])
            TT("dve", T1[:], Gim[:], qr_, ALU.mult, r=[R_Gs, R_Q], w=[R_T1]); TT("dve", T2[:], Gre[:], qi_, ALU.mult, r=[R_Gs, R_Q], w=[R_T2])
            TT("dve", Hi[:], T1[:], T2[:], ALU.add, r=[R_T1, R_T2], w=[RH[par]["i"]])
            if blk + 1 < 16:
                gen_table(blk + 1, 1)
            for (lo, hi, sl_) in ((0, 64, NSLOT - 1), (64, 128, 0)):
                S.op("act", lambda e, lo=lo, hi=hi, sl_=sl_: e.activation(out=STOr[lo:hi, gs, :], in_=Gre[lo:hi, :, :, sl_], func=AF.Copy),
                     reads=[R_Gs], writes=[R_STO[(blk, lo, 0)]])
                S.op("act", lambda e, lo=lo, hi=hi, sl_=sl_: e.activation(out=STOi[lo:hi, gs, :], in_=Gim[lo:hi, :, :, sl_], func=AF.Copy),
                     reads=[R_Gs], writes=[R_STO[(blk, lo, 1)]])

        def ymm(blk):
            g0 = blk * NG
            ct = blk // 2
            par = blk % 2
            gs = slice(g0, g0 + NG)
            U = Ub[par]
            Hr, Hi = Hre[par], Him[par]
            ybanks = [next_bank(), next_bank()]
            for g in range(NG):
                bk = ybanks[g // 2]
                ov = bk[:, (g % 2) * 204:(g % 2 + 1) * 204].rearrange("p (s k) -> p s k", s=NSEG)[:, :, 0:NCH]
                for s_ in range(NSEG):
                    S.op("pe", lambda e, g=g, ov=ov, s_=s_: e.matmul(ov[:, s_, :], lhsT=W2[:, g, :], rhs=U[:, g, s_, 0:NCH], start=True, stop=False), reads=[R_W2, RU[par]], writes=[bk.r])
                    S.op("pe", lambda e, g=g, ov=ov, s_=s_: e.matmul(ov[:, s_, :], lhsT=Csre[:, g, :], rhs=Hr[:, g, s_, 1:33], start=False, stop=False), reads=[R_Cs, RH[par]], writes=[bk.r])
                    S.op("pe", lambda e, g=g, ov=ov, s_=s_: e.matmul(ov[:, s_, :], lhsT=Csni[:, g, :], rhs=Hi[:, g, s_, 1:33], start=False, stop=True), reads=[R_Cs, RH[par]], writes=[bk.r])
            for h in range(2):
                S.op("act", lambda e, h=h: e.activation(out=Ysb[:, par * NG + 2 * h:par * NG + 2 * h + 2, :].rearrange("p g (s k) -> p g s k", s=NSEG), in_=ybanks[h][:, 0:408].rearrange("p (g s k) -> p g s k", g=2, s=NSEG)[:, :, :, 0:NCH], func=AF.Copy),
                     reads=[ybanks[h].r], writes=[RY[(par, h)]])

        def selback(blk):
            g0 = blk * NG
            ct = blk // 2
            par = blk % 2
            gs = slice(g0, g0 + NG)
            U = Ub[par]
            Hr, Hi = Hre[par], Him[par]
            if par == 1:
                zv = HT[:, ct, :].rearrange("p (c t) -> p c t", t=TCH)
                for tp in range(4):
                    bk = next_bank()
                    for t2_ in range(2):
                        t_ = tp * 2 + t2_
                        for gl in range(8):
                            S.op("pe", lambda e, t_=t_, t2_=t2_, gl=gl, bk=bk: e.matmul(bk[:, t2_ * NCOL:(t2_ + 1) * NCOL], lhsT=BAND[:, t_, 112 - 16 * gl:240 - 16 * gl],
                                                                                        rhs=Ysb[:, gl, :], start=(gl == 0), stop=(gl == 7)),
                                 reads=[BAND.r, RY], writes=[bk.r])
                        S.op("act", lambda e, t_=t_, t2_=t2_, bk=bk: e.activation(out=zv[:, :, t_], in_=bk[:, t2_ * NCOL:(t2_ + 1) * NCOL], func=AF.Gelu_apprx_tanh),
                             reads=[bk.r], writes=[HT.r[ct]])
        S.mark("L%d ssmcore" % i)
        gen_table(0, 0)
        gen_coef(0)
        gen_table(0, 1)
        usel(0)
        gen_Bs(0)
        sprime(0)
        gen_Cs(0)
        for blk_ in range(16):
            core(blk_)
            nxt = blk_ + 1 < 16
            if nxt:
                usel(blk_ + 1)
            ymm(blk_)
            if nxt:
                gen_Bs(blk_ + 1)
                sprime(blk_ + 1)
            selback(blk_)
            if nxt:
                gen_Cs(blk_ + 1)
        sv = lambda tl: flat(tl)[:, 0:64 * NSEG].rearrange("p (g s) -> p g s", g=64)
        x1_, x2_ = sv(T1), sv(T2)
        y1_ = flat(Xre)[:, 0:64 * NSEG].rearrange("p (g s) -> p g s", g=64)
        TT("dve", x1_, STOr[:], Er[:], ALU.mult, r=[R_STO, RG], w=[R_T1]); TT("dve", x2_, STOi[:], Ei[:], ALU.mult, r=[R_STO, RG], w=[R_T2])
        TT("dve", y1_, x1_, x2_, ALU.subtract, r=[R_T1, R_T2], w=[R_X])
        TT("dve", x1_, STOr[:], Ei[:], ALU.mult, r=[R_STO, RG], w=[R_T1]); TT("dve", x2_, STOi[:], Er[:], ALU.mult, r=[R_STO, RG], w=[R_T2])
        TT("dve", STOi[:], x1_, x2_, ALU.add, r=[R_T1, R_T2], w=[R_STO])
        Dv(lambda e: e.tensor_copy(out=STOr[:], in_=y1_), r=[R_X], w=[R_STO])
        dma(st_d[j, 0], STOr[:].rearrange("p g s -> p (g s)"), reads=[R_STO])
        dma(st_d[j, 1], STOi[:].rearrange("p g s -> p (g s)"), reads=[R_STO])
        S.barrier()
        S.mark("L%d ssmout" % i)
        def conso(ci, ot, tb, bk):
            dt = ci * 2 + ot // 2
            if ot % 2 == 0:
                pend[tb] = bk
            else:
                bka = pend.pop(tb)
                Mt = MS[0]
                S.op("act", lambda e: e.activation(out=TMP[:, tb, :], in_=bk[:], func=AF.Sigmoid), reads=[bk.r], writes=[TMPr[tb]])
                S.op("dve", lambda e: e.tensor_tensor(out=TMP[:, tb, :], in0=bka[:], in1=TMP[:, tb, :], op=ALU.mult), reads=[bka.r, TMPr[tb]], writes=[TMPr[tb]])
                for h in range(2):
                    sg_ = tb * 2 + h
                    S.op("dve", lambda e, h=h, sg_=sg_: e.scalar_tensor_tensor(
                        out=XT[:, dt, sg_ * SEGL:(sg_ + 1) * SEGL], in0=TMP[:, tb, h * SEGL:(h + 1) * SEGL], scalar=Mt[:, 16 + dt, sg_:sg_ + 1],
                        in1=XT[:, dt, sg_ * SEGL:(sg_ + 1) * SEGL], op0=ALU.mult, op1=ALU.add),
                        reads=[TMPr[tb], Mt.r, XT.r[dt]], writes=[XT.r[dt]])
        pend = {}
        proj(ssm_w_out_d[j], D, [[c * 256, D + c * 256, c * 256 + 128, D + c * 256 + 128] for c in range(4)], ht_src, ht_reads, conso, hint=lambda: load_chunk((("f1", i, 0), 0), ffn_w1_d[i], D, [o * 128 for o in range(4)]))

    for li, i in enumerate(layers):
        kind, j = i % 3, i // 3
        Mc = MODSL[li % 2]
        MS[0] = Mc
        if li == 0:
            adaln_load(i, 0)
            for cb in range(12):
                if cb + 1 < 12:
                    adaln_load(i, cb + 1)
                adaln_block(i, cb, Mc)
        for (At, Gt, o0) in ((A1, GMIX, 8), (A2, GFFN, 32)):
            for dt in range(8):
                S.op("dve", lambda e, At=At, Gt=Gt, o0=o0, dt=dt, i=i, Mc=Mc: e.tensor_scalar(
                    out=At[:, dt, :], in0=Mc[:, o0 + dt, :], scalar1=1.0, scalar2=Gt[:, i, dt:dt + 1], op0=ALU.add, op1=ALU.mult),
                    reads=[Mc.r, Gt.r], writes=[At.r])
        S.mark("L%d norm1" % i)
        norm_mod(A1, lambda dt, Mc=Mc: Mc[:, 0 + dt, :])
        S.mark("L%d mixer" % i)
        if kind == 2:
            conv_layer(i)
        elif kind == 1:
            gmlp_layer(i)
        else:
            ssm_layer(i, j)
        S.barrier()
        S.mark("L%d norm2" % i)
        norm_mod(A2, lambda dt, Mc=Mc: Mc[:, 24 + dt, :])
        S.mark("L%d ffn" % i)
        upd2 = x_update(40)
        mix_hint = None
        if li + 1 < len(layers):
            ni = layers[li + 1]
            nk, nj = ni % 3, ni // 3
            if nk == 2:
                mix_hint = lambda ni=ni: load_chunk((("mix", ni), 0), cv_w_in_d, D, [0, D, 2 * D])
            elif nk == 1:
                mix_hint = lambda ni=ni: load_chunk((("mix", ni), 0), gm_w_in_d, D, [o * 128 for o in range(4)])
            else:
                mix_hint = lambda ni=ni, nj=nj: load_chunk((("mix", ni), 0), ssm_w_in_d[nj], D, [o * 128 for o in range(4)])
        stg_slots[:] = [0, 1, 2, 3, 4, 5]
        ada_q = [(layers[li + 1], cb, MODSL[(li + 1) % 2]) for cb in range(12)] if li + 1 < len(layers) else []
        ada_ld = []
        ada_tr = []

        def ada_step():
            if ada_tr:
                adaln_tr(*ada_tr.pop(0))
            if ada_ld:
                blk_ = ada_ld.pop(0)
                adaln_mm(*blk_)
                ada_tr.append(blk_)
            if ada_q:
                nx = ada_q.pop(0)
                adaln_load(nx[0], nx[1])
                ada_ld.append(nx)
        for fc in range(8):
            hb = fc % 2

            def cons1(ci, ot, tb, bk, hb=hb):
                tslot = (ot * NTB + tb) % 3
                S.op("act", lambda e: e.activation(out=TMP[:, tslot, :], in_=bk[:], func=AF.Relu), reads=[bk.r], writes=[TMPr[tslot]])
                S.op("act", lambda e: e.activation(out=BIG[:, hb * 4 + ot, tb * 512:(tb + 1) * 512], in_=TMP[:, tslot, :], func=AF.Square),
                     reads=[TMPr[tslot]], writes=[BIGr[hb * 4 + ot]])
            w1cols = lambda f: [f * 512 + o * 128 for o in range(4)]
            w2cols = [o * 128 for o in range(8)]
            w2v = lambda f: ffn_w2_d[i][f * 512:(f + 1) * 512, :]
            proj(ffn_w1_d[i], D, [w1cols(fc)], ht_src, ht_reads, cons1, name=("f1", i, fc),
                 hint=lambda fc=fc: load_chunk((("f2", i, fc), 0), w2v(fc), 512, w2cols))
            ada_step()
            proj(w2v(fc), 512, [w2cols],
                 lambda kt, tb, hb=hb: BIG[:, hb * 4 + kt, tb * 512:(tb + 1) * 512], lambda kt, hb=hb: [BIGr[hb * 4 + kt]],
                 lambda ci, ot, tb, bk: upd2(ot, tb, bk), name=("f2", i, fc),
                 hint=(lambda fc=fc: load_chunk((("f1", i, fc + 1), 0), ffn_w1_d[i], D, w1cols(fc + 1))) if fc < 7 else mix_hint)
            ada_step()
        while ada_ld or ada_q or ada_tr:
            ada_step()
        stg_slots[:] = [0, 1, 2]
        S.barrier()

    S.mark("final")
    if do_final:
        for dt in range(8):
            S.op("act", lambda e, dt=dt: e.activation(out=HT[:, dt, :], in_=XT[:, dt, :], func=AF.Square), reads=[XT.r[dt]], writes=[HT.r[dt]])
        for tb in range(NTB):
            bk = next_bank()
            for dt in range(8):
                S.op("pe", lambda e, dt=dt, tb=tb, bk=bk: e.matmul(bk[:], lhsT=ONES[:], rhs=HT[:, dt, tb * 512:(tb + 1) * 512], start=(dt == 0), stop=(dt == 7)),
                     reads=[ONES.r, HT.r[dt]], writes=[bk.r])
            S.op("act", lambda e, tb=tb, bk=bk: e.activation(out=RSTD[:, tb * 512:(tb + 1) * 512], in_=bk[:], func=AF.Sqrt, bias=EPSB[:, 0:1], scale=1.0),
                 reads=[bk.r, EPSB.r], writes=[RSTD.r[tb]])
            S.op("dve", lambda e, tb=tb: e.reciprocal(out=RSTD[:, tb * 512:(tb + 1) * 512], in_=RSTD[:, tb * 512:(tb + 1) * 512]),
                 reads=[RSTD.r[tb]], writes=[RSTD.r[tb]])
        yT_v = yT_d.rearrange("(kt p) n -> p kt n", p=128)
        for dt in range(8):
            S.op("dve", lambda e, dt=dt: e.scalar_tensor_tensor(out=XT[:, dt, :], in0=XT[:, dt, :], scalar=GFIN[:, dt:dt + 1], in1=RSTD[:], op0=ALU.mult, op1=ALU.mult),
                 reads=[XT.r[dt], RSTD.r, GFIN.r], writes=[XT.r[dt]])
            dma(yT_v[:, dt, :], XT[:, dt, :], reads=[XT.r[dt]])
    S.emit()
    es.close()
    nc._marks = S.marks
    return nc


def _consts():
    c = np.zeros((5, 128, 128), np.float32)
    c[0] = np.eye(128, dtype=np.float32)
    k = np.arange(128)
    tk = k // 16
    c[1] = (tk[None, :] >= tk[:, None]).astype(np.float32)
    c[2] = (tk[:, None] >= tk[None, :]).astype(np.float32)
    c[3, :64, 0] = 1.0
    c[3, 64:, 0] = -1.0
    freq = 1.0 / (10000.0 ** (np.arange(256, dtype=np.float64) / 256.0))
    c[3, :, 1] = (freq[:128] / (2 * np.pi)).astype(np.float32)
    c[3, :, 2] = (freq[128:] / (2 * np.pi)).astype(np.float32)
    band = np.zeros((128, 8, 240), np.float32)
    for a in range(8):
        for kk in range(16 * a, 16 * a + 16):
            band[kk, a, kk - 16 * a + 112] = 1.0
    return c, band.reshape(128, 8 * 240)


def _core_segments(c):
    if c < 4:
        return [("s", c, s) for s in range(4)] + [("p", 2 * c), ("p", 2 * c + 1)]
    return [("p", 8 + 6 * (c - 4) + s) for s in range(6)]


def make_in_maps(inp):
    f = lambda a: np.ascontiguousarray(np.asarray(a, dtype=np.float32))
    consts, band = _consts()
    shared = {
        "w_mod": f(inp["w_mod"]),
        "b_modT": f(np.asarray(inp["b_mod"]).reshape(4, 48, 128).transpose(0, 2, 1)),
        "g_mixT": f(np.asarray(inp["g_mix"]).reshape(4, 8, 128).transpose(0, 2, 1)),
        "g_ffnT": f(np.asarray(inp["g_ffn"]).reshape(4, 8, 128).transpose(0, 2, 1)),
        "g_finT": f(np.asarray(inp["g_final"]).reshape(8, 128).T),
        "ffn_w1": f(inp["ffn_w1"]), "ffn_w2": f(inp["ffn_w2"]),
        "ssm_w_in": f(inp["ssm_w_in"]), "ssm_w_out": f(inp["ssm_w_out"]),
        "gm_w_in": f(np.asarray(inp["gmlp_w_in"])[0]),
        "gm_wsT": f(np.asarray(inp["gmlp_w_s"])[0].transpose(2, 0, 1)),
        "gm_bs": f(np.asarray(inp["gmlp_b_s"])[0].reshape(1, 2048)),
        "gm_w_out": f(np.asarray(inp["gmlp_w_out"])[0]),
        "cv_w_in": f(np.asarray(inp["conv_w_in"])[0]),
        "cv_wT": f(np.asarray(inp["conv_w"])[0].reshape(3, 8, 128).transpose(2, 0, 1)),
        "cv_w_out": f(np.asarray(inp["conv_w_out"])[0]),
        "consts": consts, "band": band,
    }
    lr = np.asarray(inp["ssm_lam_re"]); li = np.asarray(inp["ssm_lam_im"]); ld = np.asarray(inp["ssm_log_dt"])
    small = np.zeros((2, 3, 128, 64), np.float32)
    bc = np.zeros((2, 4, 128, 64, 16), np.float32)
    dpp = np.zeros((2, 128, 64), np.float32)
    for j in range(2):
        small[j, 0] = lr[j].transpose(0, 2, 1).reshape(128, 64)
        small[j, 1] = li[j].transpose(0, 2, 1).reshape(128, 64)
        small[j, 2] = np.broadcast_to(ld[j][:, None, :], (2, 64, 64)).reshape(128, 64)
        bc[j, 0] = np.asarray(inp["ssm_b_re"])[j].transpose(0, 2, 1, 3).reshape(128, 64, 16)
        bc[j, 1] = np.asarray(inp["ssm_b_im"])[j].transpose(0, 2, 1, 3).reshape(128, 64, 16)
        bc[j, 2] = np.asarray(inp["ssm_c_re"])[j].transpose(0, 3, 1, 2).reshape(128, 64, 16)
        bc[j, 3] = np.asarray(inp["ssm_c_im"])[j].transpose(0, 3, 1, 2).reshape(128, 64, 16)
        dpp[j] = np.tile(np.asarray(inp["ssm_d"])[j].reshape(64, 16).T, (8, 1))
    shared["ssm_small"] = small; shared["ssm_bc"] = bc; shared["ssm_dpp"] = dpp
    xp = np.asarray(inp["x_prompt"]); xs = np.asarray(inp["x_sample"])
    cc = np.asarray(inp["c"]); cctx = np.asarray(inp["c_ctx"])
    sre = np.asarray(inp["state_ssm_re"]); sim = np.asarray(inp["state_ssm_im"])
    maps = []
    for c in range(8):
        segs = _core_segments(c)
        x = np.concatenate([xs[s[1], s[2] * 256:(s[2] + 1) * 256] if s[0] == "s" else xp[s[1]] for s in segs], axis=0)
        cond = np.stack([cc[s[1]] if s[0] == "s" else cctx for s in segs], axis=1)
        m = dict(shared)
        m["xT"] = f(x.T)
        m["condT"] = f(cond)
        is_s = c < 4
        m["flag"] = np.full((1, 1), 1.0 if is_s else 0.0, np.float32)
        lf = np.zeros(6, np.float32); lb = np.zeros(6, np.float32)
        if is_s:
            lf[1:4] = 1.0; lb[0:3] = 1.0
        m["link"] = np.concatenate([lf, lb]).reshape(1, 12)
        h0 = np.zeros((2, 2, 128, 64), np.float32)
        if is_s:
            for j in range(2):
                h0[j, 0, :64] = sre[c, j, 0].T; h0[j, 0, 64:] = sre[c, j, 1].T
                h0[j, 1, :64] = sim[c, j, 0].T; h0[j, 1, 64:] = sim[c, j, 1].T
        m["h0"] = h0
        maps.append(m)
    return maps


def assemble(results):
    yp = np.zeros((32, 256, D), np.float32)
    ys = np.zeros((4, 1024, D), np.float32)
    nre = np.zeros((32, 2, 2, 64, 64), np.float32)
    nim = np.zeros((32, 2, 2, 64, 64), np.float32)
    for c in range(8):
        y = np.asarray(results[c]["yT"]).T
        st = np.asarray(results[c]["st"]).reshape(2, 2, 2, 64, 64, NSEG)
        for si, s in enumerate(_core_segments(c)):
            blk = y[si * 256:(si + 1) * 256]
            if s[0] == "s":
                ys[s[1], s[2] * 256:(s[2] + 1) * 256] = blk
            else:
                yp[s[1]] = blk
                nre[s[1]] = st[:, 0, :, :, :, si].transpose(0, 1, 3, 2)
                nim[s[1]] = st[:, 1, :, :, :, si].transpose(0, 1, 3, 2)
    return yp, ys, nre, nim


_NC_CACHE = {}


def kernel(**inputs):
    maps = make_in_maps(inputs)
    if "nc" not in _NC_CACHE:
        _NC_CACHE["nc"] = build_nc()
    nc = _NC_CACHE["nc"]
    res = run_bass_kernel_spmd(nc, maps, core_ids=list(range(8)))
    return assemble(res.results)
```
